# Optimizing a Trainium2 kernel written in Bass

```python
import jax, jax.numpy as jnp
from jax import lax
import numpy as np

D_MODEL = 2048
BATCH = 4
SEQ = 8192
DEPTH = 4

N_MIXERS = 2
N_RWKV = (DEPTH + 1) // 2
N_CONV = DEPTH // 2
HEAD_SIZE = 64
N_HEADS = D_MODEL // HEAD_SIZE
DECAY_LORA = 96
AAA_LORA = 96
VALUE_LORA = 64
GATE_LORA = 256
CONV_WIDTH = 31
FFN_CONV_WIDTH = 3
D_FF = ((8 * D_MODEL // 3 + 127) // 128) * 128
RMS_EPS = 1e-6
LN_EPS = 1e-5
GN_EPS = 64e-5

kernel_name = "rwkv7_conformer_conv_interleaved_adaln"


def rms_norm(z, g, eps=RMS_EPS):
    zf = z.astype(jnp.float32)
    y = zf * lax.rsqrt(jnp.mean(zf * zf, axis=-1, keepdims=True) + eps)
    return (y * g.astype(jnp.float32)).astype(z.dtype)


def layer_norm(z, g, b, eps=LN_EPS):
    zf = z.astype(jnp.float32)
    mu = jnp.mean(zf, axis=-1, keepdims=True)
    var = jnp.mean(jnp.square(zf - mu), axis=-1, keepdims=True)
    y = (zf - mu) * lax.rsqrt(var + eps)
    return (y * g.astype(jnp.float32) + b.astype(jnp.float32)).astype(z.dtype)


def modulate(h, shift, scale):
    return h * (1 + scale[:, None, :]) + shift[:, None, :]


def token_shift(z):
    return jnp.pad(z, ((0, 0), (1, 0), (0, 0)))[:, :-1]


def causal_depthwise_conv(z, w, b):
    K = w.shape[0]
    y = lax.conv_general_dilated(
        z, w[:, None, :].astype(z.dtype), window_strides=(1,),
        padding=((K - 1, 0),), dimension_numbers=("NWC", "WIO", "NWC"),
        feature_group_count=z.shape[-1])
    return y + b


def wkv7_scan(r, decay, k, v, a, b):
    Bsz, T, H, N = r.shape

    def step(S, inp):
        r_t, w_t, k_t, v_t, a_t, b_t = inp
        sa = jnp.einsum('bhij,bhj->bhi', S, a_t)
        S = (S * w_t[:, :, None, :] + sa[..., None] * b_t[:, :, None, :]
             + v_t[..., None] * k_t[:, :, None, :])
        y = jnp.einsum('bhij,bhj->bhi', S, r_t)
        return S, y

    seq_major = lambda z: jnp.moveaxis(z, 1, 0)
    S0 = jnp.zeros((Bsz, H, N, N), jnp.float32)
    _, ys = lax.scan(step, S0, tuple(seq_major(z) for z in (r, decay, k, v, a, b)))
    return jnp.moveaxis(ys, 0, 1)


def rwkv7_time_mix(h, mu, w_rkv, w0, w1, w2, a0, a1, a2, g1, g2, k_k, k_a, r_k,
                   lnx_g, lnx_b, w_o, v_first=None, v0=None, v1=None, v2=None):
    Bsz, T, C = h.shape
    f32 = jnp.float32
    xx = token_shift(h) - h
    x_rkv = h[None] + xx[None] * mu[:3, None, None, :]
    r, k, v = jnp.einsum('nbtc,ncd->nbtd', x_rkv, w_rkv)
    xv = x_rkv[2]
    xw, xa, xg = (h + xx * mu[n] for n in (3, 4, 5))
    log_w = -jax.nn.softplus(-(w0 + jnp.tanh(xw @ w1) @ w2)) - 0.5
    a = jax.nn.sigmoid(a0 + (xa @ a1) @ a2)
    g = jax.nn.sigmoid(xg @ g1) @ g2
    if v_first is not None:
        v = v + (v_first - v) * jax.nn.sigmoid(v0 + (xv @ v1) @ v2)
    heads = lambda z: z.reshape(Bsz, T, N_HEADS, HEAD_SIZE)
    kk = heads(k * k_k).astype(f32)
    kk = kk / jnp.maximum(jnp.linalg.norm(kk, axis=-1, keepdims=True), 1e-12)
    k = k * (1 + (a - 1) * k_a)
    rh, kh, vh, ah = heads(r), heads(k), heads(v), heads(a)
    decay = jnp.exp(-jnp.exp(heads(log_w).astype(f32)))
    y = wkv7_scan(rh.astype(f32), decay, kh.astype(f32), vh.astype(f32),
                  -kk, kk * ah.astype(f32))
    mean = jnp.mean(y, axis=-1, keepdims=True)
    var = jnp.mean(jnp.square(y - mean), axis=-1, keepdims=True)
    y = ((y - mean) * lax.rsqrt(var + GN_EPS)).reshape(Bsz, T, C)
    y = (y * lnx_g.astype(f32) + lnx_b.astype(f32)).astype(h.dtype)
    bonus = (jnp.sum(rh * kh * r_k, axis=-1, keepdims=True) * vh).reshape(Bsz, T, C)
    return ((y + bonus) * g) @ w_o, v


def conformer_conv_module(h, w_pw1, b_pw1, w_dw, b_dw, ln_g, ln_b, w_pw2, b_pw2):
    u = jax.nn.glu(h @ w_pw1 + b_pw1, axis=-1)
    u = causal_depthwise_conv(u, w_dw, b_dw)
    u = jax.nn.silu(layer_norm(u, ln_g, ln_b))
    return u @ w_pw2 + b_pw2


def conv_glu_ffn(h, w_up, w_dw, b_dw, w_down):
    gate, val = jnp.split(h @ w_up, 2, axis=-1)
    gate = causal_depthwise_conv(gate, w_dw, b_dw)
    return (jax.nn.silu(gate) * val) @ w_down


def setup_inputs(seed: int = 0) -> dict:
    key = jax.random.key(seed)
    ks = iter(jax.random.split(key, 64))
    C, F, H, N = D_MODEL, D_FF, N_HEADS, HEAD_SIZE

    def nrm(shape, std):
        return std * jax.random.normal(next(ks), shape, jnp.float32)

    def unif(shape, lo, hi):
        return jax.random.uniform(next(ks), shape, jnp.float32, lo, hi)

    return {
        "x": nrm((BATCH, SEQ, C), 1.0),
        "c": nrm((BATCH, C), 1.0),
        "ada_w": nrm((DEPTH, C, 6 * C), 0.5 * C ** -0.5),
        "ada_b": nrm((DEPTH, 6 * C), 0.01),
        "norm_mix_g": 1.0 + nrm((DEPTH, C), 0.02),
        "norm_ffn_g": 1.0 + nrm((DEPTH, C), 0.02),
        "rwkv_mu": unif((N_RWKV, 6, C), 0.0, 1.0),
        "rwkv_w_rkv": nrm((N_RWKV, 3, C, C), C ** -0.5),
        "rwkv_w0": unif((N_RWKV, C), -6.0, -1.0),
        "rwkv_w1": nrm((N_RWKV, C, DECAY_LORA), C ** -0.5),
        "rwkv_w2": nrm((N_RWKV, DECAY_LORA, C), 0.5 * DECAY_LORA ** -0.5),
        "rwkv_a0": nrm((N_RWKV, C), 0.5),
        "rwkv_a1": nrm((N_RWKV, C, AAA_LORA), C ** -0.5),
        "rwkv_a2": nrm((N_RWKV, AAA_LORA, C), 0.5 * AAA_LORA ** -0.5),
        "rwkv_v0": nrm((N_RWKV - 1, C), 0.5),
        "rwkv_v1": nrm((N_RWKV - 1, C, VALUE_LORA), C ** -0.5),
        "rwkv_v2": nrm((N_RWKV - 1, VALUE_LORA, C), 0.5 * VALUE_LORA ** -0.5),
        "rwkv_g1": nrm((N_RWKV, C, GATE_LORA), C ** -0.5),
        "rwkv_g2": nrm((N_RWKV, GATE_LORA, C), GATE_LORA ** -0.5),
        "rwkv_k_k": 0.85 + nrm((N_RWKV, C), 0.05),
        "rwkv_k_a": 1.0 + nrm((N_RWKV, C), 0.05),
        "rwkv_r_k": nrm((N_RWKV, H, N), 0.1),
        "rwkv_lnx_g": 1.0 + nrm((N_RWKV, C), 0.02),
        "rwkv_lnx_b": nrm((N_RWKV, C), 0.01),
        "rwkv_w_o": nrm((N_RWKV, C, C), C ** -0.5),
        "conv_w_pw1": nrm((N_CONV, C, 2 * C), C ** -0.5),
        "conv_b_pw1": nrm((N_CONV, 2 * C), 0.01),
        "conv_w_dw": nrm((N_CONV, CONV_WIDTH, C), CONV_WIDTH ** -0.5),
        "conv_b_dw": nrm((N_CONV, C), 0.01),
        "conv_ln_g": 1.0 + nrm((N_CONV, C), 0.02),
        "conv_ln_b": nrm((N_CONV, C), 0.01),
        "conv_w_pw2": nrm((N_CONV, C, C), C ** -0.5),
        "conv_b_pw2": nrm((N_CONV, C), 0.01),
        "ffn_w_up": nrm((DEPTH, C, 2 * F), C ** -0.5),
        "ffn_w_dw": nrm((DEPTH, FFN_CONV_WIDTH, F), FFN_CONV_WIDTH ** -0.5),
        "ffn_b_dw": nrm((DEPTH, F), 0.01),
        "ffn_w_down": nrm((DEPTH, F, C), F ** -0.5),
        "final_norm_g": 1.0 + nrm((C,), 0.02),
    }


def reference(x, c, ada_w, ada_b, norm_mix_g, norm_ffn_g,
              rwkv_mu, rwkv_w_rkv, rwkv_w0, rwkv_w1, rwkv_w2, rwkv_a0, rwkv_a1, rwkv_a2,
              rwkv_v0, rwkv_v1, rwkv_v2, rwkv_g1, rwkv_g2, rwkv_k_k, rwkv_k_a, rwkv_r_k,
              rwkv_lnx_g, rwkv_lnx_b, rwkv_w_o,
              conv_w_pw1, conv_b_pw1, conv_w_dw, conv_b_dw, conv_ln_g, conv_ln_b,
              conv_w_pw2, conv_b_pw2,
              ffn_w_up, ffn_w_dw, ffn_b_dw, ffn_w_down, final_norm_g):
    c_act = jax.nn.silu(c)
    v_first = None
    for i in range(DEPTH):
        mod = c_act @ ada_w[i] + ada_b[i]
        sh_m, sc_m, g_m, sh_f, sc_f, g_f = jnp.split(mod, 6, axis=-1)
        h = modulate(rms_norm(x, norm_mix_g[i]), sh_m, sc_m)
        j = i // N_MIXERS
        if i % N_MIXERS == 0:
            if v_first is None:
                out, v_first = rwkv7_time_mix(
                    h, rwkv_mu[j], rwkv_w_rkv[j], rwkv_w0[j], rwkv_w1[j], rwkv_w2[j],
                    rwkv_a0[j], rwkv_a1[j], rwkv_a2[j], rwkv_g1[j], rwkv_g2[j],
                    rwkv_k_k[j], rwkv_k_a[j], rwkv_r_k[j], rwkv_lnx_g[j], rwkv_lnx_b[j],
                    rwkv_w_o[j])
            else:
                out, _ = rwkv7_time_mix(
                    h, rwkv_mu[j], rwkv_w_rkv[j], rwkv_w0[j], rwkv_w1[j], rwkv_w2[j],
                    rwkv_a0[j], rwkv_a1[j], rwkv_a2[j], rwkv_g1[j], rwkv_g2[j],
                    rwkv_k_k[j], rwkv_k_a[j], rwkv_r_k[j], rwkv_lnx_g[j], rwkv_lnx_b[j],
                    rwkv_w_o[j], v_first, rwkv_v0[j - 1], rwkv_v1[j - 1], rwkv_v2[j - 1])
        else:
            out = conformer_conv_module(
                h, conv_w_pw1[j], conv_b_pw1[j], conv_w_dw[j], conv_b_dw[j],
                conv_ln_g[j], conv_ln_b[j], conv_w_pw2[j], conv_b_pw2[j])
        x = x + g_m[:, None, :] * out
        h = modulate(rms_norm(x, norm_ffn_g[i]), sh_f, sc_f)
        x = x + g_f[:, None, :] * conv_glu_ffn(h, ffn_w_up[i], ffn_w_dw[i], ffn_b_dw[i], ffn_w_down[i])
    return rms_norm(x, final_norm_g)
```

```python
import numpy as np
from contextlib import ExitStack
import concourse.bass as bass
import concourse.mybir as mybir
from concourse.bass_utils import run_bass_kernel_spmd

F32 = mybir.dt.float32
BF16 = mybir.dt.bfloat16
AF = mybir.ActivationFunctionType
ALU = mybir.AluOpType

C = 2048
NC_ = 16
FF = 5504
NF = 43
TT = 512
L = 128
NCH = TT // L
C0 = 0.6065306597126334
RMS_EPS = 1e-6
LN_EPS = 1e-5
GN_EPS = 64e-5
SLOT = 2048
NSLOT = 3


class Op:
    __slots__ = ("eng", "fn", "r", "w", "dma", "semkey", "waits", "sig", "sigval", "idx", "barrier")

    def __init__(self, eng, fn, r, w, dma, semkey):
        self.eng = eng; self.fn = fn; self.r = tuple(r); self.w = tuple(w)
        self.dma = dma; self.semkey = semkey
        self.waits = []; self.sig = False; self.sigval = 0; self.barrier = False


class Prog:
    CE = ("pe", "act", "dve", "pool")
    ROT = 30000

    def __init__(self):
        self.ops = []
        self.ins = {}

    def add(self, eng, fn, r=(), w=(), dma=False, semkey=None, at=None):
        op = Op(eng, fn, r, w, dma, semkey)
        if at is None:
            self.ops.append(op)
        else:
            self.ins.setdefault(at, []).append(op)
        return op

    def pos(self):
        return len(self.ops)

    def barrier(self):
        for e in ("pe", "act", "dve", "pool", "sp"):
            op = Op(e, None, (), (), False, None)
            op.barrier = True
            self.ops.append(op)

    def flatten(self):
        flat = []
        n = len(self.ops)
        for i in range(n + 1):
            if i in self.ins:
                flat.extend(self.ins[i])
            if i < n:
                flat.append(self.ops[i])
        for i, o in enumerate(flat):
            o.idx = i
        return flat

    def analyze(self):
        flat = self.flatten()
        lastw = {}
        lastr = {}
        waited = {e: {} for e in ("pe", "act", "dve", "pool", "sp")}
        dmacnt = {}
        for op in flat:
            if op.dma:
                dmacnt[op.semkey] = dmacnt.get(op.semkey, 0) + 16
                op.sigval = dmacnt[op.semkey]
        lastop = {}
        dmalast = {}
        for i, op in enumerate(flat):
            if op.barrier:
                wd = waited[op.eng]
                for pe_, d in lastop.items():
                    if pe_ == op.eng:
                        continue
                    if wd.get(pe_, -1) < d:
                        wd[pe_] = d; op.waits.append((pe_, d)); flat[d].sig = True
                for k_, v_ in dmalast.items():
                    key = ("dma", k_)
                    if wd.get(key, -1) < v_:
                        wd[key] = v_; op.waits.append((key, v_))
                continue
            if op.dma:
                dmalast[op.semkey] = op.sigval
            else:
                lastop[op.eng] = i
            deps = set()
            raw = set()
            for k in op.r:
                if k in lastw:
                    raw.add(lastw[k])
            for k in op.w:
                if k in lastw:
                    deps.add(lastw[k])
                lr = lastr.get(k)
                if lr:
                    deps.update(lr.values())
            deps |= raw
            deps.discard(i)
            need = {}
            for d in deps:
                p = flat[d]
                if p.dma:
                    key = ("dma", p.semkey); val = p.sigval
                else:
                    if p.eng == op.eng and not op.dma:
                        if p.eng == "pe":
                            continue
                    key = p.eng; val = d
                if need.get(key, -1) < val:
                    need[key] = val
            wd = waited[op.eng]
            for key, val in need.items():
                if wd.get(key, -1) >= val:
                    continue
                wd[key] = val
                op.waits.append((key, val))
                if not isinstance(key, tuple):
                    flat[val].sig = True
            for k in op.w:
                lastw[k] = i
                lastr[k] = {}
            for k in op.r:
                lastr.setdefault(k, {})[op.eng + ("d" if op.dma else "")] = i
        cnt = {e: 0 for e in self.CE}
        for op in flat:
            if op.dma:
                op.sig = True
            elif op.sig:
                cnt[op.eng] += 1
                op.sigval = cnt[op.eng]
        self.flat = flat
        self.cnt = cnt
        self.dmakeys = sorted(dmacnt.keys())
        return flat

    def emit(self, nc, es):
        flat = self.analyze()
        sems = {}
        for e in self.CE:
            n = self.cnt[e] // self.ROT + 1
            sems[e] = [es.enter_context(nc.semaphore(f"s_{e}_{i}")) for i in range(n)]
        dsem = {k: es.enter_context(nc.semaphore(f"d_{j}")) for j, k in enumerate(self.dmakeys)}
        block = es.enter_context(nc.Block())
        ROT = self.ROT

        def semof(e, v):
            return sems[e][(v - 1) // ROT], (v - 1) % ROT + 1

        def run(engname, handle):
            for op in flat:
                if op.eng != engname:
                    continue
                for key, val in op.waits:
                    if isinstance(key, tuple):
                        handle.wait_ge(dsem[key[1]], val)
                    else:
                        s, v = semof(key, flat[val].sigval)
                        handle.wait_ge(s, v)
                if op.fn is None:
                    continue
                inst = op.fn(handle)
                if op.dma:
                    inst.then_inc(dsem[op.semkey], 16)
                elif op.sig:
                    s, v = semof(op.eng, op.sigval)
                    inst.then_inc(s, 1)
            if engname == "sp":
                last = {}
                for op in flat:
                    if op.dma:
                        last[op.semkey] = op.sigval
                for k, v in last.items():
                    handle.wait_ge(dsem[k], v)

        @block.sync
        def _(e):
            run("sp", e)

        @block.tensor
        def _(e):
            run("pe", e)

        @block.scalar
        def _(e):
            run("act", e)

        @block.vector
        def _(e):
            run("dve", e)

        @block.gpsimd
        def _(e):
            run("pool", e)


class Pool:
    def __init__(self, name, aps):
        self.name = name; self.aps = aps; self.i = 0

    def get(self):
        j = self.i % len(self.aps); self.i += 1
        return self.aps[j], f"{self.name}{j}"


def _cols(v):
    v = np.asarray(v, np.float32).reshape(-1)
    return np.ascontiguousarray(v.reshape(-1, 128).T)


class ParamLayout:
    def __init__(self, nlayers):
        self.off = {}
        self.n = 0
        for l in range(nlayers):
            self._a(f"adab{l}", 96); self._a(f"gmix{l}", 16); self._a(f"gffn{l}", 16)
            self._a(f"fdw{l}", 3 * NF); self._a(f"fdb{l}", NF)
            if l % 2 == 0:
                for nm, n in (("mu", 96), ("w0", 16), ("a0", 16), ("v0", 16), ("kk", 16), ("ka", 16),
                              ("rk", 16), ("lng", 16), ("lnb", 16)):
                    self._a(f"{nm}{l}", n)
            else:
                for nm, n in (("b1", 32), ("cw", 31 * 16), ("cb", 16), ("clg", 16), ("clb", 16), ("b2", 16)):
                    self._a(f"{nm}{l}", n)
        self._a("fin", 16); self._a("c", 16)

    def _a(self, k, n):
        self.off[k] = (self.n, n); self.n += n


def pack_params(inp, b, nlayers, lay):
    P = np.zeros((128, lay.n), np.float32)

    def put(k, arr):
        o, n = lay.off[k]
        a = _cols(arr)
        assert a.shape[1] == n, (k, a.shape, n)
        P[:, o:o + n] = a

    for l in range(nlayers):
        j = l // 2
        put(f"adab{l}", inp["ada_b"][l]); put(f"gmix{l}", inp["norm_mix_g"][l]); put(f"gffn{l}", inp["norm_ffn_g"][l])
        put(f"fdw{l}", inp["ffn_w_dw"][l]); put(f"fdb{l}", inp["ffn_b_dw"][l])
        if l % 2 == 0:
            put(f"mu{l}", inp["rwkv_mu"][j]); put(f"w0{l}", inp["rwkv_w0"][j]); put(f"a0{l}", inp["rwkv_a0"][j])
            if j > 0:
                put(f"v0{l}", inp["rwkv_v0"][j - 1])
            put(f"kk{l}", inp["rwkv_k_k"][j]); put(f"ka{l}", inp["rwkv_k_a"][j]); put(f"rk{l}", inp["rwkv_r_k"][j])
            put(f"lng{l}", inp["rwkv_lnx_g"][j]); put(f"lnb{l}", inp["rwkv_lnx_b"][j])
        else:
            put(f"b1{l}", inp["conv_b_pw1"][j]); put(f"cw{l}", inp["conv_w_dw"][j]); put(f"cb{l}", inp["conv_b_dw"][j])
            put(f"clg{l}", inp["conv_ln_g"][j]); put(f"clb{l}", inp["conv_ln_b"][j]); put(f"b2{l}", inp["conv_b_pw2"][j])
    put("fin", inp["final_norm_g"]); put("c", inp["c"][b])
    return P


def make_consts():
    K = np.zeros((128, 10 * 128 + TT), np.float32)
    i = np.arange(128)
    K[:, 0:128] = np.eye(128)
    K[:, 128:256] = (i[None, :] > i[:, None])
    K[:, 256:384] = (i[None, :] >= i[:, None])
    K[:, 384:512] = (i[None, :] < i[:, None])
    K[:, 512:640] = ((i[None, :] // 64) == (i[:, None] // 64))
    K[:, 640:768] = K[:, 512:640] / 64.0
    K[:, 768:896] = 1.0 / 2048.0
    m = np.ones(TT, np.float32); m[::L] = 0.0
    K[:, 896:896 + TT] = m[None, :]
    o = 896 + TT
    bd = ((i[None, :] // 64) == (i[:, None] // 64))
    K[:, o:o + 128] = K[:, 128:256] * bd
    K[:, o + 128:o + 256] = K[:, 384:512] * bd
    K[:, o + 256:o + 384] = (i[:, None] < 64) & (i[None, :] >= 64)
    return K


def build(T, nlayers, dbg=None):
    NT = T // TT
    lay = ParamLayout(nlayers)
    nc = bass.Bass("TRN2", target_bir_lowering=False)
    es = ExitStack()
    P = Prog()

    def din(name, shape):
        return nc.dram_tensor(name, list(shape), F32, kind="ExternalInput").ap()

    xin = din("x", (C, T))
    prm = din("prm", (128, lay.n))
    cst = din("cst", (128, 10 * 128 + TT))
    out = nc.dram_tensor("out", [C, T], F32, kind="ExternalOutput").ap()
    W = {}
    for l in range(nlayers):
        W[f"ada{l}"] = din(f"ada{l}", (C, 6 * C))
        W[f"up{l}"] = din(f"up{l}", (C, 2 * FF)); W[f"dn{l}"] = din(f"dn{l}", (FF, C))
        if l % 2 == 0:
            for n in "rkv":
                W[f"{n}{l}"] = din(f"{n}{l}", (C, C))
            W[f"wo{l}"] = din(f"wo{l}", (C, C))
            W[f"w1{l}"] = din(f"w1{l}", (C, 96)); W[f"w2{l}"] = din(f"w2{l}", (96, C))
            W[f"a1{l}"] = din(f"a1{l}", (C, 96)); W[f"a2{l}"] = din(f"a2{l}", (96, C))
            W[f"g1{l}"] = din(f"g1{l}", (C, 256)); W[f"g2{l}"] = din(f"g2{l}", (256, C))
            if l >= 2:
                W[f"v1{l}"] = din(f"v1{l}", (C, 64)); W[f"v2{l}"] = din(f"v2{l}", (64, C))
        else:
            W[f"pw1{l}"] = din(f"pw1{l}", (C, 2 * C)); W[f"pw2{l}"] = din(f"pw2{l}", (C, C))

    SCR = {}

    def scr(name, n_oc, n_kc, Mb):
        SCR[name] = (nc.dram_tensor("s_" + name, [n_oc, 128, n_kc * Mb], BF16, kind="Internal").ap(), n_kc, Mb)

    for l in range(nlayers):
        scr(f"up{l}", 2 * NF, 16, 128); scr(f"dnA{l}", 16, 15, 128); scr(f"dnB{l}", 16, 14, 128); scr(f"dnC{l}", 16, 14, 128)
        if l % 2 == 0:
            for n in ("r", "k", "v", "wo"):
                scr(f"{n}{l}", 16, 16, 128)
            scr(f"g1a{l}", 1, 16, 128); scr(f"g1b{l}", 1, 16, 128)
            scr(f"w1{l}", 1, 16, 96); scr(f"a1{l}", 1, 16, 96)
            if l >= 2:
                scr(f"v1{l}", 1, 16, 128)
            scr(f"lo2{l}", 16, 5, 128)
        else:
            scr(f"pw1{l}", 32, 16, 128); scr(f"pw2{l}", 16, 16, 128)
    vfs = nc.dram_tensor("s_vfirst", [16, 128, TT], F32, kind="Internal").ap()

    def sb(name, shape, dt=F32):
        return es.enter_context(nc.sbuf_tensor(name, list(shape), dt))

    def ps(name, shape, dt=F32):
        return es.enter_context(nc.psum_tensor(name, list(shape), dt))

    prm_t = sb("prm_t", (128, lay.n))
    drv_t = sb("drv_t", (128, nlayers * 112))
    x_t = sb("x_t", (128, NC_, TT))
    hb = sb("hb", (128, NC_, TT + 1), BF16)
    ident = sb("ident", (128, 128), BF16)
    identF = sb("identF", (128, 128))
    mask2 = sb("mask2", (128, 256), BF16)
    mask2a = sb("mask2a", (128, 256), BF16)
    maskSL = sb("maskSL", (128, 128), BF16)
    maskUR = sb("maskUR", (128, 128), BF16)
    bo1 = sb("bo1", (128, 128), BF16)
    bo64 = sb("bo64", (128, 128), BF16)
    oneC = sb("oneC", (128, 128), BF16)
    rmask = sb("rmask", (128, TT))
    wsl = sb("wsl", (128, NSLOT, SLOT), BF16)
    nrw = (nlayers + 1) // 2
    ncv = nlayers // 2
    S_t = sb("S_t", (128, nrw * NC_ * 64))
    Sb_t = sb("Sb_t", (128, nrw * NC_ * 64), BF16)
    shc = sb("shc", (128, nrw, NC_), BF16)
    cvc = sb("cvc", (128, max(ncv, 1), NC_, 30))
    ffc = sb("ffc", (128, nlayers, NF, 2))

    def pc(k, i=0, n=1):
        o, _ = lay.off[k]
        return prm_t[:, o + i:o + i + n]

    def dc(l, i, n=1):
        return drv_t[:, l * 112 + i:l * 112 + i + n]
    omu = sb("omu", (128, nrw, 96))
    ca_t = sb("ca_t", (128, NC_))

    pd_t = [ps(f"pd{i}", (128, 512)) for i in range(3)]
    pw_t = [ps(f"pw{i}", (128, 512)) for i in range(4)]
    pt_t = ps("pt", (128, 8, 128), BF16)
    pd = Pool("pd", [t for t in pd_t])
    pw = Pool("pw", [t for t in pw_t])
    ptp = Pool("pt", [pt_t])

    def mm(o, lhsT, rhs, start, stop, r, w):
        P.add("pe", lambda e: e.matmul(o, lhsT=lhsT, rhs=rhs, start=start, stop=stop), r=r, w=w)

    def tr(o, in_, r, w):
        P.add("pe", lambda e: e.transpose(o, in_, ident[:, :]), r=list(r) + ["const"], w=w)

    def act(o, in_, func, r, w, bias=None, scale=None):
        kw = {}
        if bias is not None:
            kw["bias"] = bias
        if scale is not None:
            kw["scale"] = scale
        P.add("act", lambda e: e.activation(out=o, in_=in_, func=func, **kw), r=r, w=w)

    def tsc(eng, o, in0, s1, s2, op0, op1, r, w):
        if op1 is None:
            P.add(eng, lambda e: e.tensor_scalar(out=o, in0=in0, scalar1=s1, scalar2=None, op0=op0), r=r, w=w)
        else:
            P.add(eng, lambda e: e.tensor_scalar(out=o, in0=in0, scalar1=s1, scalar2=s2, op0=op0, op1=op1), r=r, w=w)

    def rsqrt(o, in_, eps, r, w, premax=None):
        if premax is not None:
            tsc("dve", o, in_, premax, None, ALU.max, None, r=r, w=w)
            act(o, o, AF.Ln, r=w, w=w)
        else:
            act(o, in_, AF.Ln, r=r, w=w, bias=eps)
        act(o, o, AF.Exp, r=w, w=w, scale=-0.5)

    def tt(eng, o, in0, in1, op, r, w):
        P.add(eng, lambda e: e.tensor_tensor(out=o, in0=in0, in1=in1, op=op), r=r, w=w)

    def stt(o, in0, scalar, in1, op0, op1, r, w):
        P.add("dve", lambda e: e.scalar_tensor_tensor(out=o, in0=in0, scalar=scalar, in1=in1, op0=op0, op1=op1), r=r, w=w)

    def cp(eng, o, in_, r, w):
        if eng == "act":
            act(o, in_, AF.Copy, r, w)
        else:
            P.add(eng, lambda e: e.tensor_copy(out=o, in_=in_), r=r, w=w)

    def dma(o, in_, r, w, semkey, at=None, eng="sp"):
        return P.add(eng, lambda e: e.dma_start(out=o, in_=in_), r=r, w=w, dma=True, semkey=semkey, at=at)

    with ExitStack() as pes:
        def psb(name, shape, dt=F32):
            return pes.enter_context(nc.sbuf_tensor(name, list(shape), dt))
        cst_t = psb("cst_t", (128, 10 * 128 + TT))
        dma(cst_t[:, :], cst[:, :], r=[], w=["cst_t"], semkey="cst_t")
        dma(prm_t[:, :], prm[:, :], r=[], w=["prm"], semkey="prm")
        for i, tgt in enumerate((ident, None, None, None, bo1, bo64, oneC)):
            if tgt is not None:
                cp("dve", tgt[:, :], cst_t[:, i * 128:(i + 1) * 128], r=["cst_t"], w=["const"])
        cp("dve", mask2[:, :], cst_t[:, 128:384], r=["cst_t"], w=["const"])
        co_ = 896 + TT
        cp("dve", mask2a[:, 0:128], cst_t[:, co_:co_ + 128], r=["cst_t"], w=["const"])
        cp("dve", mask2a[:, 128:256], cst_t[:, 256:384], r=["cst_t"], w=["const"])
        cp("dve", maskSL[:, :], cst_t[:, co_ + 128:co_ + 256], r=["cst_t"], w=["const"])
        cp("dve", maskUR[:, :], cst_t[:, co_ + 256:co_ + 384], r=["cst_t"], w=["const"])
        cp("dve", identF[:, :], cst_t[:, 0:128], r=["cst_t"], w=["const"])
        cp("act", rmask[:, :], cst_t[:, 896:896 + TT], r=["cst_t"], w=["const"])
        for t_, k_ in ((S_t, "S"), (Sb_t, "Sb"), (shc, "shc"), (cvc, "cvc"), (ffc, "ffc")):
            P.add("pool", (lambda tt_: (lambda e: e.memset(tt_, 0.0)))(t_[:]), r=[], w=[k_])
        act(ca_t[:, :], pc("c", 0, 16), AF.Silu, r=["prm"], w=["ca"])
        st32 = [psb(f"st32_{i}", (128, SLOT)) for i in range(3)]
        st16 = [psb(f"st16_{i}", (128, SLOT), BF16) for i in range(3)]
        cnt = [0]
        cast_engs = ("act", "dve", "pool")

        def cast_block(src_ap, kp, n_kc, Mb, dst_ap):
            i = cnt[0] % 3; cnt[0] += 1
            ne = n_kc * Mb
            s32 = st32[i][0:kp, 0:ne]; s16 = st16[i][0:kp, 0:ne]
            dma(s32.rearrange("p (k m) -> p k m", m=Mb) if n_kc > 1 else s32, src_ap, r=[], w=[f"st32_{i}"], semkey=f"st32_{i}")
            cp(cast_engs[(cnt[0]) % 3], s16, s32, r=[f"st32_{i}"], w=[f"st16_{i}"])
            dma(dst_ap, s16, r=[f"st16_{i}"], w=[], semkey=f"st16_{i}")

        def cast_mat(name, src, r0, n_kc, Mb, c0, n_oc):
            dst, nk, mb = SCR[name]
            for oc in range(n_oc):
                sap = src[r0:r0 + n_kc * 128, c0 + oc * Mb:c0 + (oc + 1) * Mb].rearrange("(k p) m -> p k m", p=128)
                cast_block(sap, 128, n_kc, Mb, dst[oc, :, :])

        for l in range(nlayers):
            cast_mat(f"up{l}", W[f"up{l}"], 0, 16, 128, 0, 2 * NF)
            cast_mat(f"dnA{l}", W[f"dn{l}"], 0, 15, 128, 0, 16)
            cast_mat(f"dnB{l}", W[f"dn{l}"], 15 * 128, 14, 128, 0, 16)
            cast_mat(f"dnC{l}", W[f"dn{l}"], 29 * 128, 14, 128, 0, 16)
            if l % 2 == 0:
                for n in ("r", "k", "v", "wo"):
                    cast_mat(f"{n}{l}", W[f"{n}{l}"], 0, 16, 128, 0, 16)
                cast_mat(f"g1a{l}", W[f"g1{l}"], 0, 16, 128, 0, 1)
                cast_mat(f"g1b{l}", W[f"g1{l}"], 0, 16, 128, 128, 1)
                cast_mat(f"w1{l}", W[f"w1{l}"], 0, 16, 96, 0, 1)
                cast_mat(f"a1{l}", W[f"a1{l}"], 0, 16, 96, 0, 1)
                if l >= 2:
                    i = cnt[0] % 3; cnt[0] += 1
                    s16f = st16[i][:, 0:C]
                    P.add("pool", (lambda a_: (lambda e: e.memset(a_, 0.0)))(s16f), r=[], w=[f"st16_{i}"])
                    s32 = st32[i][:, 0:16 * 64]
                    dma(s32.rearrange("p (k m) -> p k m", m=64), W[f"v1{l}"].rearrange("(k p) m -> p k m", p=128),
                        r=[], w=[f"st32_{i}"], semkey=f"st32_{i}")
                    cp(cast_engs[cnt[0] % 3], s16f.rearrange("p (k m) -> p k m", m=128)[:, :, 0:64],
                       s32.rearrange("p (k m) -> p k m", m=64), r=[f"st32_{i}"], w=[f"st16_{i}"])
                    dma(SCR[f"v1{l}"][0][0, :, :], s16f, r=[f"st16_{i}"], w=[], semkey=f"st16_{i}")
                lo2 = SCR[f"lo2{l}"][0]
                srcs = [(W[f"w2{l}"], 0, 96), (W[f"a2{l}"], 0, 96)]
                srcs.append((W[f"v2{l}"], 0, 64) if l >= 2 else None)
                srcs += [(W[f"g2{l}"], 0, 128), (W[f"g2{l}"], 128, 128)]
                for s_i, sd in enumerate(srcs):
                    i = cnt[0] % 3; cnt[0] += 1
                    s16f = st16[i][:, 0:C]
                    P.add("pool", (lambda a_: (lambda e: e.memset(a_, 0.0)))(s16f), r=[], w=[f"st16_{i}"])
                    if sd is not None:
                        src, r0, kp = sd
                        s32 = st32[i][0:kp, 0:C]; s16 = st16[i][0:kp, 0:C]
                        dma(s32, src[r0:r0 + kp, :], r=[], w=[f"st32_{i}"], semkey=f"st32_{i}")
                        cp(cast_engs[cnt[0] % 3], s16, s32, r=[f"st32_{i}"], w=[f"st16_{i}"])
                    dma(lo2[:, :, s_i * 128:(s_i + 1) * 128].rearrange("f p j -> p f j"),
                        s16f.rearrange("p (f j) -> p f j", j=128), r=[f"st16_{i}"], w=[], semkey=f"st16_{i}")
            else:
                cast_mat(f"pw1{l}", W[f"pw1{l}"], 0, 16, 128, 0, 32)
                cast_mat(f"pw2{l}", W[f"pw2{l}"], 0, 16, 128, 0, 16)

        adw = [psb(f"adw{i}", (128, NC_, 128)) for i in range(2)]
        for l in range(nlayers):
            pm_, pmk = pd.get()
            for oc in range(96):
                a_, ak = adw[oc % 2], f"adw{oc % 2}"
                dma(a_[:, :, :], W[f"ada{l}"][:, oc * 128:(oc + 1) * 128].rearrange("(k p) m -> p k m", p=128),
                    r=[], w=[ak], semkey=ak)
                for kc in range(NC_):
                    mm(pm_[:, oc:oc + 1], a_[:, kc, :], ca_t[:, kc:kc + 1], kc == 0, kc == NC_ - 1, r=[ak, "ca"], w=[pmk])
            mod = psb(f"mod{l}", (128, 96))
            tt("dve", mod[:, :], pm_[:, 0:96], pc(f"adab{l}", 0, 96), ALU.add, r=[pmk, "prm"], w=[f"mod{l}"])
            for hlf, gname in ((0, f"gmix{l}"), (1, f"gffn{l}")):
                b0 = hlf * 48
                stt(dc(l, b0, 16), mod[:, (3 * hlf + 1) * 16:(3 * hlf + 2) * 16], 1.0, pc(gname, 0, 16), ALU.add, ALU.mult,
                    r=[f"mod{l}", "prm"], w=["drv"])
                cp("dve", dc(l, b0 + 16, 16), mod[:, (3 * hlf) * 16:(3 * hlf + 1) * 16], r=[f"mod{l}"], w=["drv"])
                cp("dve", dc(l, b0 + 32, 16), mod[:, (3 * hlf + 2) * 16:(3 * hlf + 3) * 16], r=[f"mod{l}"], w=["drv"])
            if l % 2 == 0:
                tsc("dve", dc(l, 96, 16), pc(f"ka{l}", 0, 16), -1.0, 1.0, ALU.mult, ALU.add, r=["prm"], w=["drv"])
                tsc("dve", omu[:, l // 2, :], pc(f"mu{l}", 0, 96), -1.0, 1.0, ALU.mult, ALU.add, r=["prm"], w=["drv"])
            else:
                tt("dve", dc(l, 96, 16), pc(f"b2{l}", 0, 16), dc(l, 32, 16), ALU.mult, r=["prm", "drv"], w=["drv"])
    P.barrier()
    prologue_end = P.pos()

    slot_i = [0]
    slot_last = [prologue_end] * NSLOT

    def wget(name, oc):
        dst, n_kc, Mb = SCR[name]
        j = slot_i[0] % NSLOT; slot_i[0] += 1
        ne = n_kc * Mb
        at = max(slot_last[j], prologue_end)
        dma(wsl[:, j, 0:ne], dst[oc, :, :], r=[], w=[f"ws{j}"], semkey=f"ws{j}", at=at)
        return j, f"ws{j}"

    def wdone(j):
        slot_last[j] = P.pos()

    def wblk(j, kc, Mb=128, kp=128):
        return wsl[0:kp, j, kc * Mb:(kc + 1) * Mb]

    def mkpool(name, n, shape, dt=F32):
        return Pool(name, [sb(f"{name}{i}", shape, dt) for i in range(n)])

    sqp = mkpool("sq", 2, (128, TT), BF16)
    f32p = mkpool("f", 4, (128, TT))
    rwp = mkpool("rw", 8, (128, TT))
    rstd_p = mkpool("rstd", 2, (128, TT))
    arena = sb("arena", (128, 12288))
    arena_b = arena[:, :].bitcast(BF16)
    xm_v = arena_b.rearrange("p (a k t) -> p a k t", a=3, k=NC_)
    xm = [xm_v[:, i] for i in range(3)]
    hid = arena_b[:, 0:NF * TT].rearrange("p (f t) -> p f t", t=TT)
    cb_t = arena[:, 0:NC_ * TT].rearrange("p (k t) -> p k t", t=TT)
    gbp = mkpool("gb", 1, (128, TT + 2))
    ubp = mkpool("ub", 1, (128, TT + 30))
    l1p = mkpool("l1", 5, (128, TT), BF16)
    b16p = mkpool("b", 6, (128, TT), BF16)
    ARp = mkpool("AR", 1, (128, NCH, 256), BF16)
    TRp = mkpool("TR", 3, (128, NCH, 128), BF16)
    e_p = mkpool("E", 4, (128, 256), BF16)
    am_p = mkpool("AM", 2, (128, 256), BF16)
    m_p = mkpool("M", 3, (128, 128), BF16)
    ct_p = mkpool("Ct", 2, (128, 128), BF16)
    tt_p = mkpool("Tt", 3, (128, 128), BF16)
    xu_p = mkpool("XU", 6, (128, 64), BF16)
    yt_p = mkpool("yt", 1, (128, TT))
    xs_p = mkpool("xs", 2, (128, 64))
    ytok_p = mkpool("ytok", 1, (128, NCH, 128))
    vf_p = mkpool("vf", 1, (128, TT))
    o_p = f32p

    def norm_mod(l, which, func=AF.Identity):
        b0 = 0 if which == "mix" else 48
        acc, ak = pd.get()
        for kc in range(NC_):
            s, sk = sqp.get()
            act(s[:, :], x_t[:, kc, :], AF.Square, r=[f"x{kc}"], w=[sk])
            mm(acc[:, :], oneC[:, :], s[:, :], kc == 0, kc == NC_ - 1, r=[sk, "const"], w=[ak])
        rs, rk = rstd_p.get()
        rsqrt(rs[:, :], acc[:, :], RMS_EPS, r=[ak], w=[rk])
        for kc in range(NC_):
            t, tk = f32p.get()
            stt(t[:, :], x_t[:, kc, :], dc(l, b0 + kc), rs[:, :], ALU.mult, ALU.mult, r=[f"x{kc}", "drv", rk], w=[tk])
            act(hb[:, kc, 1:TT + 1], t[:, :], func, bias=dc(l, b0 + 16 + kc), r=[tk, "drv"], w=[f"hb{kc}"])

    def proj16(name, oc, src_keys, rhs_of, M=128):
        j, wk = wget(name, oc)
        acc, ak = pd.get()
        Mb = SCR[name][2]
        for kc in range(NC_):
            mm(acc[0:M, :], wblk(j, kc, Mb), rhs_of(kc), kc == 0, kc == NC_ - 1, r=[wk, src_keys[kc]], w=[ak])
        wdone(j)
        return acc, ak

    hbk = [f"hb{kc}" for kc in range(NC_)]

    def ffn(l):
        norm_mod(l, "ffn")
        for fc in range(NF):
            pg, pgk = proj16(f"up{l}", fc, hbk, lambda kc: hb[:, kc, 1:TT + 1])
            pv, pvk = proj16(f"up{l}", NF + fc, hbk, lambda kc: hb[:, kc, 1:TT + 1])
            gb, gk = gbp.get()
            cp("pool", gb[:, 0:2], ffc[:, l, fc, :], r=["ffc"], w=[gk])
            act(gb[:, 2:TT + 2], pg[:, :], AF.Copy, r=[pgk], w=[gk])
            t1, t1k = f32p.get()
            act(t1[:, :], pg[:, :], AF.Identity, scale=pc(f"fdw{l}", 2 * NF + fc), bias=pc(f"fdb{l}", fc), r=[pgk, "prm"], w=[t1k])
            cp("pool", ffc[:, l, fc, :], gb[:, TT:TT + 2], r=[gk], w=["ffc"])
            t2, t2k = f32p.get()
            stt(t2[:, :], gb[:, 1:TT + 1], pc(f"fdw{l}", NF + fc), t1[:, :], ALU.mult, ALU.add, r=[gk, t1k, "prm"], w=[t2k])
            t3, t3k = f32p.get()
            stt(t3[:, :], gb[:, 0:TT], pc(f"fdw{l}", fc), t2[:, :], ALU.mult, ALU.add, r=[gk, t2k, "prm"], w=[t3k])
            t4, t4k = f32p.get()
            act(t4[:, :], t3[:, :], AF.Silu, r=[t3k], w=[t4k])
            tt("dve", hid[:, fc, :], pv[:, :], t4[:, :], ALU.mult, r=[t4k, pvk], w=[f"hid{fc}"])
        for oc in range(NC_):
            acc, ak = pd.get()
            for nm_, k0, nk in ((f"dnA{l}", 0, 15), (f"dnB{l}", 15, 14), (f"dnC{l}", 29, 14)):
                j, wk = wget(nm_, oc)
                for kc in range(nk):
                    mm(acc[:, :], wblk(j, kc), hid[:, k0 + kc, :], k0 + kc == 0, k0 + kc == NF - 1, r=[wk, f"hid{k0 + kc}"], w=[ak])
                wdone(j)
            stt(x_t[:, oc, :], acc[:, :], dc(l, 80 + oc), x_t[:, oc, :], ALU.mult, ALU.add, r=[ak, "drv", f"x{oc}"], w=[f"x{oc}"])

    def conv(l):
        j_ = l // 2
        norm_mod(l, "mix")
        for cc in range(NC_):
            pa, pak = proj16(f"pw1{l}", cc, hbk, lambda kc: hb[:, kc, 1:TT + 1])
            pg, pgk = proj16(f"pw1{l}", NC_ + cc, hbk, lambda kc: hb[:, kc, 1:TT + 1])
            sg, sgk = f32p.get()
            act(sg[:, :], pg[:, :], AF.Sigmoid, bias=pc(f"b1{l}", NC_ + cc), r=[pgk, "prm"], w=[sgk])
            ub, uk = ubp.get()
            cp("pool", ub[:, 0:30], cvc[:, j_, cc, :], r=["cvc"], w=[uk])
            stt(ub[:, 30:TT + 30], pa[:, :], pc(f"b1{l}", cc), sg[:, :], ALU.add, ALU.mult, r=[pak, sgk, "prm"], w=[uk])
            cp("pool", cvc[:, j_, cc, :], ub[:, TT:TT + 30], r=[uk], w=["cvc"])
            a0, a0k = f32p.get(); a1, a1k = f32p.get()
            tsc("dve", a0[:, :], ub[:, 0:TT], pc(f"cw{l}", 0 * 16 + cc), pc(f"cb{l}", cc), ALU.mult, ALU.add, r=[uk, "prm"], w=[a0k])
            tsc("dve", a1[:, :], ub[:, 1:TT + 1], pc(f"cw{l}", 1 * 16 + cc), None, ALU.mult, None, r=[uk, "prm"], w=[a1k])
            for k in range(2, 31):
                a, ak_ = (a0, a0k) if k % 2 == 0 else (a1, a1k)
                stt(a[:, :], ub[:, k:k + TT], pc(f"cw{l}", k * 16 + cc), a[:, :], ALU.mult, ALU.add, r=[uk, "prm", ak_], w=[ak_])
            tt("dve", cb_t[:, cc, :], a0[:, :], a1[:, :], ALU.add, r=[a0k, a1k], w=[f"cb{cc}"])
        pm_, pmk = pd.get()
        for cc in range(NC_):
            s, sk = sqp.get()
            cp("act", s[:, :], cb_t[:, cc, :], r=[f"cb{cc}"], w=[sk])
            mm(pm_[:, :], oneC[:, :], s[:, :], cc == 0, cc == NC_ - 1, r=[sk, "const"], w=[pmk])
        pv_, pvk = pd.get()
        for cc in range(NC_):
            stt(cb_t[:, cc, :], pm_[:, :], -1.0, cb_t[:, cc, :], ALU.mult, ALU.add, r=[f"cb{cc}", pmk], w=[f"cb{cc}"])
            s, sk = sqp.get()
            act(s[:, :], cb_t[:, cc, :], AF.Square, r=[f"cb{cc}"], w=[sk])
            mm(pv_[:, :], oneC[:, :], s[:, :], cc == 0, cc == NC_ - 1, r=[sk, "const"], w=[pvk])
        rs, rk = rstd_p.get()
        rsqrt(rs[:, :], pv_[:, :], LN_EPS, r=[pvk], w=[rk])
        for cc in range(NC_):
            t, tk = f32p.get()
            stt(t[:, :], cb_t[:, cc, :], pc(f"clg{l}", cc), rs[:, :], ALU.mult, ALU.mult, r=[f"cb{cc}", "prm", rk], w=[tk])
            act(hb[:, cc, 1:TT + 1], t[:, :], AF.Silu, bias=pc(f"clb{l}", cc), r=[tk, "prm"], w=[f"hb{cc}"])
        for oc in range(NC_):
            po, pok = proj16(f"pw2{l}", oc, hbk, lambda kc: hb[:, kc, 1:TT + 1])
            t, tk = f32p.get()
            act(t[:, :], po[:, :], AF.Identity, scale=dc(l, 32 + oc), bias=dc(l, 96 + oc), r=[pok, "drv"], w=[tk])
            tt("pool", x_t[:, oc, :], x_t[:, oc, :], t[:, :], ALU.add, r=[tk, f"x{oc}"], w=[f"x{oc}"])

    def rwkv(l, ti):
        j_ = l // 2
        vfirst = l >= 2
        norm_mod(l, "mix")
        for kc in range(NC_):
            cp("pool", hb[:, kc, 0:1], shc[:, j_, kc:kc + 1], r=["shc"], w=[f"hb{kc}"])
            cp("pool", shc[:, j_, kc:kc + 1], hb[:, kc, TT:TT + 1], r=[f"hb{kc}"], w=["shc"])

        def mix(n, dst, di):
            for kc in range(NC_):
                t, tk = f32p.get()
                act(t[:, :], hb[:, kc, 1:TT + 1], AF.Identity, scale=omu[:, j_, n * 16 + kc:n * 16 + kc + 1], r=[f"hb{kc}", "drv"], w=[tk])
                stt(dst[:, kc, :], hb[:, kc, 0:TT], pc(f"mu{l}", n * 16 + kc), t[:, :], ALU.mult, ALU.add,
                    r=[f"hb{kc}", "prm", tk], w=[f"xm{di}_{kc}"])
        if dbg == "r1":
            return
        xk0 = [f"xm0_{kc}" for kc in range(NC_)]
        xk1 = [f"xm1_{kc}" for kc in range(NC_)]
        xk2 = [f"xm2_{kc}" for kc in range(NC_)]
        mix(5, xm[0], 0)
        g1s = []
        for hf, nm in enumerate((f"g1a{l}", f"g1b{l}")):
            p_, pk_ = proj16(nm, 0, xk0, lambda kc: xm[0][:, kc, :])
            g, gk = l1p.get()
            act(g[:, :], p_[:, :], AF.Sigmoid, r=[pk_], w=[gk])
            g1s.append((g, gk))
        if dbg == "r2":
            return
        mix(3, xm[0], 0)
        p_, pk_ = proj16(f"w1{l}", 0, xk0, lambda kc: xm[0][:, kc, :], M=96)
        lw1, lw1k = l1p.get()
        act(lw1[0:96, :], p_[0:96, :], AF.Tanh, r=[pk_], w=[lw1k])
        mix(4, xm[0], 0)
        p_, pk_ = proj16(f"a1{l}", 0, xk0, lambda kc: xm[0][:, kc, :], M=96)
        la1, la1k = l1p.get()
        act(la1[0:96, :], p_[0:96, :], AF.Copy, r=[pk_], w=[la1k])
        mix(2, xm[2], 2)
        if vfirst:
            p_, pk_ = proj16(f"v1{l}", 0, xk2, lambda kc: xm[2][:, kc, :])
            lv1, lv1k = l1p.get()
            act(lv1[:, :], p_[:, :], AF.Copy, r=[pk_], w=[lv1k])
        mix(1, xm[1], 1)
        mix(0, xm[0], 0)

        if dbg == "r3":
            return
        for fc in range(NC_):
            if dbg in ("r4", "r5", "r6", "u1", "u2", "u3", "u4", "u5", "u4a", "u4b", "u4c") and fc > 0:
                return
            Sk = f"S{j_}_{fc}"; Sbk = f"Sb{j_}_{fc}"
            pr, prk = proj16(f"r{l}", fc, xk0, lambda kc: xm[0][:, kc, :])
            r_, rk_ = rwp.get()
            act(r_[:, :], pr[:, :], AF.Copy, r=[prk], w=[rk_])
            pk, pkk = proj16(f"k{l}", fc, xk1, lambda kc: xm[1][:, kc, :])
            k_, kk_ = rwp.get()
            act(k_[:, :], pk[:, :], AF.Copy, r=[pkk], w=[kk_])
            pv, pvk = proj16(f"v{l}", fc, xk2, lambda kc: xm[2][:, kc, :])
            v_, vk_ = rwp.get()
            act(v_[:, :], pv[:, :], AF.Copy, r=[pvk], w=[vk_])
            j2, w2k = wget(f"lo2{l}", fc)
            pw_, pwk = pd.get()
            mm(pw_[:, :], wblk(j2, 0, 128, 96), lw1[0:96, :], True, True, r=[w2k, lw1k], w=[pwk])
            sw, swk = rwp.get()
            act(sw[:, :], pw_[:, :], AF.Sigmoid, bias=pc(f"w0{l}", fc), r=[pwk, "prm"], w=[swk])
            pa_, pak = pd.get()
            mm(pa_[:, :], wblk(j2, 1, 128, 96), la1[0:96, :], True, True, r=[w2k, la1k], w=[pak])
            asg, ask = rwp.get()
            act(asg[:, :], pa_[:, :], AF.Sigmoid, bias=pc(f"a0{l}", fc), r=[pak, "prm"], w=[ask])
            pg_, pgk = pd.get()
            mm(pg_[:, :], wblk(j2, 3), g1s[0][0][:, :], True, False, r=[w2k, g1s[0][1]], w=[pgk])
            mm(pg_[:, :], wblk(j2, 4), g1s[1][0][:, :], False, True, r=[w2k, g1s[1][1]], w=[pgk])
            gt, gtk = b16p.get()
            act(gt[:, :], pg_[:, :], AF.Copy, r=[pgk], w=[gtk])
            if vfirst:
                pv2, pv2k = pd.get()
                mm(pv2[:, :], wblk(j2, 2), lv1[:, :], True, True, r=[w2k, lv1k], w=[pv2k])
                sv, svk = f32p.get()
                act(sv[:, :], pv2[:, :], AF.Sigmoid, bias=pc(f"v0{l}", fc), r=[pv2k, "prm"], w=[svk])
                vf, vfk = vf_p.get()
                dma(vf[:, :], vfs[fc, :, :], r=[f"vfs{fc}"], w=[vfk], semkey=vfk)
                tt("pool", vf[:, :], vf[:, :], v_[:, :], ALU.subtract, r=[vfk, vk_], w=[vfk])
                tt("pool", vf[:, :], vf[:, :], sv[:, :], ALU.mult, r=[vfk, svk], w=[vfk])
                tt("pool", v_[:, :], v_[:, :], vf[:, :], ALU.add, r=[vfk, vk_], w=[vk_])
            elif nlayers > 2:
                dma(vfs[fc, :, :], v_[:, :], r=[vk_], w=[f"vfs{fc}"], semkey=vk_)
            wdone(j2)
            if dbg == "r4":
                return
            kkr, kkrk = rwp.get()
            tsc("dve", kkr[:, :], k_[:, :], pc(f"kk{l}", fc), None, ALU.mult, None, r=[kk_, "prm"], w=[kkrk])
            s, sk = sqp.get()
            act(s[:, :], kkr[:, :], AF.Square, r=[kkrk], w=[sk])
            pn, pnk = pd.get()
            mm(pn[:, :], bo1[:, :], s[:, :], True, True, r=[sk, "const"], w=[pnk])
            rn, rnk = rstd_p.get()
            rsqrt(rn[:, :], pn[:, :], None, r=[pnk], w=[rnk], premax=1e-24)
            tt("dve", kkr[:, :], kkr[:, :], rn[:, :], ALU.mult, r=[kkrk, rnk], w=[kkrk])
            f_, fk = f32p.get()
            act(f_[:, :], asg[:, :], AF.Identity, scale=pc(f"ka{l}", fc), bias=dc(l, 96 + fc), r=[ask, "prm", "drv"], w=[fk])
            tt("pool", k_[:, :], k_[:, :], f_[:, :], ALU.mult, r=[kk_, fk], w=[kk_])
            tt("pool", asg[:, :], asg[:, :], kkr[:, :], ALU.mult, r=[ask, kkrk], w=[ask])
            cw, cwk = rwp.get()
            P.add("dve", lambda e, cw=cw, sw=sw: e.tensor_tensor_scan(out=cw[:, :], data0=rmask[:, :], data1=sw[:, :],
                                                                     initial=0.0, op0=ALU.mult, op1=ALU.add),
                  r=[swk, "const"], w=[cwk])
            tt("pool", sw[:, :], cw[:, :], sw[:, :], ALU.subtract, r=[cwk, swk], w=[swk])
            Wt, Wk = rwp.get()
            act(Wt[:, :], cw[:, :], AF.Exp, scale=-C0, r=[cwk], w=[Wk])
            act(sw[:, :], sw[:, :], AF.Exp, scale=-C0, r=[swk], w=[swk])
            act(cw[:, :], cw[:, :], AF.Exp, scale=C0, r=[cwk], w=[cwk])
            AR, ARk = ARp.get()
            tt("dve", AR[:, :, 128:256], r_[:, :].rearrange("p (c t) -> p c t", t=L), Wt[:, :].rearrange("p (c t) -> p c t", t=L),
               ALU.mult, r=[rk_, Wk], w=[ARk])
            stt(AR[:, :, 0:128], kkr[:, :].rearrange("p (c t) -> p c t", t=L), -1.0, sw[:, :].rearrange("p (c t) -> p c t", t=L),
                ALU.mult, ALU.mult, r=[kkrk, swk], w=[ARk])
            BT, BTk = b16p.get()
            tt("dve", BT[:, :], asg[:, :], cw[:, :], ALU.mult, r=[ask, cwk], w=[BTk])
            KT, KTk = b16p.get()
            tt("dve", KT[:, :], k_[:, :], cw[:, :], ALU.mult, r=[kk_, cwk], w=[KTk])
            bend, bendk = b16p.get(); kend, kendk = b16p.get()
            for c in range(NCH):
                wl = Wt[:, c * L + L - 1:c * L + L]
                tsc("pool", bend[:, c * L:(c + 1) * L], BT[:, c * L:(c + 1) * L], wl, None, ALU.mult, None, r=[BTk, Wk], w=[bendk])
                tsc("pool", kend[:, c * L:(c + 1) * L], KT[:, c * L:(c + 1) * L], wl, None, ALU.mult, None, r=[KTk, Wk], w=[kendk])
            vb, vbk = b16p.get()
            cp("act", vb[:, :], v_[:, :], r=[vk_], w=[vbk])
            rkr, rkrk = sqp.get()
            stt(rkr[:, :], r_[:, :], pc(f"rk{l}", fc), k_[:, :], ALU.mult, ALU.mult, r=[rk_, kk_, "prm"], w=[rkrk])
            if dbg == "r5":
                return
            VT, VTk = TRp.get(); BET, BETk = TRp.get(); KET, KETk = TRp.get()
            for ei, (src, srck, dst, dstk) in enumerate(((vb, vbk, VT, VTk), (bend, bendk, BET, BETk), (kend, kendk, KET, KETk))):
                p_, pk_ = ptp.get()
                for c in range(NCH):
                    tr(p_[:, c, :], src[:, c * L:(c + 1) * L], r=[srck], w=[pk_])
                cp("act" if ei % 2 == 0 else "dve", dst[:, :, :], p_[:, 0:NCH, :], r=[pk_], w=[dstk])
            if dbg == "r6":
                return
            pb, pbk = pd.get()
            mm(pb[:, :], bo1[:, :], rkr[:, :], True, True, r=[rkrk, "const"], w=[pbk])
            tt("dve", v_[:, :], pb[:, :], v_[:, :], ALU.mult, r=[vk_, pbk], w=[vk_])
            yt, ytk = yt_p.get()
            ytok, ytokk = ytok_p.get()
            for c in range(NCH):
                for hh in range(2):
                    rows = slice(hh * 64, hh * 64 + 64)
                    if dbg in ("u1", "u2", "u3", "u4", "u4a", "u4b", "u4c") and (hh > 0 or c > 0):
                        continue
                    if dbg == "u5" and c > 0:
                        continue
                    ARc = AR[:, c, :]
                    BTc = BT[:, c * L:(c + 1) * L]; KTc = KT[:, c * L:(c + 1) * L]
                    bA, bAk = pw.get(); bB, bBk = pw.get()
                    P1 = bA[:, 0:256]; P2 = bA[:, 256:512]; P3 = bB[:, 0:128]
                    mm(P1, BTc[rows, :], ARc[rows, :], True, True, r=[BTk, ARk], w=[bAk])
                    mm(P2, KTc[rows, :], ARc[rows, :], True, True, r=[KTk, ARk], w=[bAk])
                    mm(P3, ARc[rows, 0:128], BTc[rows, :], True, True, r=[BTk, ARk], w=[bBk])
                    E1, E1k = e_p.get(); E2, E2k = e_p.get()
                    tt("dve", E1[:, :], P1, mask2a[:, :], ALU.mult, r=[bAk, "const"], w=[E1k])
                    Ct, Ctk = ct_p.get()
                    tt("dve", Ct[:, :], P1[:, 0:128], maskUR[:, :], ALU.mult, r=[bAk, "const"], w=[Ctk])
                    tt("dve", E2[:, :], P2, mask2[:, :], ALU.mult, r=[bAk, "const"], w=[E2k])
                    M0, M0k = m_p.get()
                    tt("dve", M0[:, :], P3, maskSL[:, :], ALU.mult, r=[bBk, "const"], w=[M0k])
                    if dbg == "u1":
                        continue
                    Tt, Ttk = tt_p.get()
                    tt("pool", Tt[:, :], E1[:, 0:128], ident[:, :], ALU.add, r=[E1k, "const"], w=[Ttk])
                    Aprev, Apk = E1[:, 0:128], E1k
                    Mprev, Mpk = M0[:, :], M0k
                    Q = bB[:, 0:256]; Q2 = bB[:, 256:384]
                    for lev in range(1, 6):
                        if lev < 5:
                            mm(Q[:, 0:128], Mprev, Aprev, True, True, r=[Apk, Mpk], w=[bBk])
                            mm(Q[:, 128:256], Aprev, Mprev, True, True, r=[Apk, Mpk], w=[bBk])
                            AM, AMk = am_p.get()
                            cp("act", AM[:, :], Q, r=[bBk], w=[AMk])
                            Aprev, Apk = AM[:, 0:128], AMk
                            Mprev, Mpk = AM[:, 128:256], AMk
                        else:
                            mm(Q[:, 0:128], Aprev, Mprev, True, True, r=[Apk, Mpk], w=[bBk])
                            M6, M6k = m_p.get()
                            cp("act", M6[:, :], Q[:, 0:128], r=[bBk], w=[M6k])
                            Mprev, Mpk = M6[:, :], M6k
                        mm(Q2, ident[:, :], Tt[:, :], True, False, r=["const", Ttk], w=[bBk])
                        mm(Q2, Mprev, Tt[:, :], False, True, r=[Mpk, Ttk], w=[bBk])
                        Tn, Tnk = tt_p.get()
                        cp("dve" if lev % 2 else "act", Tn[:, :], Q2, r=[bBk], w=[Tnk])
                        Tt, Ttk = Tn, Tnk
                    if dbg == "u2":
                        continue
                    PX = bB[:, 384:448]; PU = bB[:, 448:512]
                    mm(PX, ARc[rows, 0:128], Sb_t[rows, (j_ * NC_ + fc) * 64:(j_ * NC_ + fc) * 64 + 64], True, False, r=[ARk, Sbk], w=[bBk])
                    mm(PX, E2[:, 0:128], VT[:, c, hh * 64:hh * 64 + 64], False, True, r=[E2k, VTk], w=[bBk])
                    X0, X0k = xu_p.get()
                    cp("act", X0[:, :], PX, r=[bBk], w=[X0k])
                    mm(PU, Tt[:, :], X0[:, :], True, True, r=[Ttk, X0k], w=[bBk])
                    W1, W1k = xu_p.get()
                    cp("dve", W1[:, :], PU, r=[bBk], w=[W1k])
                    mm(PX, ident[:, :], X0[:, :], True, False, r=["const", X0k], w=[bBk])
                    mm(PX, Ct[:, :], W1[:, :], False, True, r=[Ctk, W1k], w=[bBk])
                    X2, X2k = xu_p.get()
                    cp("act", X2[:, :], PX, r=[bBk], w=[X2k])
                    mm(PU, Tt[:, :], X2[:, :], True, True, r=[Ttk, X2k], w=[bBk])
                    U, Uk = xu_p.get()
                    cp("dve", U[:, :], PU, r=[bBk], w=[Uk])
                    if dbg == "u3":
                        continue
                    hc = slice(hh * 64, hh * 64 + 64)
                    PY = bA[:, 0:64]; PS = bA[:, 64:128]
                    mm(PY, ARc[rows, 128:256], Sb_t[rows, (j_ * NC_ + fc) * 64:(j_ * NC_ + fc) * 64 + 64], True, False, r=[Sbk, ARk], w=[bAk])
                    mm(PY, E1[:, 128:256], U[:, :], False, False, r=[Uk, E1k], w=[bAk])
                    mm(PY, E2[:, 128:256], VT[:, c, hc], False, True, r=[VTk, E2k], w=[bAk])
                    if dbg == "u4a":
                        cp("act", ytok[:, c, hc], PY, r=[bAk], w=[ytokk])
                        continue
                    mm(PS, BET[:, c, :], U[:, :], True, False, r=[BETk, Uk], w=[bAk])
                    mm(PS, KET[:, c, :], VT[:, c, hc], False, True, r=[KETk, VTk], w=[bAk])
                    cp("act", ytok[:, c, hc], PY, r=[bAk], w=[ytokk])
                    if dbg == "u4b":
                        continue
                    rows_ = rows
                    so = (j_ * NC_ + fc) * 64
                    stmp, stk = xs_p.get()
                    act(stmp[rows_, :], S_t[rows_, so:so + 64], AF.Identity, scale=Wt[rows_, c * L + L - 1:c * L + L], r=[Sk, Wk], w=[stk])
                    tt("dve", S_t[rows_, so:so + 64], PS[rows_, :], stmp[rows_, :], ALU.add, r=[stk, bAk], w=[Sk])
                    if dbg == "u4c":
                        continue
                    cp("act", Sb_t[rows_, (j_ * NC_ + fc) * 64:(j_ * NC_ + fc) * 64 + 64], S_t[rows_, (j_ * NC_ + fc) * 64:(j_ * NC_ + fc) * 64 + 64], r=[Sk], w=[Sbk])
            if dbg in ("u1", "u2", "u3", "u4", "u5", "u4a", "u4b", "u4c"):
                return
            py_, pyk = pd.get()
            for c in range(NCH):
                P.add("pe", (lambda o_, i_: (lambda e: e.transpose(o_, i_, identF[:, :])))(py_[:, c * L:(c + 1) * L], ytok[:, c, :]),
                      r=[ytokk, "const"], w=[pyk])
            cp("act", yt[:, :], py_[:, :], r=[pyk], w=[ytk])
            yb, ybk = sqp.get()
            cp("act", yb[:, :], yt[:, :], r=[ytk], w=[ybk])
            pm_, pmk = pd.get()
            mm(pm_[:, :], bo64[:, :], yb[:, :], True, True, r=[ybk, "const"], w=[pmk])
            stt(yt[:, :], pm_[:, :], -1.0, yt[:, :], ALU.mult, ALU.add, r=[ytk, pmk], w=[ytk])
            s2, s2k = sqp.get()
            act(s2[:, :], yt[:, :], AF.Square, r=[ytk], w=[s2k])
            pvv, pvvk = pd.get()
            mm(pvv[:, :], bo64[:, :], s2[:, :], True, True, r=[s2k, "const"], w=[pvvk])
            rs, rsk = rstd_p.get()
            rsqrt(rs[:, :], pvv[:, :], GN_EPS, r=[pvvk], w=[rsk])
            tt("dve", yt[:, :], yt[:, :], rs[:, :], ALU.mult, r=[ytk, rsk], w=[ytk])
            act(yt[:, :], yt[:, :], AF.Identity, scale=pc(f"lng{l}", fc), bias=pc(f"lnb{l}", fc), r=[ytk, "prm"], w=[ytk])
            tt("pool", yt[:, :], yt[:, :], v_[:, :], ALU.add, r=[ytk, vk_], w=[ytk])
            tt("pool", hb[:, fc, 1:TT + 1], yt[:, :], gt[:, :], ALU.mult, r=[ytk, gtk], w=[f"hb{fc}"])
        for oc in range(NC_):
            po, pok = proj16(f"wo{l}", oc, hbk, lambda kc: hb[:, kc, 1:TT + 1])
            stt(x_t[:, oc, :], po[:, :], dc(l, 32 + oc), x_t[:, oc, :], ALU.mult, ALU.add, r=[pok, "drv", f"x{oc}"], w=[f"x{oc}"])

    xv = xin.rearrange("(k p) t -> p k t", p=128)
    ov = out.rearrange("(k p) t -> p k t", p=128)
    for ti in range(NT):
        t0 = ti * TT
        for kc in range(NC_):
            dma(x_t[:, kc, :], xv[:, kc, t0:t0 + TT], r=[], w=[f"x{kc}"], semkey=f"x{kc}")
        for l in range(nlayers):
            if dbg == "pro":
                break
            if l % 2 == 0:
                rwkv(l, ti)
            else:
                conv(l)
            if dbg in ("mix", "r1", "r2", "r3", "r4", "r5", "r6", "u1", "u2", "u3", "u4", "u5", "u4a", "u4b", "u4c"):
                break
            ffn(l)
        acc, ak = pd.get()
        for kc in range(NC_):
            s, sk = sqp.get()
            act(s[:, :], x_t[:, kc, :], AF.Square, r=[f"x{kc}"], w=[sk])
            mm(acc[:, :], oneC[:, :], s[:, :], kc == 0, kc == NC_ - 1, r=[sk, "const"], w=[ak])
        rs, rk = rstd_p.get()
        rsqrt(rs[:, :], acc[:, :], RMS_EPS, r=[ak], w=[rk])
        for kc in range(NC_):
            o, ok = o_p.get()
            stt(o[:, :], x_t[:, kc, :], pc("fin", kc), rs[:, :], ALU.mult, ALU.mult, r=[f"x{kc}", "prm", rk], w=[ok])
            dma(ov[:, kc, t0:t0 + TT], o[:, :], r=[ok], w=[], semkey=ok)

    P.emit(nc, es)
    es.close()
    return nc, lay


def make_inmaps(inp, nlayers, T, lay, batches):
    maps = []
    cst = make_consts()
    for b in batches:
        m = {"x": np.ascontiguousarray(np.asarray(inp["x"][b], np.float32)[:T].T),
             "prm": pack_params(inp, b, nlayers, lay), "cst": cst}
        for l in range(nlayers):
            j = l // 2
            m[f"ada{l}"] = np.ascontiguousarray(inp["ada_w"][l]); m[f"up{l}"] = np.ascontiguousarray(inp["ffn_w_up"][l])
            m[f"dn{l}"] = np.ascontiguousarray(inp["ffn_w_down"][l])
            if l % 2 == 0:
                for i, n in enumerate("rkv"):
                    m[f"{n}{l}"] = np.ascontiguousarray(inp["rwkv_w_rkv"][j][i])
                m[f"wo{l}"] = np.ascontiguousarray(inp["rwkv_w_o"][j])
                for n in ("w1", "w2", "a1", "a2", "g1", "g2"):
                    m[f"{n}{l}"] = np.ascontiguousarray(inp["rwkv_" + n][j])
                if l >= 2:
                    m[f"v1{l}"] = np.ascontiguousarray(inp["rwkv_v1"][j - 1]); m[f"v2{l}"] = np.ascontiguousarray(inp["rwkv_v2"][j - 1])
            else:
                m[f"pw1{l}"] = np.ascontiguousarray(inp["conv_w_pw1"][j]); m[f"pw2{l}"] = np.ascontiguousarray(inp["conv_w_pw2"][j])
        maps.append(m)
    return maps


def run(inp, nlayers, T, batches, dbg=None):
    nc, lay = build(T, nlayers, dbg)
    maps = make_inmaps(inp, nlayers, T, lay, batches)
    res = run_bass_kernel_spmd(nc, maps, core_ids=list(range(len(batches))))
    return [np.ascontiguousarray(r["out"].T) for r in res.results]


def kernel(**inputs):
    inp = {k: np.asarray(v) for k, v in inputs.items()}
    B, T, _ = inp["x"].shape
    outs = run(inp, 4, T, [0, 1, 2, 3, 0, 1, 2, 3])
    return np.stack(outs[:4], axis=0).astype(np.float32)
```

```python
import numpy as np
from contextlib import ExitStack
import concourse.bass as bass
import concourse.mybir as mybir
from concourse.bass_utils import run_bass_kernel_spmd

F32 = mybir.dt.float32
BF16 = mybir.dt.bfloat16
AF = mybir.ActivationFunctionType
ALU = mybir.AluOpType

C = 2048
NC_ = 16
FF = 5504
NF = 43
TT = 512
L = 128
NCH = TT // L
C0 = 0.6065306597126334
RMS_EPS = 1e-6
LN_EPS = 1e-5
GN_EPS = 64e-5
SLOT = 2048
NSLOT = 3


class Op:
    __slots__ = ("eng", "fn", "r", "w", "dma", "semkey", "waits", "sig", "sigval", "idx", "barrier")

    def __init__(self, eng, fn, r, w, dma, semkey):
        self.eng = eng; self.fn = fn; self.r = tuple(r); self.w = tuple(w)
        self.dma = dma; self.semkey = semkey
        self.waits = []; self.sig = False; self.sigval = 0; self.barrier = False


class Prog:
    CE = ("pe", "act", "dve", "pool")
    ROT = 30000

    def __init__(self):
        self.ops = []
        self.ins = {}

    def add(self, eng, fn, r=(), w=(), dma=False, semkey=None, at=None):
        op = Op(eng, fn, r, w, dma, semkey)
        if at is None:
            self.ops.append(op)
        else:
            self.ins.setdefault(at, []).append(op)
        return op

    def pos(self):
        return len(self.ops)

    def barrier(self):
        for e in ("pe", "act", "dve", "pool", "sp"):
            op = Op(e, None, (), (), False, None)
            op.barrier = True
            self.ops.append(op)

    def flatten(self):
        flat = []
        n = len(self.ops)
        for i in range(n + 1):
            if i in self.ins:
                flat.extend(self.ins[i])
            if i < n:
                flat.append(self.ops[i])
        for i, o in enumerate(flat):
            o.idx = i
        return flat

    def analyze(self):
        flat = self.flatten()
        lastw = {}
        lastr = {}
        waited = {e: {} for e in ("pe", "act", "dve", "pool", "sp")}
        dmacnt = {}
        for op in flat:
            if op.dma:
                dmacnt[op.semkey] = dmacnt.get(op.semkey, 0) + 16
                op.sigval = dmacnt[op.semkey]
        lastop = {}
        dmalast = {}
        for i, op in enumerate(flat):
            if op.barrier:
                wd = waited[op.eng]
                for pe_, d in lastop.items():
                    if pe_ == op.eng:
                        continue
                    if wd.get(pe_, -1) < d:
                        wd[pe_] = d; op.waits.append((pe_, d)); flat[d].sig = True
                for k_, v_ in dmalast.items():
                    key = ("dma", k_)
                    if wd.get(key, -1) < v_:
                        wd[key] = v_; op.waits.append((key, v_))
                continue
            if op.dma:
                dmalast[op.semkey] = op.sigval
            else:
                lastop[op.eng] = i
            deps = set()
            raw = set()
            for k in op.r:
                if k in lastw:
                    raw.add(lastw[k])
            for k in op.w:
                if k in lastw:
                    deps.add(lastw[k])
                lr = lastr.get(k)
                if lr:
                    deps.update(lr.values())
            deps |= raw
            deps.discard(i)
            need = {}
            for d in deps:
                p = flat[d]
                if p.dma:
                    key = ("dma", p.semkey); val = p.sigval
                else:
                    if p.eng == op.eng and not op.dma:
                        if p.eng == "pe":
                            continue
                    key = p.eng; val = d
                if need.get(key, -1) < val:
                    need[key] = val
            wd = waited[op.eng]
            for key, val in need.items():
                if wd.get(key, -1) >= val:
                    continue
                wd[key] = val
                op.waits.append((key, val))
                if not isinstance(key, tuple):
                    flat[val].sig = True
            for k in op.w:
                lastw[k] = i
                lastr[k] = {}
            for k in op.r:
                lastr.setdefault(k, {})[op.eng + ("d" if op.dma else "")] = i
        cnt = {e: 0 for e in self.CE}
        for op in flat:
            if op.dma:
                op.sig = True
            elif op.sig:
                cnt[op.eng] += 1
                op.sigval = cnt[op.eng]
        self.flat = flat
        self.cnt = cnt
        self.dmakeys = sorted(dmacnt.keys())
        return flat

    def emit(self, nc, es):
        flat = self.analyze()
        sems = {}
        for e in self.CE:
            n = self.cnt[e] // self.ROT + 1
            sems[e] = [es.enter_context(nc.semaphore(f"s_{e}_{i}")) for i in range(n)]
        dsem = {k: es.enter_context(nc.semaphore(f"d_{j}")) for j, k in enumerate(self.dmakeys)}
        block = es.enter_context(nc.Block())
        ROT = self.ROT

        def semof(e, v):
            return sems[e][(v - 1) // ROT], (v - 1) % ROT + 1

        def run(engname, handle):
            for op in flat:
                if op.eng != engname:
                    continue
                for key, val in op.waits:
                    if isinstance(key, tuple):
                        handle.wait_ge(dsem[key[1]], val)
                    else:
                        s, v = semof(key, flat[val].sigval)
                        handle.wait_ge(s, v)
                if op.fn is None:
                    continue
                inst = op.fn(handle)
                if op.dma:
                    inst.then_inc(dsem[op.semkey], 16)
                elif op.sig:
                    s, v = semof(op.eng, op.sigval)
                    inst.then_inc(s, 1)
            if engname == "sp":
                last = {}
                for op in flat:
                    if op.dma:
                        last[op.semkey] = op.sigval
                for k, v in last.items():
                    handle.wait_ge(dsem[k], v)

        @block.sync
        def _(e):
            run("sp", e)

        @block.tensor
        def _(e):
            run("pe", e)

        @block.scalar
        def _(e):
            run("act", e)

        @block.vector
        def _(e):
            run("dve", e)

        @block.gpsimd
        def _(e):
            run("pool", e)


class Pool:
    def __init__(self, name, aps):
        self.name = name; self.aps = aps; self.i = 0

    def get(self):
        j = self.i % len(self.aps); self.i += 1
        return self.aps[j], f"{self.name}{j}"


def _cols(v):
    v = np.asarray(v, np.float32).reshape(-1)
    return np.ascontiguousarray(v.reshape(-1, 128).T)


class ParamLayout:
    def __init__(self, nlayers):
        self.off = {}
        self.n = 0
        for l in range(nlayers):
            self._a(f"adab{l}", 96); self._a(f"gmix{l}", 16); self._a(f"gffn{l}", 16)
            self._a(f"fdw{l}", 3 * NF); self._a(f"fdb{l}", NF)
            if l % 2 == 0:
                for nm, n in (("mu", 96), ("w0", 16), ("a0", 16), ("v0", 16), ("kk", 16), ("ka", 16),
                              ("rk", 16), ("lng", 16), ("lnb", 16)):
                    self._a(f"{nm}{l}", n)
            else:
                for nm, n in (("b1", 32), ("cw", 31 * 16), ("cb", 16), ("clg", 16), ("clb", 16), ("b2", 16)):
                    self._a(f"{nm}{l}", n)
        self._a("fin", 16); self._a("c", 16)

    def _a(self, k, n):
        self.off[k] = (self.n, n); self.n += n


def pack_params(inp, b, nlayers, lay):
    P = np.zeros((128, lay.n), np.float32)

    def put(k, arr):
        o, n = lay.off[k]
        a = _cols(arr)
        assert a.shape[1] == n, (k, a.shape, n)
        P[:, o:o + n] = a

    for l in range(nlayers):
        j = l // 2
        put(f"adab{l}", inp["ada_b"][l]); put(f"gmix{l}", inp["norm_mix_g"][l]); put(f"gffn{l}", inp["norm_ffn_g"][l])
        put(f"fdw{l}", inp["ffn_w_dw"][l]); put(f"fdb{l}", inp["ffn_b_dw"][l])
        if l % 2 == 0:
            put(f"mu{l}", inp["rwkv_mu"][j]); put(f"w0{l}", inp["rwkv_w0"][j]); put(f"a0{l}", inp["rwkv_a0"][j])
            if j > 0:
                put(f"v0{l}", inp["rwkv_v0"][j - 1])
            put(f"kk{l}", inp["rwkv_k_k"][j]); put(f"ka{l}", inp["rwkv_k_a"][j]); put(f"rk{l}", inp["rwkv_r_k"][j])
            put(f"lng{l}", inp["rwkv_lnx_g"][j]); put(f"lnb{l}", inp["rwkv_lnx_b"][j])
        else:
            put(f"b1{l}", inp["conv_b_pw1"][j]); put(f"cw{l}", inp["conv_w_dw"][j]); put(f"cb{l}", inp["conv_b_dw"][j])
            put(f"clg{l}", inp["conv_ln_g"][j]); put(f"clb{l}", inp["conv_ln_b"][j]); put(f"b2{l}", inp["conv_b_pw2"][j])
    put("fin", inp["final_norm_g"]); put("c", inp["c"][b])
    return P


def make_consts():
    K = np.zeros((128, 10 * 128 + TT), np.float32)
    i = np.arange(128)
    K[:, 0:128] = np.eye(128)
    K[:, 128:256] = (i[None, :] > i[:, None])
    K[:, 256:384] = (i[None, :] >= i[:, None])
    K[:, 384:512] = (i[None, :] < i[:, None])
    K[:, 512:640] = ((i[None, :] // 64) == (i[:, None] // 64))
    K[:, 640:768] = K[:, 512:640] / 64.0
    K[:, 768:896] = 1.0 / 2048.0
    m = np.ones(TT, np.float32); m[::L] = 0.0
    K[:, 896:896 + TT] = m[None, :]
    o = 896 + TT
    bd = ((i[None, :] // 64) == (i[:, None] // 64))
    K[:, o:o + 128] = K[:, 128:256] * bd
    K[:, o + 128:o + 256] = K[:, 384:512] * bd
    K[:, o + 256:o + 384] = (i[:, None] < 64) & (i[None, :] >= 64)
    return K


def build(T, nlayers, dbg=None):
    NT = T // TT
    lay = ParamLayout(nlayers)
    nc = bass.Bass("TRN2", target_bir_lowering=False)
    es = ExitStack()
    P = Prog()

    def din(name, shape):
        return nc.dram_tensor(name, list(shape), F32, kind="ExternalInput").ap()

    xin = din("x", (C, T))
    prm = din("prm", (128, lay.n))
    cst = din("cst", (128, 10 * 128 + TT))
    out = nc.dram_tensor("out", [C, T], F32, kind="ExternalOutput").ap()
    W = {}
    for l in range(nlayers):
        W[f"ada{l}"] = din(f"ada{l}", (C, 6 * C))
        W[f"up{l}"] = din(f"up{l}", (C, 2 * FF)); W[f"dn{l}"] = din(f"dn{l}", (FF, C))
        if l % 2 == 0:
            for n in "rkv":
                W[f"{n}{l}"] = din(f"{n}{l}", (C, C))
            W[f"wo{l}"] = din(f"wo{l}", (C, C))
            W[f"w1{l}"] = din(f"w1{l}", (C, 96)); W[f"w2{l}"] = din(f"w2{l}", (96, C))
            W[f"a1{l}"] = din(f"a1{l}", (C, 96)); W[f"a2{l}"] = din(f"a2{l}", (96, C))
            W[f"g1{l}"] = din(f"g1{l}", (C, 256)); W[f"g2{l}"] = din(f"g2{l}", (256, C))
            if l >= 2:
                W[f"v1{l}"] = din(f"v1{l}", (C, 64)); W[f"v2{l}"] = din(f"v2{l}", (64, C))
        else:
            W[f"pw1{l}"] = din(f"pw1{l}", (C, 2 * C)); W[f"pw2{l}"] = din(f"pw2{l}", (C, C))

    SCR = {}

    def scr(name, n_oc, n_kc, Mb):
        SCR[name] = (nc.dram_tensor("s_" + name, [n_oc, 128, n_kc * Mb], BF16, kind="Internal").ap(), n_kc, Mb)

    for l in range(nlayers):
        scr(f"up{l}", 2 * NF, 16, 128); scr(f"dnA{l}", 16, 15, 128); scr(f"dnB{l}", 16, 14, 128); scr(f"dnC{l}", 16, 14, 128)
        if l % 2 == 0:
            for n in ("r", "k", "v", "wo"):
                scr(f"{n}{l}", 16, 16, 128)
            scr(f"g1a{l}", 1, 16, 128); scr(f"g1b{l}", 1, 16, 128)
            scr(f"w1{l}", 1, 16, 96); scr(f"a1{l}", 1, 16, 96)
            if l >= 2:
                scr(f"v1{l}", 1, 16, 128)
            scr(f"lo2{l}", 16, 5, 128)
        else:
            scr(f"pw1{l}", 32, 16, 128); scr(f"pw2{l}", 16, 16, 128)
    vfs = nc.dram_tensor("s_vfirst", [16, 128, TT], F32, kind="Internal").ap()

    def sb(name, shape, dt=F32):
        return es.enter_context(nc.sbuf_tensor(name, list(shape), dt))

    def ps(name, shape, dt=F32):
        return es.enter_context(nc.psum_tensor(name, list(shape), dt))

    prm_t = sb("prm_t", (128, lay.n))
    drv_t = sb("drv_t", (128, nlayers * 112))
    x_t = sb("x_t", (128, NC_, TT))
    hb = sb("hb", (128, NC_, TT + 1), BF16)
    ident = sb("ident", (128, 128), BF16)
    identF = sb("identF", (128, 128))
    mask2 = sb("mask2", (128, 256), BF16)
    mask2a = sb("mask2a", (128, 256), BF16)
    maskSL = sb("maskSL", (128, 128), BF16)
    maskUR = sb("maskUR", (128, 128), BF16)
    bo1 = sb("bo1", (128, 128), BF16)
    bo64 = sb("bo64", (128, 128), BF16)
    oneC = sb("oneC", (128, 128), BF16)
    rmask = sb("rmask", (128, TT))
    wsl = sb("wsl", (128, NSLOT, SLOT), BF16)
    nrw = (nlayers + 1) // 2
    ncv = nlayers // 2
    S_t = sb("S_t", (128, nrw * NC_ * 64))
    Sb_t = sb("Sb_t", (128, nrw * NC_ * 64), BF16)
    shc = sb("shc", (128, nrw, NC_), BF16)
    cvc = sb("cvc", (128, max(ncv, 1), NC_, 30))
    ffc = sb("ffc", (128, nlayers, NF, 2))

    def pc(k, i=0, n=1):
        o, _ = lay.off[k]
        return prm_t[:, o + i:o + i + n]

    def dc(l, i, n=1):
        return drv_t[:, l * 112 + i:l * 112 + i + n]
    omu = sb("omu", (128, nrw, 96))
    ca_t = sb("ca_t", (128, NC_))

    pd_t = [ps(f"pd{i}", (128, 512)) for i in range(3)]
    pw_t = [ps(f"pw{i}", (128, 512)) for i in range(4)]
    pt_t = ps("pt", (128, 8, 128), BF16)
    pd = Pool("pd", [t for t in pd_t])
    pw = Pool("pw", [t for t in pw_t])
    ptp = Pool("pt", [pt_t])

    def mm(o, lhsT, rhs, start, stop, r, w):
        P.add("pe", lambda e: e.matmul(o, lhsT=lhsT, rhs=rhs, start=start, stop=stop), r=r, w=w)

    def tr(o, in_, r, w):
        P.add("pe", lambda e: e.transpose(o, in_, ident[:, :]), r=list(r) + ["const"], w=w)

    def act(o, in_, func, r, w, bias=None, scale=None):
        kw = {}
        if bias is not None:
            kw["bias"] = bias
        if scale is not None:
            kw["scale"] = scale
        P.add("act", lambda e: e.activation(out=o, in_=in_, func=func, **kw), r=r, w=w)

    def tsc(eng, o, in0, s1, s2, op0, op1, r, w):
        if op1 is None:
            P.add(eng, lambda e: e.tensor_scalar(out=o, in0=in0, scalar1=s1, scalar2=None, op0=op0), r=r, w=w)
        else:
            P.add(eng, lambda e: e.tensor_scalar(out=o, in0=in0, scalar1=s1, scalar2=s2, op0=op0, op1=op1), r=r, w=w)

    def rsqrt(o, in_, eps, r, w, premax=None):
        if premax is not None:
            tsc("dve", o, in_, premax, None, ALU.max, None, r=r, w=w)
            act(o, o, AF.Ln, r=w, w=w)
        else:
            act(o, in_, AF.Ln, r=r, w=w, bias=eps)
        act(o, o, AF.Exp, r=w, w=w, scale=-0.5)

    def tt(eng, o, in0, in1, op, r, w):
        P.add(eng, lambda e: e.tensor_tensor(out=o, in0=in0, in1=in1, op=op), r=r, w=w)

    def stt(o, in0, scalar, in1, op0, op1, r, w):
        P.add("dve", lambda e: e.scalar_tensor_tensor(out=o, in0=in0, scalar=scalar, in1=in1, op0=op0, op1=op1), r=r, w=w)

    def cp(eng, o, in_, r, w):
        if eng == "act":
            act(o, in_, AF.Copy, r, w)
        else:
            P.add(eng, lambda e: e.tensor_copy(out=o, in_=in_), r=r, w=w)

    def dma(o, in_, r, w, semkey, at=None, eng="sp"):
        return P.add(eng, lambda e: e.dma_start(out=o, in_=in_), r=r, w=w, dma=True, semkey=semkey, at=at)

    with ExitStack() as pes:
        def psb(name, shape, dt=F32):
            return pes.enter_context(nc.sbuf_tensor(name, list(shape), dt))
        cst_t = psb("cst_t", (128, 10 * 128 + TT))
        dma(cst_t[:, :], cst[:, :], r=[], w=["cst_t"], semkey="cst_t")
        dma(prm_t[:, :], prm[:, :], r=[], w=["prm"], semkey="prm")
        for i, tgt in enumerate((ident, None, None, None, bo1, bo64, oneC)):
            if tgt is not None:
                cp("dve", tgt[:, :], cst_t[:, i * 128:(i + 1) * 128], r=["cst_t"], w=["const"])
        cp("dve", mask2[:, :], cst_t[:, 128:384], r=["cst_t"], w=["const"])
        co_ = 896 + TT
        cp("dve", mask2a[:, 0:128], cst_t[:, co_:co_ + 128], r=["cst_t"], w=["const"])
        cp("dve", mask2a[:, 128:256], cst_t[:, 256:384], r=["cst_t"], w=["const"])
        cp("dve", maskSL[:, :], cst_t[:, co_ + 128:co_ + 256], r=["cst_t"], w=["const"])
        cp("dve", maskUR[:, :], cst_t[:, co_ + 256:co_ + 384], r=["cst_t"], w=["const"])
        cp("dve", identF[:, :], cst_t[:, 0:128], r=["cst_t"], w=["const"])
        cp("act", rmask[:, :], cst_t[:, 896:896 + TT], r=["cst_t"], w=["const"])
        for t_, k_ in ((S_t, "S"), (Sb_t, "Sb"), (shc, "shc"), (cvc, "cvc"), (ffc, "ffc")):
            P.add("pool", (lambda tt_: (lambda e: e.memset(tt_, 0.0)))(t_[:]), r=[], w=[k_])
        act(ca_t[:, :], pc("c", 0, 16), AF.Silu, r=["prm"], w=["ca"])
        st32 = [psb(f"st32_{i}", (128, SLOT)) for i in range(3)]
        st16 = [psb(f"st16_{i}", (128, SLOT), BF16) for i in range(3)]
        cnt = [0]
        cast_engs = ("act", "dve", "pool")

        def cast_block(src_ap, kp, n_kc, Mb, dst_ap):
            i = cnt[0] % 3; cnt[0] += 1
            ne = n_kc * Mb
            s32 = st32[i][0:kp, 0:ne]; s16 = st16[i][0:kp, 0:ne]
            dma(s32.rearrange("p (k m) -> p k m", m=Mb) if n_kc > 1 else s32, src_ap, r=[], w=[f"st32_{i}"], semkey=f"st32_{i}")
            cp(cast_engs[(cnt[0]) % 3], s16, s32, r=[f"st32_{i}"], w=[f"st16_{i}"])
            dma(dst_ap, s16, r=[f"st16_{i}"], w=[], semkey=f"st16_{i}")

        def cast_mat(name, src, r0, n_kc, Mb, c0, n_oc):
            dst, nk, mb = SCR[name]
            for oc in range(n_oc):
                sap = src[r0:r0 + n_kc * 128, c0 + oc * Mb:c0 + (oc + 1) * Mb].rearrange("(k p) m -> p k m", p=128)
                cast_block(sap, 128, n_kc, Mb, dst[oc, :, :])

        for l in range(nlayers):
            cast_mat(f"up{l}", W[f"up{l}"], 0, 16, 128, 0, 2 * NF)
            cast_mat(f"dnA{l}", W[f"dn{l}"], 0, 15, 128, 0, 16)
            cast_mat(f"dnB{l}", W[f"dn{l}"], 15 * 128, 14, 128, 0, 16)
            cast_mat(f"dnC{l}", W[f"dn{l}"], 29 * 128, 14, 128, 0, 16)
            if l % 2 == 0:
                for n in ("r", "k", "v", "wo"):
                    cast_mat(f"{n}{l}", W[f"{n}{l}"], 0, 16, 128, 0, 16)
                cast_mat(f"g1a{l}", W[f"g1{l}"], 0, 16, 128, 0, 1)
                cast_mat(f"g1b{l}", W[f"g1{l}"], 0, 16, 128, 128, 1)
                cast_mat(f"w1{l}", W[f"w1{l}"], 0, 16, 96, 0, 1)
                cast_mat(f"a1{l}", W[f"a1{l}"], 0, 16, 96, 0, 1)
                if l >= 2:
                    i = cnt[0] % 3; cnt[0] += 1
                    s16f = st16[i][:, 0:C]
                    P.add("pool", (lambda a_: (lambda e: e.memset(a_, 0.0)))(s16f), r=[], w=[f"st16_{i}"])
                    s32 = st32[i][:, 0:16 * 64]
                    dma(s32.rearrange("p (k m) -> p k m", m=64), W[f"v1{l}"].rearrange("(k p) m -> p k m", p=128),
                        r=[], w=[f"st32_{i}"], semkey=f"st32_{i}")
                    cp(cast_engs[cnt[0] % 3], s16f.rearrange("p (k m) -> p k m", m=128)[:, :, 0:64],
                       s32.rearrange("p (k m) -> p k m", m=64), r=[f"st32_{i}"], w=[f"st16_{i}"])
                    dma(SCR[f"v1{l}"][0][0, :, :], s16f, r=[f"st16_{i}"], w=[], semkey=f"st16_{i}")
                lo2 = SCR[f"lo2{l}"][0]
                srcs = [(W[f"w2{l}"], 0, 96), (W[f"a2{l}"], 0, 96)]
                srcs.append((W[f"v2{l}"], 0, 64) if l >= 2 else None)
                srcs += [(W[f"g2{l}"], 0, 128), (W[f"g2{l}"], 128, 128)]
                for s_i, sd in enumerate(srcs):
                    i = cnt[0] % 3; cnt[0] += 1
                    s16f = st16[i][:, 0:C]
                    P.add("pool", (lambda a_: (lambda e: e.memset(a_, 0.0)))(s16f), r=[], w=[f"st16_{i}"])
                    if sd is not None:
                        src, r0, kp = sd
                        s32 = st32[i][0:kp, 0:C]; s16 = st16[i][0:kp, 0:C]
                        dma(s32, src[r0:r0 + kp, :], r=[], w=[f"st32_{i}"], semkey=f"st32_{i}")
                        cp(cast_engs[cnt[0] % 3], s16, s32, r=[f"st32_{i}"], w=[f"st16_{i}"])
                    dma(lo2[:, :, s_i * 128:(s_i + 1) * 128].rearrange("f p j -> p f j"),
                        s16f.rearrange("p (f j) -> p f j", j=128), r=[f"st16_{i}"], w=[], semkey=f"st16_{i}")
            else:
                cast_mat(f"pw1{l}", W[f"pw1{l}"], 0, 16, 128, 0, 32)
                cast_mat(f"pw2{l}", W[f"pw2{l}"], 0, 16, 128, 0, 16)

        adw = [psb(f"adw{i}", (128, NC_, 128)) for i in range(2)]
        for l in range(nlayers):
            pm_, pmk = pd.get()
            for oc in range(96):
                a_, ak = adw[oc % 2], f"adw{oc % 2}"
                dma(a_[:, :, :], W[f"ada{l}"][:, oc * 128:(oc + 1) * 128].rearrange("(k p) m -> p k m", p=128),
                    r=[], w=[ak], semkey=ak)
                for kc in range(NC_):
                    mm(pm_[:, oc:oc + 1], a_[:, kc, :], ca_t[:, kc:kc + 1], kc == 0, kc == NC_ - 1, r=[ak, "ca"], w=[pmk])
            mod = psb(f"mod{l}", (128, 96))
            tt("dve", mod[:, :], pm_[:, 0:96], pc(f"adab{l}", 0, 96), ALU.add, r=[pmk, "prm"], w=[f"mod{l}"])
            for hlf, gname in ((0, f"gmix{l}"), (1, f"gffn{l}")):
                b0 = hlf * 48
                stt(dc(l, b0, 16), mod[:, (3 * hlf + 1) * 16:(3 * hlf + 2) * 16], 1.0, pc(gname, 0, 16), ALU.add, ALU.mult,
                    r=[f"mod{l}", "prm"], w=["drv"])
                cp("dve", dc(l, b0 + 16, 16), mod[:, (3 * hlf) * 16:(3 * hlf + 1) * 16], r=[f"mod{l}"], w=["drv"])
                cp("dve", dc(l, b0 + 32, 16), mod[:, (3 * hlf + 2) * 16:(3 * hlf + 3) * 16], r=[f"mod{l}"], w=["drv"])
            if l % 2 == 0:
                tsc("dve", dc(l, 96, 16), pc(f"ka{l}", 0, 16), -1.0, 1.0, ALU.mult, ALU.add, r=["prm"], w=["drv"])
                tsc("dve", omu[:, l // 2, :], pc(f"mu{l}", 0, 96), -1.0, 1.0, ALU.mult, ALU.add, r=["prm"], w=["drv"])
            else:
                tt("dve", dc(l, 96, 16), pc(f"b2{l}", 0, 16), dc(l, 32, 16), ALU.mult, r=["prm", "drv"], w=["drv"])
    P.barrier()
    prologue_end = P.pos()

    slot_i = [0]
    slot_last = [prologue_end] * NSLOT

    def wget(name, oc):
        dst, n_kc, Mb = SCR[name]
        j = slot_i[0] % NSLOT; slot_i[0] += 1
        ne = n_kc * Mb
        at = max(slot_last[j], prologue_end)
        dma(wsl[:, j, 0:ne], dst[oc, :, :], r=[], w=[f"ws{j}"], semkey=f"ws{j}", at=at)
        return j, f"ws{j}"

    def wdone(j):
        slot_last[j] = P.pos()

    def wblk(j, kc, Mb=128, kp=128):
        return wsl[0:kp, j, kc * Mb:(kc + 1) * Mb]

    def mkpool(name, n, shape, dt=F32):
        return Pool(name, [sb(f"{name}{i}", shape, dt) for i in range(n)])

    sqp = mkpool("sq", 2, (128, TT), BF16)
    f32p = mkpool("f", 4, (128, TT))
    rwp = mkpool("rw", 8, (128, TT))
    rstd_p = mkpool("rstd", 2, (128, TT))
    arena = sb("arena", (128, 12288))
    arena_b = arena[:, :].bitcast(BF16)
    xm_v = arena_b.rearrange("p (a k t) -> p a k t", a=3, k=NC_)
    xm = [xm_v[:, i] for i in range(3)]
    hid = arena_b[:, 0:NF * TT].rearrange("p (f t) -> p f t", t=TT)
    cb_t = arena[:, 0:NC_ * TT].rearrange("p (k t) -> p k t", t=TT)
    gbp = mkpool("gb", 1, (128, TT + 2))
    ubp = mkpool("ub", 1, (128, TT + 30))
    l1p = mkpool("l1", 5, (128, TT), BF16)
    b16p = mkpool("b", 6, (128, TT), BF16)
    ARp = mkpool("AR", 1, (128, NCH, 256), BF16)
    TRp = mkpool("TR", 3, (128, NCH, 128), BF16)
    e_p = mkpool("E", 4, (128, 256), BF16)
    am_p = mkpool("AM", 2, (128, 256), BF16)
    m_p = mkpool("M", 3, (128, 128), BF16)
    ct_p = mkpool("Ct", 2, (128, 128), BF16)
    tt_p = mkpool("Tt", 3, (128, 128), BF16)
    xu_p = mkpool("XU", 6, (128, 64), BF16)
    yt_p = mkpool("yt", 1, (128, TT))
    xs_p = mkpool("xs", 2, (128, 64))
    ytok_p = mkpool("ytok", 1, (128, NCH, 128))
    vf_p = mkpool("vf", 1, (128, TT))
    o_p = f32p

    def norm_mod(l, which, func=AF.Identity):
        b0 = 0 if which == "mix" else 48
        acc, ak = pd.get()
        for kc in range(NC_):
            s, sk = sqp.get()
            act(s[:, :], x_t[:, kc, :], AF.Square, r=[f"x{kc}"], w=[sk])
            mm(acc[:, :], oneC[:, :], s[:, :], kc == 0, kc == NC_ - 1, r=[sk, "const"], w=[ak])
        rs, rk = rstd_p.get()
        rsqrt(rs[:, :], acc[:, :], RMS_EPS, r=[ak], w=[rk])
        for kc in range(NC_):
            t, tk = f32p.get()
            stt(t[:, :], x_t[:, kc, :], dc(l, b0 + kc), rs[:, :], ALU.mult, ALU.mult, r=[f"x{kc}", "drv", rk], w=[tk])
            act(hb[:, kc, 1:TT + 1], t[:, :], func, bias=dc(l, b0 + 16 + kc), r=[tk, "drv"], w=[f"hb{kc}"])

    def proj16(name, oc, src_keys, rhs_of, M=128):
        j, wk = wget(name, oc)
        acc, ak = pd.get()
        Mb = SCR[name][2]
        for kc in range(NC_):
            mm(acc[0:M, :], wblk(j, kc, Mb), rhs_of(kc), kc == 0, kc == NC_ - 1, r=[wk, src_keys[kc]], w=[ak])
        wdone(j)
        return acc, ak

    hbk = [f"hb{kc}" for kc in range(NC_)]

    def ffn(l):
        norm_mod(l, "ffn")
        for fc in range(NF):
            pg, pgk = proj16(f"up{l}", fc, hbk, lambda kc: hb[:, kc, 1:TT + 1])
            pv, pvk = proj16(f"up{l}", NF + fc, hbk, lambda kc: hb[:, kc, 1:TT + 1])
            gb, gk = gbp.get()
            cp("pool", gb[:, 0:2], ffc[:, l, fc, :], r=["ffc"], w=[gk])
            act(gb[:, 2:TT + 2], pg[:, :], AF.Copy, r=[pgk], w=[gk])
            t1, t1k = f32p.get()
            act(t1[:, :], pg[:, :], AF.Identity, scale=pc(f"fdw{l}", 2 * NF + fc), bias=pc(f"fdb{l}", fc), r=[pgk, "prm"], w=[t1k])
            cp("pool", ffc[:, l, fc, :], gb[:, TT:TT + 2], r=[gk], w=["ffc"])
            t2, t2k = f32p.get()
            stt(t2[:, :], gb[:, 1:TT + 1], pc(f"fdw{l}", NF + fc), t1[:, :], ALU.mult, ALU.add, r=[gk, t1k, "prm"], w=[t2k])
            t3, t3k = f32p.get()
            stt(t3[:, :], gb[:, 0:TT], pc(f"fdw{l}", fc), t2[:, :], ALU.mult, ALU.add, r=[gk, t2k, "prm"], w=[t3k])
            t4, t4k = f32p.get()
            act(t4[:, :], t3[:, :], AF.Silu, r=[t3k], w=[t4k])
            tt("dve", hid[:, fc, :], pv[:, :], t4[:, :], ALU.mult, r=[t4k, pvk], w=[f"hid{fc}"])
        for oc in range(NC_):
            acc, ak = pd.get()
            for nm_, k0, nk in ((f"dnA{l}", 0, 15), (f"dnB{l}", 15, 14), (f"dnC{l}", 29, 14)):
                j, wk = wget(nm_, oc)
                for kc in range(nk):
                    mm(acc[:, :], wblk(j, kc), hid[:, k0 + kc, :], k0 + kc == 0, k0 + kc == NF - 1, r=[wk, f"hid{k0 + kc}"], w=[ak])
                wdone(j)
            stt(x_t[:, oc, :], acc[:, :], dc(l, 80 + oc), x_t[:, oc, :], ALU.mult, ALU.add, r=[ak, "drv", f"x{oc}"], w=[f"x{oc}"])

    def conv(l):
        j_ = l // 2
        norm_mod(l, "mix")
        for cc in range(NC_):
            pa, pak = proj16(f"pw1{l}", cc, hbk, lambda kc: hb[:, kc, 1:TT + 1])
            pg, pgk = proj16(f"pw1{l}", NC_ + cc, hbk, lambda kc: hb[:, kc, 1:TT + 1])
            sg, sgk = f32p.get()
            act(sg[:, :], pg[:, :], AF.Sigmoid, bias=pc(f"b1{l}", NC_ + cc), r=[pgk, "prm"], w=[sgk])
            ub, uk = ubp.get()
            cp("pool", ub[:, 0:30], cvc[:, j_, cc, :], r=["cvc"], w=[uk])
            stt(ub[:, 30:TT + 30], pa[:, :], pc(f"b1{l}", cc), sg[:, :], ALU.add, ALU.mult, r=[pak, sgk, "prm"], w=[uk])
            cp("pool", cvc[:, j_, cc, :], ub[:, TT:TT + 30], r=[uk], w=["cvc"])
            a0, a0k = f32p.get(); a1, a1k = f32p.get()
            tsc("dve", a0[:, :], ub[:, 0:TT], pc(f"cw{l}", 0 * 16 + cc), pc(f"cb{l}", cc), ALU.mult, ALU.add, r=[uk, "prm"], w=[a0k])
            tsc("dve", a1[:, :], ub[:, 1:TT + 1], pc(f"cw{l}", 1 * 16 + cc), None, ALU.mult, None, r=[uk, "prm"], w=[a1k])
            for k in range(2, 31):
                a, ak_ = (a0, a0k) if k % 2 == 0 else (a1, a1k)
                stt(a[:, :], ub[:, k:k + TT], pc(f"cw{l}", k * 16 + cc), a[:, :], ALU.mult, ALU.add, r=[uk, "prm", ak_], w=[ak_])
            tt("dve", cb_t[:, cc, :], a0[:, :], a1[:, :], ALU.add, r=[a0k, a1k], w=[f"cb{cc}"])
        pm_, pmk = pd.get()
        for cc in range(NC_):
            s, sk = sqp.get()
            cp("act", s[:, :], cb_t[:, cc, :], r=[f"cb{cc}"], w=[sk])
            mm(pm_[:, :], oneC[:, :], s[:, :], cc == 0, cc == NC_ - 1, r=[sk, "const"], w=[pmk])
        pv_, pvk = pd.get()
        for cc in range(NC_):
            stt(cb_t[:, cc, :], pm_[:, :], -1.0, cb_t[:, cc, :], ALU.mult, ALU.add, r=[f"cb{cc}", pmk], w=[f"cb{cc}"])
            s, sk = sqp.get()
            act(s[:, :], cb_t[:, cc, :], AF.Square, r=[f"cb{cc}"], w=[sk])
            mm(pv_[:, :], oneC[:, :], s[:, :], cc == 0, cc == NC_ - 1, r=[sk, "const"], w=[pvk])
        rs, rk = rstd_p.get()
        rsqrt(rs[:, :], pv_[:, :], LN_EPS, r=[pvk], w=[rk])
        for cc in range(NC_):
            t, tk = f32p.get()
            stt(t[:, :], cb_t[:, cc, :], pc(f"clg{l}", cc), rs[:, :], ALU.mult, ALU.mult, r=[f"cb{cc}", "prm", rk], w=[tk])
            act(hb[:, cc, 1:TT + 1], t[:, :], AF.Silu, bias=pc(f"clb{l}", cc), r=[tk, "prm"], w=[f"hb{cc}"])
        for oc in range(NC_):
            po, pok = proj16(f"pw2{l}", oc, hbk, lambda kc: hb[:, kc, 1:TT + 1])
            t, tk = f32p.get()
            act(t[:, :], po[:, :], AF.Identity, scale=dc(l, 32 + oc), bias=dc(l, 96 + oc), r=[pok, "drv"], w=[tk])
            tt("pool", x_t[:, oc, :], x_t[:, oc, :], t[:, :], ALU.add, r=[tk, f"x{oc}"], w=[f"x{oc}"])

    def rwkv(l, ti):
        j_ = l // 2
        vfirst = l >= 2
        norm_mod(l, "mix")
        for kc in range(NC_):
            cp("pool", hb[:, kc, 0:1], shc[:, j_, kc:kc + 1], r=["shc"], w=[f"hb{kc}"])
            cp("pool", shc[:, j_, kc:kc + 1], hb[:, kc, TT:TT + 1], r=[f"hb{kc}"], w=["shc"])

        def mix(n, dst, di):
            for kc in range(NC_):
                t, tk = f32p.get()
                act(t[:, :], hb[:, kc, 1:TT + 1], AF.Identity, scale=omu[:, j_, n * 16 + kc:n * 16 + kc + 1], r=[f"hb{kc}", "drv"], w=[tk])
                stt(dst[:, kc, :], hb[:, kc, 0:TT], pc(f"mu{l}", n * 16 + kc), t[:, :], ALU.mult, ALU.add,
                    r=[f"hb{kc}", "prm", tk], w=[f"xm{di}_{kc}"])
        if dbg == "r1":
            return
        xk0 = [f"xm0_{kc}" for kc in range(NC_)]
        xk1 = [f"xm1_{kc}" for kc in range(NC_)]
        xk2 = [f"xm2_{kc}" for kc in range(NC_)]
        mix(5, xm[0], 0)
        g1s = []
        for hf, nm in enumerate((f"g1a{l}", f"g1b{l}")):
            p_, pk_ = proj16(nm, 0, xk0, lambda kc: xm[0][:, kc, :])
            g, gk = l1p.get()
            act(g[:, :], p_[:, :], AF.Sigmoid, r=[pk_], w=[gk])
            g1s.append((g, gk))
        if dbg == "r2":
            return
        mix(3, xm[0], 0)
        p_, pk_ = proj16(f"w1{l}", 0, xk0, lambda kc: xm[0][:, kc, :], M=96)
        lw1, lw1k = l1p.get()
        act(lw1[0:96, :], p_[0:96, :], AF.Tanh, r=[pk_], w=[lw1k])
        mix(4, xm[0], 0)
        p_, pk_ = proj16(f"a1{l}", 0, xk0, lambda kc: xm[0][:, kc, :], M=96)
        la1, la1k = l1p.get()
        act(la1[0:96, :], p_[0:96, :], AF.Copy, r=[pk_], w=[la1k])
        mix(2, xm[2], 2)
        if vfirst:
            p_, pk_ = proj16(f"v1{l}", 0, xk2, lambda kc: xm[2][:, kc, :])
            lv1, lv1k = l1p.get()
            act(lv1[:, :], p_[:, :], AF.Copy, r=[pk_], w=[lv1k])
        mix(1, xm[1], 1)
        mix(0, xm[0], 0)

        if dbg == "r3":
            return
        for fc in range(NC_):
            if dbg in ("r4", "r5", "r6", "u1", "u2", "u3", "u4", "u5", "u4a", "u4b", "u4c") and fc > 0:
                return
            Sk = f"S{j_}_{fc}"; Sbk = f"Sb{j_}_{fc}"
            pr, prk = proj16(f"r{l}", fc, xk0, lambda kc: xm[0][:, kc, :])
            r_, rk_ = rwp.get()
            act(r_[:, :], pr[:, :], AF.Copy, r=[prk], w=[rk_])
            pk, pkk = proj16(f"k{l}", fc, xk1, lambda kc: xm[1][:, kc, :])
            k_, kk_ = rwp.get()
            act(k_[:, :], pk[:, :], AF.Copy, r=[pkk], w=[kk_])
            pv, pvk = proj16(f"v{l}", fc, xk2, lambda kc: xm[2][:, kc, :])
            v_, vk_ = rwp.get()
            act(v_[:, :], pv[:, :], AF.Copy, r=[pvk], w=[vk_])
            j2, w2k = wget(f"lo2{l}", fc)
            pw_, pwk = pd.get()
            mm(pw_[:, :], wblk(j2, 0, 128, 96), lw1[0:96, :], True, True, r=[w2k, lw1k], w=[pwk])
            sw, swk = rwp.get()
            act(sw[:, :], pw_[:, :], AF.Sigmoid, bias=pc(f"w0{l}", fc), r=[pwk, "prm"], w=[swk])
            pa_, pak = pd.get()
            mm(pa_[:, :], wblk(j2, 1, 128, 96), la1[0:96, :], True, True, r=[w2k, la1k], w=[pak])
            asg, ask = rwp.get()
            act(asg[:, :], pa_[:, :], AF.Sigmoid, bias=pc(f"a0{l}", fc), r=[pak, "prm"], w=[ask])
            pg_, pgk = pd.get()
            mm(pg_[:, :], wblk(j2, 3), g1s[0][0][:, :], True, False, r=[w2k, g1s[0][1]], w=[pgk])
            mm(pg_[:, :], wblk(j2, 4), g1s[1][0][:, :], False, True, r=[w2k, g1s[1][1]], w=[pgk])
            gt, gtk = b16p.get()
            act(gt[:, :], pg_[:, :], AF.Copy, r=[pgk], w=[gtk])
            if vfirst:
                pv2, pv2k = pd.get()
                mm(pv2[:, :], wblk(j2, 2), lv1[:, :], True, True, r=[w2k, lv1k], w=[pv2k])
                sv, svk = f32p.get()
                act(sv[:, :], pv2[:, :], AF.Sigmoid, bias=pc(f"v0{l}", fc), r=[pv2k, "prm"], w=[svk])
                vf, vfk = vf_p.get()
                dma(vf[:, :], vfs[fc, :, :], r=[f"vfs{fc}"], w=[vfk], semkey=vfk)
                tt("pool", vf[:, :], vf[:, :], v_[:, :], ALU.subtract, r=[vfk, vk_], w=[vfk])
                tt("pool", vf[:, :], vf[:, :], sv[:, :], ALU.mult, r=[vfk, svk], w=[vfk])
                tt("pool", v_[:, :], v_[:, :], vf[:, :], ALU.add, r=[vfk, vk_], w=[vk_])
            elif nlayers > 2:
                dma(vfs[fc, :, :], v_[:, :], r=[vk_], w=[f"vfs{fc}"], semkey=vk_)
            wdone(j2)
            if dbg == "r4":
                return
            kkr, kkrk = rwp.get()
            tsc("dve", kkr[:, :], k_[:, :], pc(f"kk{l}", fc), None, ALU.mult, None, r=[kk_, "prm"], w=[kkrk])
            s, sk = sqp.get()
            act(s[:, :], kkr[:, :], AF.Square, r=[kkrk], w=[sk])
            pn, pnk = pd.get()
            mm(pn[:, :], bo1[:, :], s[:, :], True, True, r=[sk, "const"], w=[pnk])
            rn, rnk = rstd_p.get()
            rsqrt(rn[:, :], pn[:, :], None, r=[pnk], w=[rnk], premax=1e-24)
            tt("dve", kkr[:, :], kkr[:, :], rn[:, :], ALU.mult, r=[kkrk, rnk], w=[kkrk])
            f_, fk = f32p.get()
            act(f_[:, :], asg[:, :], AF.Identity, scale=pc(f"ka{l}", fc), bias=dc(l, 96 + fc), r=[ask, "prm", "drv"], w=[fk])
            tt("pool", k_[:, :], k_[:, :], f_[:, :], ALU.mult, r=[kk_, fk], w=[kk_])
            tt("pool", asg[:, :], asg[:, :], kkr[:, :], ALU.mult, r=[ask, kkrk], w=[ask])
            cw, cwk = rwp.get()
            P.add("dve", lambda e, cw=cw, sw=sw: e.tensor_tensor_scan(out=cw[:, :], data0=rmask[:, :], data1=sw[:, :],
                                                                     initial=0.0, op0=ALU.mult, op1=ALU.add),
                  r=[swk, "const"], w=[cwk])
            tt("pool", sw[:, :], cw[:, :], sw[:, :], ALU.subtract, r=[cwk, swk], w=[swk])
            Wt, Wk = rwp.get()
            act(Wt[:, :], cw[:, :], AF.Exp, scale=-C0, r=[cwk], w=[Wk])
            act(sw[:, :], sw[:, :], AF.Exp, scale=-C0, r=[swk], w=[swk])
            act(cw[:, :], cw[:, :], AF.Exp, scale=C0, r=[cwk], w=[cwk])
            AR, ARk = ARp.get()
            tt("dve", AR[:, :, 128:256], r_[:, :].rearrange("p (c t) -> p c t", t=L), Wt[:, :].rearrange("p (c t) -> p c t", t=L),
               ALU.mult, r=[rk_, Wk], w=[ARk])
            stt(AR[:, :, 0:128], kkr[:, :].rearrange("p (c t) -> p c t", t=L), -1.0, sw[:, :].rearrange("p (c t) -> p c t", t=L),
                ALU.mult, ALU.mult, r=[kkrk, swk], w=[ARk])
            BT, BTk = b16p.get()
            tt("dve", BT[:, :], asg[:, :], cw[:, :], ALU.mult, r=[ask, cwk], w=[BTk])
            KT, KTk = b16p.get()
            tt("dve", KT[:, :], k_[:, :], cw[:, :], ALU.mult, r=[kk_, cwk], w=[KTk])
            bend, bendk = b16p.get(); kend, kendk = b16p.get()
            for c in range(NCH):
                wl = Wt[:, c * L + L - 1:c * L + L]
                tsc("pool", bend[:, c * L:(c + 1) * L], BT[:, c * L:(c + 1) * L], wl, None, ALU.mult, None, r=[BTk, Wk], w=[bendk])
                tsc("pool", kend[:, c * L:(c + 1) * L], KT[:, c * L:(c + 1) * L], wl, None, ALU.mult, None, r=[KTk, Wk], w=[kendk])
            vb, vbk = b16p.get()
            cp("act", vb[:, :], v_[:, :], r=[vk_], w=[vbk])
            rkr, rkrk = sqp.get()
            stt(rkr[:, :], r_[:, :], pc(f"rk{l}", fc), k_[:, :], ALU.mult, ALU.mult, r=[rk_, kk_, "prm"], w=[rkrk])
            if dbg == "r5":
                return
            VT, VTk = TRp.get(); BET, BETk = TRp.get(); KET, KETk = TRp.get()
            for ei, (src, srck, dst, dstk) in enumerate(((vb, vbk, VT, VTk), (bend, bendk, BET, BETk), (kend, kendk, KET, KETk))):
                p_, pk_ = ptp.get()
                for c in range(NCH):
                    tr(p_[:, c, :], src[:, c * L:(c + 1) * L], r=[srck], w=[pk_])
                cp("act" if ei % 2 == 0 else "dve", dst[:, :, :], p_[:, 0:NCH, :], r=[pk_], w=[dstk])
            if dbg == "r6":
                return
            pb, pbk = pd.get()
            mm(pb[:, :], bo1[:, :], rkr[:, :], True, True, r=[rkrk, "const"], w=[pbk])
            tt("dve", v_[:, :], pb[:, :], v_[:, :], ALU.mult, r=[vk_, pbk], w=[vk_])
            yt, ytk = yt_p.get()
            ytok, ytokk = ytok_p.get()
            for c in range(NCH):
                for hh in range(2):
                    rows = slice(hh * 64, hh * 64 + 64)
                    if dbg in ("u1", "u2", "u3", "u4", "u4a", "u4b", "u4c") and (hh > 0 or c > 0):
                        continue
                    if dbg == "u5" and c > 0:
                        continue
                    ARc = AR[:, c, :]
                    BTc = BT[:, c * L:(c + 1) * L]; KTc = KT[:, c * L:(c + 1) * L]
                    bA, bAk = pw.get(); bB, bBk = pw.get()
                    P1 = bA[:, 0:256]; P2 = bA[:, 256:512]; P3 = bB[:, 0:128]
                    mm(P1, BTc[rows, :], ARc[rows, :], True, True, r=[BTk, ARk], w=[bAk])
                    mm(P2, KTc[rows, :], ARc[rows, :], True, True, r=[KTk, ARk], w=[bAk])
                    mm(P3, ARc[rows, 0:128], BTc[rows, :], True, True, r=[BTk, ARk], w=[bBk])
                    E1, E1k = e_p.get(); E2, E2k = e_p.get()
                    tt("dve", E1[:, :], P1, mask2a[:, :], ALU.mult, r=[bAk, "const"], w=[E1k])
                    Ct, Ctk = ct_p.get()
                    tt("dve", Ct[:, :], P1[:, 0:128], maskUR[:, :], ALU.mult, r=[bAk, "const"], w=[Ctk])
                    tt("dve", E2[:, :], P2, mask2[:, :], ALU.mult, r=[bAk, "const"], w=[E2k])
                    M0, M0k = m_p.get()
                    tt("dve", M0[:, :], P3, maskSL[:, :], ALU.mult, r=[bBk, "const"], w=[M0k])
                    if dbg == "u1":
                        continue
                    Tt, Ttk = tt_p.get()
                    tt("pool", Tt[:, :], E1[:, 0:128], ident[:, :], ALU.add, r=[E1k, "const"], w=[Ttk])
                    Aprev, Apk = E1[:, 0:128], E1k
                    Mprev, Mpk = M0[:, :], M0k
                    Q = bB[:, 0:256]; Q2 = bB[:, 256:384]
                    for lev in range(1, 6):
                        if lev < 5:
                            mm(Q[:, 0:128], Mprev, Aprev, True, True, r=[Apk, Mpk], w=[bBk])
                            mm(Q[:, 128:256], Aprev, Mprev, True, True, r=[Apk, Mpk], w=[bBk])
                            AM, AMk = am_p.get()
                            cp("act", AM[:, :], Q, r=[bBk], w=[AMk])
                            Aprev, Apk = AM[:, 0:128], AMk
                            Mprev, Mpk = AM[:, 128:256], AMk
                        else:
                            mm(Q[:, 0:128], Aprev, Mprev, True, True, r=[Apk, Mpk], w=[bBk])
                            M6, M6k = m_p.get()
                            cp("act", M6[:, :], Q[:, 0:128], r=[bBk], w=[M6k])
                            Mprev, Mpk = M6[:, :], M6k
                        mm(Q2, ident[:, :], Tt[:, :], True, False, r=["const", Ttk], w=[bBk])
                        mm(Q2, Mprev, Tt[:, :], False, True, r=[Mpk, Ttk], w=[bBk])
                        Tn, Tnk = tt_p.get()
                        cp("dve" if lev % 2 else "act", Tn[:, :], Q2, r=[bBk], w=[Tnk])
                        Tt, Ttk = Tn, Tnk
                    if dbg == "u2":
                        continue
                    PX = bB[:, 384:448]; PU = bB[:, 448:512]
                    mm(PX, ARc[rows, 0:128], Sb_t[rows, (j_ * NC_ + fc) * 64:(j_ * NC_ + fc) * 64 + 64], True, False, r=[ARk, Sbk], w=[bBk])
                    mm(PX, E2[:, 0:128], VT[:, c, hh * 64:hh * 64 + 64], False, True, r=[E2k, VTk], w=[bBk])
                    X0, X0k = xu_p.get()
                    cp("act", X0[:, :], PX, r=[bBk], w=[X0k])
                    mm(PU, Tt[:, :], X0[:, :], True, True, r=[Ttk, X0k], w=[bBk])
                    W1, W1k = xu_p.get()
                    cp("dve", W1[:, :], PU, r=[bBk], w=[W1k])
                    mm(PX, ident[:, :], X0[:, :], True, False, r=["const", X0k], w=[bBk])
                    mm(PX, Ct[:, :], W1[:, :], False, True, r=[Ctk, W1k], w=[bBk])
                    X2, X2k = xu_p.get()
                    cp("act", X2[:, :], PX, r=[bBk], w=[X2k])
                    mm(PU, Tt[:, :], X2[:, :], True, True, r=[Ttk, X2k], w=[bBk])
                    U, Uk = xu_p.get()
                    cp("dve", U[:, :], PU, r=[bBk], w=[Uk])
                    if dbg == "u3":
                        continue
                    hc = slice(hh * 64, hh * 64 + 64)
                    PY = bA[:, 0:64]; PS = bA[:, 64:128]
                    mm(PY, ARc[rows, 128:256], Sb_t[rows, (j_ * NC_ + fc) * 64:(j_ * NC_ + fc) * 64 + 64], True, False, r=[Sbk, ARk], w=[bAk])
                    mm(PY, E1[:, 128:256], U[:, :], False, False, r=[Uk, E1k], w=[bAk])
                    mm(PY, E2[:, 128:256], VT[:, c, hc], False, True, r=[VTk, E2k], w=[bAk])
                    if dbg == "u4a":
                        cp("act", ytok[:, c, hc], PY, r=[bAk], w=[ytokk])
                        continue
                    mm(PS, BET[:, c, :], U[:, :], True, False, r=[BETk, Uk], w=[bAk])
                    mm(PS, KET[:, c, :], VT[:, c, hc], False, True, r=[KETk, VTk], w=[bAk])
                    cp("act", ytok[:, c, hc], PY, r=[bAk], w=[ytokk])
                    if dbg == "u4b":
                        continue
                    rows_ = rows
                    so = (j_ * NC_ + fc) * 64
                    stmp, stk = xs_p.get()
                    act(stmp[rows_, :], S_t[rows_, so:so + 64], AF.Identity, scale=Wt[rows_, c * L + L - 1:c * L + L], r=[Sk, Wk], w=[stk])
                    tt("dve", S_t[rows_, so:so + 64], PS[rows_, :], stmp[rows_, :], ALU.add, r=[stk, bAk], w=[Sk])
                    if dbg == "u4c":
                        continue
                    cp("act", Sb_t[rows_, (j_ * NC_ + fc) * 64:(j_ * NC_ + fc) * 64 + 64], S_t[rows_, (j_ * NC_ + fc) * 64:(j_ * NC_ + fc) * 64 + 64], r=[Sk], w=[Sbk])
            if dbg in ("u1", "u2", "u3", "u4", "u5", "u4a", "u4b", "u4c"):
                return
            py_, pyk = pd.get()
            for c in range(NCH):
                P.add("pe", (lambda o_, i_: (lambda e: e.transpose(o_, i_, identF[:, :])))(py_[:, c * L:(c + 1) * L], ytok[:, c, :]),
                      r=[ytokk, "const"], w=[pyk])
            cp("act", yt[:, :], py_[:, :], r=[pyk], w=[ytk])
            yb, ybk = sqp.get()
            cp("act", yb[:, :], yt[:, :], r=[ytk], w=[ybk])
            pm_, pmk = pd.get()
            mm(pm_[:, :], bo64[:, :], yb[:, :], True, True, r=[ybk, "const"], w=[pmk])
            stt(yt[:, :], pm_[:, :], -1.0, yt[:, :], ALU.mult, ALU.add, r=[ytk, pmk], w=[ytk])
            s2, s2k = sqp.get()
            act(s2[:, :], yt[:, :], AF.Square, r=[ytk], w=[s2k])
            pvv, pvvk = pd.get()
            mm(pvv[:, :], bo64[:, :], s2[:, :], True, True, r=[s2k, "const"], w=[pvvk])
            rs, rsk = rstd_p.get()
            rsqrt(rs[:, :], pvv[:, :], GN_EPS, r=[pvvk], w=[rsk])
            tt("dve", yt[:, :], yt[:, :], rs[:, :], ALU.mult, r=[ytk, rsk], w=[ytk])
            act(yt[:, :], yt[:, :], AF.Identity, scale=pc(f"lng{l}", fc), bias=pc(f"lnb{l}", fc), r=[ytk, "prm"], w=[ytk])
            tt("pool", yt[:, :], yt[:, :], v_[:, :], ALU.add, r=[ytk, vk_], w=[ytk])
            tt("pool", hb[:, fc, 1:TT + 1], yt[:, :], gt[:, :], ALU.mult, r=[ytk, gtk], w=[f"hb{fc}"])
        for oc in range(NC_):
            po, pok = proj16(f"wo{l}", oc, hbk, lambda kc: hb[:, kc, 1:TT + 1])
            stt(x_t[:, oc, :], po[:, :], dc(l, 32 + oc), x_t[:, oc, :], ALU.mult, ALU.add, r=[pok, "drv", f"x{oc}"], w=[f"x{oc}"])

    xv = xin.rearrange("(k p) t -> p k t", p=128)
    ov = out.rearrange("(k p) t -> p k t", p=128)
    for ti in range(NT):
        t0 = ti * TT
        for kc in range(NC_):
            dma(x_t[:, kc, :], xv[:, kc, t0:t0 + TT], r=[], w=[f"x{kc}"], semkey=f"x{kc}")
        for l in range(nlayers):
            if dbg == "pro":
                break
            if l % 2 == 0:
                rwkv(l, ti)
            else:
                conv(l)
            if dbg in ("mix", "r1", "r2", "r3", "r4", "r5", "r6", "u1", "u2", "u3", "u4", "u5", "u4a", "u4b", "u4c"):
                break
            ffn(l)
        acc, ak = pd.get()
        for kc in range(NC_):
            s, sk = sqp.get()
            act(s[:, :], x_t[:, kc, :], AF.Square, r=[f"x{kc}"], w=[sk])
            mm(acc[:, :], oneC[:, :], s[:, :], kc == 0, kc == NC_ - 1, r=[sk, "const"], w=[ak])
        rs, rk = rstd_p.get()
        rsqrt(rs[:, :], acc[:, :], RMS_EPS, r=[ak], w=[rk])
        for kc in range(NC_):
            o, ok = o_p.get()
            stt(o[:, :], x_t[:, kc, :], pc("fin", kc), rs[:, :], ALU.mult, ALU.mult, r=[f"x{kc}", "prm", rk], w=[ok])
            dma(ov[:, kc, t0:t0 + TT], o[:, :], r=[ok], w=[], semkey=ok)

    P.emit(nc, es)
    es.close()
    return nc, lay


def make_inmaps(inp, nlayers, T, lay, batches):
    maps = []
    cst = make_consts()
    for b in batches:
        m = {"x": np.ascontiguousarray(np.asarray(inp["x"][b], np.float32)[:T].T),
             "prm": pack_params(inp, b, nlayers, lay), "cst": cst}
        for l in range(nlayers):
            j = l // 2
            m[f"ada{l}"] = np.ascontiguousarray(inp["ada_w"][l]); m[f"up{l}"] = np.ascontiguousarray(inp["ffn_w_up"][l])
            m[f"dn{l}"] = np.ascontiguousarray(inp["ffn_w_down"][l])
            if l % 2 == 0:
                for i, n in enumerate("rkv"):
                    m[f"{n}{l}"] = np.ascontiguousarray(inp["rwkv_w_rkv"][j][i])
                m[f"wo{l}"] = np.ascontiguousarray(inp["rwkv_w_o"][j])
                for n in ("w1", "w2", "a1", "a2", "g1", "g2"):
                    m[f"{n}{l}"] = np.ascontiguousarray(inp["rwkv_" + n][j])
                if l >= 2:
                    m[f"v1{l}"] = np.ascontiguousarray(inp["rwkv_v1"][j - 1]); m[f"v2{l}"] = np.ascontiguousarray(inp["rwkv_v2"][j - 1])
            else:
                m[f"pw1{l}"] = np.ascontiguousarray(inp["conv_w_pw1"][j]); m[f"pw2{l}"] = np.ascontiguousarray(inp["conv_w_pw2"][j])
        maps.append(m)
    return maps


def run(inp, nlayers, T, batches, dbg=None):
    nc, lay = build(T, nlayers, dbg)
    maps = make_inmaps(inp, nlayers, T, lay, batches)
    res = run_bass_kernel_spmd(nc, maps, core_ids=list(range(len(batches))))
    return [np.ascontiguousarray(r["out"].T) for r in res.results]


def kernel(**inputs):
    inp = {k: np.asarray(v) for k, v in inputs.items()}
    B, T, _ = inp["x"].shape
    outs = run(inp, 4, T, [0, 1, 2, 3])
    return np.stack(outs[:4], axis=0).astype(np.float32)
```

```python
import numpy as np
from contextlib import ExitStack
import concourse.bass as bass
import concourse.mybir as mybir
from concourse.bass_utils import run_bass_kernel_spmd

F32 = mybir.dt.float32
BF16 = mybir.dt.bfloat16
AF = mybir.ActivationFunctionType
ALU = mybir.AluOpType

C = 2048
NC_ = 16
FF = 5504
NF = 43
TT = 512
L = 128
NCH = TT // L
C0 = 0.6065306597126334
RMS_EPS = 1e-6
LN_EPS = 1e-5
GN_EPS = 64e-5
SLOT = 2048
NSLOT = 3


class Op:
    __slots__ = ("eng", "fn", "r", "w", "dma", "semkey", "waits", "sig", "sigval", "idx", "barrier")

    def __init__(self, eng, fn, r, w, dma, semkey):
        self.eng = eng; self.fn = fn; self.r = tuple(r); self.w = tuple(w)
        self.dma = dma; self.semkey = semkey
        self.waits = []; self.sig = False; self.sigval = 0; self.barrier = False


class Prog:
    CE = ("pe", "act", "dve", "pool")
    ROT = 30000

    def __init__(self):
        self.ops = []
        self.ins = {}

    def add(self, eng, fn, r=(), w=(), dma=False, semkey=None, at=None):
        op = Op(eng, fn, r, w, dma, semkey)
        if at is None:
            self.ops.append(op)
        else:
            self.ins.setdefault(at, []).append(op)
        return op

    def pos(self):
        return len(self.ops)

    def barrier(self):
        for e in ("pe", "act", "dve", "pool", "sp"):
            op = Op(e, None, (), (), False, None)
            op.barrier = True
            self.ops.append(op)

    def flatten(self):
        flat = []
        n = len(self.ops)
        for i in range(n + 1):
            if i in self.ins:
                flat.extend(self.ins[i])
            if i < n:
                flat.append(self.ops[i])
        for i, o in enumerate(flat):
            o.idx = i
        return flat

    def analyze(self):
        flat = self.flatten()
        lastw = {}
        lastr = {}
        waited = {e: {} for e in ("pe", "act", "dve", "pool", "sp")}
        dmacnt = {}
        for op in flat:
            if op.dma:
                dmacnt[op.semkey] = dmacnt.get(op.semkey, 0) + 16
                op.sigval = dmacnt[op.semkey]
        lastop = {}
        dmalast = {}
        for i, op in enumerate(flat):
            if op.barrier:
                wd = waited[op.eng]
                for pe_, d in lastop.items():
                    if pe_ == op.eng:
                        continue
                    if wd.get(pe_, -1) < d:
                        wd[pe_] = d; op.waits.append((pe_, d)); flat[d].sig = True
                for k_, v_ in dmalast.items():
                    key = ("dma", k_)
                    if wd.get(key, -1) < v_:
                        wd[key] = v_; op.waits.append((key, v_))
                continue
            if op.dma:
                dmalast[op.semkey] = op.sigval
            else:
                lastop[op.eng] = i
            deps = set()
            raw = set()
            for k in op.r:
                if k in lastw:
                    raw.add(lastw[k])
            for k in op.w:
                if k in lastw:
                    deps.add(lastw[k])
                lr = lastr.get(k)
                if lr:
                    deps.update(lr.values())
            deps |= raw
            deps.discard(i)
            need = {}
            for d in deps:
                p = flat[d]
                if p.dma:
                    key = ("dma", p.semkey); val = p.sigval
                else:
                    if p.eng == op.eng and not op.dma:
                        if p.eng == "pe":
                            continue
                    key = p.eng; val = d
                if need.get(key, -1) < val:
                    need[key] = val
            wd = waited[op.eng]
            for key, val in need.items():
                if wd.get(key, -1) >= val:
                    continue
                wd[key] = val
                op.waits.append((key, val))
                if not isinstance(key, tuple):
                    flat[val].sig = True
            for k in op.w:
                lastw[k] = i
                lastr[k] = {}
            for k in op.r:
                lastr.setdefault(k, {})[op.eng + ("d" if op.dma else "")] = i
        cnt = {e: 0 for e in self.CE}
        for op in flat:
            if op.dma:
                op.sig = True
            elif op.sig:
                cnt[op.eng] += 1
                op.sigval = cnt[op.eng]
        self.flat = flat
        self.cnt = cnt
        self.dmakeys = sorted(dmacnt.keys())
        return flat

    def emit(self, nc, es):
        flat = self.analyze()
        sems = {}
        for e in self.CE:
            n = self.cnt[e] // self.ROT + 1
            sems[e] = [es.enter_context(nc.semaphore(f"s_{e}_{i}")) for i in range(n)]
        dsem = {k: es.enter_context(nc.semaphore(f"d_{j}")) for j, k in enumerate(self.dmakeys)}
        block = es.enter_context(nc.Block())
        ROT = self.ROT

        def semof(e, v):
            return sems[e][(v - 1) // ROT], (v - 1) % ROT + 1

        def run(engname, handle):
            for op in flat:
                if op.eng != engname:
                    continue
                for key, val in op.waits:
                    if isinstance(key, tuple):
                        handle.wait_ge(dsem[key[1]], val)
                    else:
                        s, v = semof(key, flat[val].sigval)
                        handle.wait_ge(s, v)
                if op.fn is None:
                    continue
                inst = op.fn(handle)
                if op.dma:
                    inst.then_inc(dsem[op.semkey], 16)
                elif op.sig:
                    s, v = semof(op.eng, op.sigval)
                    inst.then_inc(s, 1)
            if engname == "sp":
                last = {}
                for op in flat:
                    if op.dma:
                        last[op.semkey] = op.sigval
                for k, v in last.items():
                    handle.wait_ge(dsem[k], v)

        @block.sync
        def _(e):
            run("sp", e)

        @block.tensor
        def _(e):
            run("pe", e)

        @block.scalar
        def _(e):
            run("act", e)

        @block.vector
        def _(e):
            run("dve", e)

        @block.gpsimd
        def _(e):
            run("pool", e)


class Pool:
    def __init__(self, name, aps):
        self.name = name; self.aps = aps; self.i = 0

    def get(self):
        j = self.i % len(self.aps); self.i += 1
        return self.aps[j], f"{self.name}{j}"


def _cols(v):
    v = np.asarray(v, np.float32).reshape(-1)
    return np.ascontiguousarray(v.reshape(-1, 128).T)


class ParamLayout:
    def __init__(self, nlayers):
        self.off = {}
        self.n = 0
        for l in range(nlayers):
            self._a(f"adab{l}", 96); self._a(f"gmix{l}", 16); self._a(f"gffn{l}", 16)
            self._a(f"fdw{l}", 3 * NF); self._a(f"fdb{l}", NF)
            if l % 2 == 0:
                for nm, n in (("mu", 96), ("w0", 16), ("a0", 16), ("v0", 16), ("kk", 16), ("ka", 16),
                              ("rk", 16), ("lng", 16), ("lnb", 16)):
                    self._a(f"{nm}{l}", n)
            else:
                for nm, n in (("b1", 32), ("cw", 31 * 16), ("cb", 16), ("clg", 16), ("clb", 16), ("b2", 16)):
                    self._a(f"{nm}{l}", n)
        self._a("fin", 16); self._a("c", 16)

    def _a(self, k, n):
        self.off[k] = (self.n, n); self.n += n


def pack_params(inp, b, nlayers, lay):
    P = np.zeros((128, lay.n), np.float32)

    def put(k, arr):
        o, n = lay.off[k]
        a = _cols(arr)
        assert a.shape[1] == n, (k, a.shape, n)
        P[:, o:o + n] = a

    for l in range(nlayers):
        j = l // 2
        put(f"adab{l}", inp["ada_b"][l]); put(f"gmix{l}", inp["norm_mix_g"][l]); put(f"gffn{l}", inp["norm_ffn_g"][l])
        put(f"fdw{l}", inp["ffn_w_dw"][l]); put(f"fdb{l}", inp["ffn_b_dw"][l])
        if l % 2 == 0:
            put(f"mu{l}", inp["rwkv_mu"][j]); put(f"w0{l}", inp["rwkv_w0"][j]); put(f"a0{l}", inp["rwkv_a0"][j])
            if j > 0:
                put(f"v0{l}", inp["rwkv_v0"][j - 1])
            put(f"kk{l}", inp["rwkv_k_k"][j]); put(f"ka{l}", inp["rwkv_k_a"][j]); put(f"rk{l}", inp["rwkv_r_k"][j])
            put(f"lng{l}", inp["rwkv_lnx_g"][j]); put(f"lnb{l}", inp["rwkv_lnx_b"][j])
        else:
            put(f"b1{l}", inp["conv_b_pw1"][j]); put(f"cw{l}", inp["conv_w_dw"][j]); put(f"cb{l}", inp["conv_b_dw"][j])
            put(f"clg{l}", inp["conv_ln_g"][j]); put(f"clb{l}", inp["conv_ln_b"][j]); put(f"b2{l}", inp["conv_b_pw2"][j])
    put("fin", inp["final_norm_g"]); put("c", inp["c"][b])
    return P


def make_consts():
    K = np.zeros((128, 10 * 128 + TT), np.float32)
    i = np.arange(128)
    K[:, 0:128] = np.eye(128)
    K[:, 128:256] = (i[None, :] > i[:, None])
    K[:, 256:384] = (i[None, :] >= i[:, None])
    K[:, 384:512] = (i[None, :] < i[:, None])
    K[:, 512:640] = ((i[None, :] // 64) == (i[:, None] // 64))
    K[:, 640:768] = K[:, 512:640] / 64.0
    K[:, 768:896] = 1.0 / 2048.0
    m = np.ones(TT, np.float32); m[::L] = 0.0
    K[:, 896:896 + TT] = m[None, :]
    o = 896 + TT
    bd = ((i[None, :] // 64) == (i[:, None] // 64))
    K[:, o:o + 128] = K[:, 128:256] * bd
    K[:, o + 128:o + 256] = K[:, 384:512] * bd
    K[:, o + 256:o + 384] = (i[:, None] < 64) & (i[None, :] >= 64)
    return K


def build(T, nlayers, dbg=None):
    NT = T // TT
    lay = ParamLayout(nlayers)
    nc = bass.Bass("TRN2", target_bir_lowering=False)
    es = ExitStack()
    P = Prog()

    def din(name, shape):
        return nc.dram_tensor(name, list(shape), F32, kind="ExternalInput").ap()

    xin = din("x", (C, T))
    prm = din("prm", (128, lay.n))
    cst = din("cst", (128, 10 * 128 + TT))
    out = nc.dram_tensor("out", [C, T], F32, kind="ExternalOutput").ap()
    W = {}
    for l in range(nlayers):
        W[f"ada{l}"] = din(f"ada{l}", (C, 6 * C))
        W[f"up{l}"] = din(f"up{l}", (C, 2 * FF)); W[f"dn{l}"] = din(f"dn{l}", (FF, C))
        if l % 2 == 0:
            for n in "rkv":
                W[f"{n}{l}"] = din(f"{n}{l}", (C, C))
            W[f"wo{l}"] = din(f"wo{l}", (C, C))
            W[f"w1{l}"] = din(f"w1{l}", (C, 96)); W[f"w2{l}"] = din(f"w2{l}", (96, C))
            W[f"a1{l}"] = din(f"a1{l}", (C, 96)); W[f"a2{l}"] = din(f"a2{l}", (96, C))
            W[f"g1{l}"] = din(f"g1{l}", (C, 256)); W[f"g2{l}"] = din(f"g2{l}", (256, C))
            if l >= 2:
                W[f"v1{l}"] = din(f"v1{l}", (C, 64)); W[f"v2{l}"] = din(f"v2{l}", (64, C))
        else:
            W[f"pw1{l}"] = din(f"pw1{l}", (C, 2 * C)); W[f"pw2{l}"] = din(f"pw2{l}", (C, C))

    SCR = {}

    def scr(name, n_oc, n_kc, Mb):
        SCR[name] = (nc.dram_tensor("s_" + name, [n_oc, 128, n_kc * Mb], BF16, kind="Internal").ap(), n_kc, Mb)

    for l in range(nlayers):
        scr(f"up{l}", 2 * NF, 16, 128); scr(f"dnA{l}", 16, 15, 128); scr(f"dnB{l}", 16, 14, 128); scr(f"dnC{l}", 16, 14, 128)
        if l % 2 == 0:
            for n in ("r", "k", "v", "wo"):
                scr(f"{n}{l}", 16, 16, 128)
            scr(f"g1a{l}", 1, 16, 128); scr(f"g1b{l}", 1, 16, 128)
            scr(f"w1{l}", 1, 16, 96); scr(f"a1{l}", 1, 16, 96)
            if l >= 2:
                scr(f"v1{l}", 1, 16, 128)
            scr(f"lo2{l}", 16, 5, 128)
        else:
            scr(f"pw1{l}", 32, 16, 128); scr(f"pw2{l}", 16, 16, 128)
    vfs = nc.dram_tensor("s_vfirst", [16, 128, TT], F32, kind="Internal").ap()

    def sb(name, shape, dt=F32):
        return es.enter_context(nc.sbuf_tensor(name, list(shape), dt))

    def ps(name, shape, dt=F32):
        return es.enter_context(nc.psum_tensor(name, list(shape), dt))

    prm_t = sb("prm_t", (128, lay.n))
    drv_t = sb("drv_t", (128, nlayers * 112))
    x_t = sb("x_t", (128, NC_, TT))
    hb = sb("hb", (128, NC_, TT + 1), BF16)
    ident = sb("ident", (128, 128), BF16)
    identF = sb("identF", (128, 128))
    mask2 = sb("mask2", (128, 256), BF16)
    mask2a = sb("mask2a", (128, 256), BF16)
    maskSL = sb("maskSL", (128, 128), BF16)
    maskUR = sb("maskUR", (128, 128), BF16)
    bo1 = sb("bo1", (128, 128), BF16)
    bo64 = sb("bo64", (128, 128), BF16)
    oneC = sb("oneC", (128, 128), BF16)
    rmask = sb("rmask", (128, TT))
    wsl = sb("wsl", (128, NSLOT, SLOT), BF16)
    nrw = (nlayers + 1) // 2
    ncv = nlayers // 2
    S_t = sb("S_t", (128, nrw * NC_ * 64))
    Sb_t = sb("Sb_t", (128, nrw * NC_ * 64), BF16)
    shc = sb("shc", (128, nrw, NC_), BF16)
    cvc = sb("cvc", (128, max(ncv, 1), NC_, 30))
    ffc = sb("ffc", (128, nlayers, NF, 2))

    def pc(k, i=0, n=1):
        o, _ = lay.off[k]
        return prm_t[:, o + i:o + i + n]

    def dc(l, i, n=1):
        return drv_t[:, l * 112 + i:l * 112 + i + n]
    omu = sb("omu", (128, nrw, 96))
    ca_t = sb("ca_t", (128, NC_))

    pd_t = [ps(f"pd{i}", (128, 512)) for i in range(3)]
    pw_t = [ps(f"pw{i}", (128, 512)) for i in range(4)]
    pt_t = ps("pt", (128, 8, 128), BF16)
    pd = Pool("pd", [t for t in pd_t])
    pw = Pool("pw", [t for t in pw_t])
    ptp = Pool("pt", [pt_t])

    def mm(o, lhsT, rhs, start, stop, r, w):
        P.add("pe", lambda e: e.matmul(o, lhsT=lhsT, rhs=rhs, start=start, stop=stop), r=r, w=w)

    def tr(o, in_, r, w):
        P.add("pe", lambda e: e.transpose(o, in_, ident[:, :]), r=list(r) + ["const"], w=w)

    def act(o, in_, func, r, w, bias=None, scale=None):
        kw = {}
        if bias is not None:
            kw["bias"] = bias
        if scale is not None:
            kw["scale"] = scale
        P.add("act", lambda e: e.activation(out=o, in_=in_, func=func, **kw), r=r, w=w)

    def tsc(eng, o, in0, s1, s2, op0, op1, r, w):
        if op1 is None:
            P.add(eng, lambda e: e.tensor_scalar(out=o, in0=in0, scalar1=s1, scalar2=None, op0=op0), r=r, w=w)
        else:
            P.add(eng, lambda e: e.tensor_scalar(out=o, in0=in0, scalar1=s1, scalar2=s2, op0=op0, op1=op1), r=r, w=w)

    def rsqrt(o, in_, eps, r, w, premax=None):
        if premax is not None:
            tsc("dve", o, in_, premax, None, ALU.max, None, r=r, w=w)
            act(o, o, AF.Ln, r=w, w=w)
        else:
            act(o, in_, AF.Ln, r=r, w=w, bias=eps)
        act(o, o, AF.Exp, r=w, w=w, scale=-0.5)

    def tt(eng, o, in0, in1, op, r, w):
        P.add(eng, lambda e: e.tensor_tensor(out=o, in0=in0, in1=in1, op=op), r=r, w=w)

    def stt(o, in0, scalar, in1, op0, op1, r, w):
        P.add("dve", lambda e: e.scalar_tensor_tensor(out=o, in0=in0, scalar=scalar, in1=in1, op0=op0, op1=op1), r=r, w=w)

    def cp(eng, o, in_, r, w):
        if eng == "act":
            act(o, in_, AF.Copy, r, w)
        else:
            P.add(eng, lambda e: e.tensor_copy(out=o, in_=in_), r=r, w=w)

    def dma(o, in_, r, w, semkey, at=None, eng="sp"):
        return P.add(eng, lambda e: e.dma_start(out=o, in_=in_), r=r, w=w, dma=True, semkey=semkey, at=at)

    with ExitStack() as pes:
        def psb(name, shape, dt=F32):
            return pes.enter_context(nc.sbuf_tensor(name, list(shape), dt))
        cst_t = psb("cst_t", (128, 10 * 128 + TT))
        dma(cst_t[:, :], cst[:, :], r=[], w=["cst_t"], semkey="cst_t")
        dma(prm_t[:, :], prm[:, :], r=[], w=["prm"], semkey="prm")
        for i, tgt in enumerate((ident, None, None, None, bo1, bo64, oneC)):
            if tgt is not None:
                cp("dve", tgt[:, :], cst_t[:, i * 128:(i + 1) * 128], r=["cst_t"], w=["const"])
        cp("dve", mask2[:, :], cst_t[:, 128:384], r=["cst_t"], w=["const"])
        co_ = 896 + TT
        cp("dve", mask2a[:, 0:128], cst_t[:, co_:co_ + 128], r=["cst_t"], w=["const"])
        cp("dve", mask2a[:, 128:256], cst_t[:, 256:384], r=["cst_t"], w=["const"])
        cp("dve", maskSL[:, :], cst_t[:, co_ + 128:co_ + 256], r=["cst_t"], w=["const"])
        cp("dve", maskUR[:, :], cst_t[:, co_ + 256:co_ + 384], r=["cst_t"], w=["const"])
        cp("dve", identF[:, :], cst_t[:, 0:128], r=["cst_t"], w=["const"])
        cp("act", rmask[:, :], cst_t[:, 896:896 + TT], r=["cst_t"], w=["const"])
        for t_, k_ in ((S_t, "S"), (Sb_t, "Sb"), (shc, "shc"), (cvc, "cvc"), (ffc, "ffc")):
            P.add("pool", (lambda tt_: (lambda e: e.memset(tt_, 0.0)))(t_[:]), r=[], w=[k_])
        act(ca_t[:, :], pc("c", 0, 16), AF.Silu, r=["prm"], w=["ca"])
        st32 = [psb(f"st32_{i}", (128, SLOT)) for i in range(3)]
        st16 = [psb(f"st16_{i}", (128, SLOT), BF16) for i in range(3)]
        cnt = [0]
        cast_engs = ("act", "dve", "pool")

        def cast_block(src_ap, kp, n_kc, Mb, dst_ap):
            i = cnt[0] % 3; cnt[0] += 1
            ne = n_kc * Mb
            s32 = st32[i][0:kp, 0:ne]; s16 = st16[i][0:kp, 0:ne]
            dma(s32.rearrange("p (k m) -> p k m", m=Mb) if n_kc > 1 else s32, src_ap, r=[], w=[f"st32_{i}"], semkey=f"st32_{i}")
            cp(cast_engs[(cnt[0]) % 3], s16, s32, r=[f"st32_{i}"], w=[f"st16_{i}"])
            dma(dst_ap, s16, r=[f"st16_{i}"], w=[], semkey=f"st16_{i}")

        def cast_mat(name, src, r0, n_kc, Mb, c0, n_oc):
            dst, nk, mb = SCR[name]
            for oc in range(n_oc):
                sap = src[r0:r0 + n_kc * 128, c0 + oc * Mb:c0 + (oc + 1) * Mb].rearrange("(k p) m -> p k m", p=128)
                cast_block(sap, 128, n_kc, Mb, dst[oc, :, :])

        for l in range(nlayers):
            cast_mat(f"up{l}", W[f"up{l}"], 0, 16, 128, 0, 2 * NF)
            cast_mat(f"dnA{l}", W[f"dn{l}"], 0, 15, 128, 0, 16)
            cast_mat(f"dnB{l}", W[f"dn{l}"], 15 * 128, 14, 128, 0, 16)
            cast_mat(f"dnC{l}", W[f"dn{l}"], 29 * 128, 14, 128, 0, 16)
            if l % 2 == 0:
                for n in ("r", "k", "v", "wo"):
                    cast_mat(f"{n}{l}", W[f"{n}{l}"], 0, 16, 128, 0, 16)
                cast_mat(f"g1a{l}", W[f"g1{l}"], 0, 16, 128, 0, 1)
                cast_mat(f"g1b{l}", W[f"g1{l}"], 0, 16, 128, 128, 1)
                cast_mat(f"w1{l}", W[f"w1{l}"], 0, 16, 96, 0, 1)
                cast_mat(f"a1{l}", W[f"a1{l}"], 0, 16, 96, 0, 1)
                if l >= 2:
                    i = cnt[0] % 3; cnt[0] += 1
                    s16f = st16[i][:, 0:C]
                    P.add("pool", (lambda a_: (lambda e: e.memset(a_, 0.0)))(s16f), r=[], w=[f"st16_{i}"])
                    s32 = st32[i][:, 0:16 * 64]
                    dma(s32.rearrange("p (k m) -> p k m", m=64), W[f"v1{l}"].rearrange("(k p) m -> p k m", p=128),
                        r=[], w=[f"st32_{i}"], semkey=f"st32_{i}")
                    cp(cast_engs[cnt[0] % 3], s16f.rearrange("p (k m) -> p k m", m=128)[:, :, 0:64],
                       s32.rearrange("p (k m) -> p k m", m=64), r=[f"st32_{i}"], w=[f"st16_{i}"])
                    dma(SCR[f"v1{l}"][0][0, :, :], s16f, r=[f"st16_{i}"], w=[], semkey=f"st16_{i}")
                lo2 = SCR[f"lo2{l}"][0]
                srcs = [(W[f"w2{l}"], 0, 96), (W[f"a2{l}"], 0, 96)]
                srcs.append((W[f"v2{l}"], 0, 64) if l >= 2 else None)
                srcs += [(W[f"g2{l}"], 0, 128), (W[f"g2{l}"], 128, 128)]
                for s_i, sd in enumerate(srcs):
                    i = cnt[0] % 3; cnt[0] += 1
                    s16f = st16[i][:, 0:C]
                    P.add("pool", (lambda a_: (lambda e: e.memset(a_, 0.0)))(s16f), r=[], w=[f"st16_{i}"])
                    if sd is not None:
                        src, r0, kp = sd
                        s32 = st32[i][0:kp, 0:C]; s16 = st16[i][0:kp, 0:C]
                        dma(s32, src[r0:r0 + kp, :], r=[], w=[f"st32_{i}"], semkey=f"st32_{i}")
                        cp(cast_engs[cnt[0] % 3], s16, s32, r=[f"st32_{i}"], w=[f"st16_{i}"])
                    dma(lo2[:, :, s_i * 128:(s_i + 1) * 128].rearrange("f p j -> p f j"),
                        s16f.rearrange("p (f j) -> p f j", j=128), r=[f"st16_{i}"], w=[], semkey=f"st16_{i}")
            else:
                cast_mat(f"pw1{l}", W[f"pw1{l}"], 0, 16, 128, 0, 32)
                cast_mat(f"pw2{l}", W[f"pw2{l}"], 0, 16, 128, 0, 16)

        adw = [psb(f"adw{i}", (128, NC_, 128)) for i in range(2)]
        for l in range(nlayers):
            pm_, pmk = pd.get()
            for oc in range(96):
                a_, ak = adw[oc % 2], f"adw{oc % 2}"
                dma(a_[:, :, :], W[f"ada{l}"][:, oc * 128:(oc + 1) * 128].rearrange("(k p) m -> p k m", p=128),
                    r=[], w=[ak], semkey=ak)
                for kc in range(NC_):
                    mm(pm_[:, oc:oc + 1], a_[:, kc, :], ca_t[:, kc:kc + 1], kc == 0, kc == NC_ - 1, r=[ak, "ca"], w=[pmk])
            mod = psb(f"mod{l}", (128, 96))
            tt("dve", mod[:, :], pm_[:, 0:96], pc(f"adab{l}", 0, 96), ALU.add, r=[pmk, "prm"], w=[f"mod{l}"])
            for hlf, gname in ((0, f"gmix{l}"), (1, f"gffn{l}")):
                b0 = hlf * 48
                stt(dc(l, b0, 16), mod[:, (3 * hlf + 1) * 16:(3 * hlf + 2) * 16], 1.0, pc(gname, 0, 16), ALU.add, ALU.mult,
                    r=[f"mod{l}", "prm"], w=["drv"])
                cp("dve", dc(l, b0 + 16, 16), mod[:, (3 * hlf) * 16:(3 * hlf + 1) * 16], r=[f"mod{l}"], w=["drv"])
                cp("dve", dc(l, b0 + 32, 16), mod[:, (3 * hlf + 2) * 16:(3 * hlf + 3) * 16], r=[f"mod{l}"], w=["drv"])
            if l % 2 == 0:
                tsc("dve", dc(l, 96, 16), pc(f"ka{l}", 0, 16), -1.0, 1.0, ALU.mult, ALU.add, r=["prm"], w=["drv"])
                tsc("dve", omu[:, l // 2, :], pc(f"mu{l}", 0, 96), -1.0, 1.0, ALU.mult, ALU.add, r=["prm"], w=["drv"])
            else:
                tt("dve", dc(l, 96, 16), pc(f"b2{l}", 0, 16), dc(l, 32, 16), ALU.mult, r=["prm", "drv"], w=["drv"])
    P.barrier()
    prologue_end = P.pos()

    slot_i = [0]
    slot_last = [prologue_end] * NSLOT

    def wget(name, oc):
        dst, n_kc, Mb = SCR[name]
        j = slot_i[0] % NSLOT; slot_i[0] += 1
        ne = n_kc * Mb
        at = max(slot_last[j], prologue_end)
        dma(wsl[:, j, 0:ne], dst[oc, :, :], r=[], w=[f"ws{j}"], semkey=f"ws{j}", at=at)
        return j, f"ws{j}"

    def wdone(j):
        slot_last[j] = P.pos()

    def wblk(j, kc, Mb=128, kp=128):
        return wsl[0:kp, j, kc * Mb:(kc + 1) * Mb]

    def mkpool(name, n, shape, dt=F32):
        return Pool(name, [sb(f"{name}{i}", shape, dt) for i in range(n)])

    sqp = mkpool("sq", 2, (128, TT), BF16)
    f32p = mkpool("f", 4, (128, TT))
    rwp = mkpool("rw", 8, (128, TT))
    rstd_p = mkpool("rstd", 2, (128, TT))
    arena = sb("arena", (128, 12288))
    arena_b = arena[:, :].bitcast(BF16)
    xm_v = arena_b.rearrange("p (a k t) -> p a k t", a=3, k=NC_)
    xm = [xm_v[:, i] for i in range(3)]
    hid = arena_b[:, 0:NF * TT].rearrange("p (f t) -> p f t", t=TT)
    cb_t = arena[:, 0:NC_ * TT].rearrange("p (k t) -> p k t", t=TT)
    gbp = mkpool("gb", 1, (128, TT + 2))
    ubp = mkpool("ub", 1, (128, TT + 30))
    l1p = mkpool("l1", 5, (128, TT), BF16)
    b16p = mkpool("b", 6, (128, TT), BF16)
    ARp = mkpool("AR", 1, (128, NCH, 256), BF16)
    TRp = mkpool("TR", 3, (128, NCH, 128), BF16)
    e_p = mkpool("E", 4, (128, 256), BF16)
    am_p = mkpool("AM", 2, (128, 256), BF16)
    m_p = mkpool("M", 3, (128, 128), BF16)
    ct_p = mkpool("Ct", 2, (128, 128), BF16)
    tt_p = mkpool("Tt", 3, (128, 128), BF16)
    xu_p = mkpool("XU", 6, (128, 64), BF16)
    yt_p = mkpool("yt", 1, (128, TT))
    xs_p = mkpool("xs", 2, (128, 64))
    ytok_p = mkpool("ytok", 1, (128, NCH, 128))
    vf_p = mkpool("vf", 1, (128, TT))
    o_p = f32p

    def norm_mod(l, which, func=AF.Identity):
        b0 = 0 if which == "mix" else 48
        acc, ak = pd.get()
        for kc in range(NC_):
            s, sk = sqp.get()
            act(s[:, :], x_t[:, kc, :], AF.Square, r=[f"x{kc}"], w=[sk])
            mm(acc[:, :], oneC[:, :], s[:, :], kc == 0, kc == NC_ - 1, r=[sk, "const"], w=[ak])
        rs, rk = rstd_p.get()
        rsqrt(rs[:, :], acc[:, :], RMS_EPS, r=[ak], w=[rk])
        for kc in range(NC_):
            t, tk = f32p.get()
            stt(t[:, :], x_t[:, kc, :], dc(l, b0 + kc), rs[:, :], ALU.mult, ALU.mult, r=[f"x{kc}", "drv", rk], w=[tk])
            act(hb[:, kc, 1:TT + 1], t[:, :], func, bias=dc(l, b0 + 16 + kc), r=[tk, "drv"], w=[f"hb{kc}"])

    def proj16(name, oc, src_keys, rhs_of, M=128):
        j, wk = wget(name, oc)
        acc, ak = pd.get()
        Mb = SCR[name][2]
        for kc in range(NC_):
            mm(acc[0:M, :], wblk(j, kc, Mb), rhs_of(kc), kc == 0, kc == NC_ - 1, r=[wk, src_keys[kc]], w=[ak])
        wdone(j)
        return acc, ak

    hbk = [f"hb{kc}" for kc in range(NC_)]

    def ffn(l):
        norm_mod(l, "ffn")
        for fc in range(NF):
            pg, pgk = proj16(f"up{l}", fc, hbk, lambda kc: hb[:, kc, 1:TT + 1])
            pv, pvk = proj16(f"up{l}", NF + fc, hbk, lambda kc: hb[:, kc, 1:TT + 1])
            gb, gk = gbp.get()
            cp("pool", gb[:, 0:2], ffc[:, l, fc, :], r=["ffc"], w=[gk])
            act(gb[:, 2:TT + 2], pg[:, :], AF.Copy, r=[pgk], w=[gk])
            t1, t1k = f32p.get()
            act(t1[:, :], pg[:, :], AF.Identity, scale=pc(f"fdw{l}", 2 * NF + fc), bias=pc(f"fdb{l}", fc), r=[pgk, "prm"], w=[t1k])
            cp("pool", ffc[:, l, fc, :], gb[:, TT:TT + 2], r=[gk], w=["ffc"])
            t2, t2k = f32p.get()
            stt(t2[:, :], gb[:, 1:TT + 1], pc(f"fdw{l}", NF + fc), t1[:, :], ALU.mult, ALU.add, r=[gk, t1k, "prm"], w=[t2k])
            t3, t3k = f32p.get()
            stt(t3[:, :], gb[:, 0:TT], pc(f"fdw{l}", fc), t2[:, :], ALU.mult, ALU.add, r=[gk, t2k, "prm"], w=[t3k])
            t4, t4k = f32p.get()
            act(t4[:, :], t3[:, :], AF.Silu, r=[t3k], w=[t4k])
            tt("dve", hid[:, fc, :], pv[:, :], t4[:, :], ALU.mult, r=[t4k, pvk], w=[f"hid{fc}"])
        for oc in range(NC_):
            acc, ak = pd.get()
            for nm_, k0, nk in ((f"dnA{l}", 0, 15), (f"dnB{l}", 15, 14), (f"dnC{l}", 29, 14)):
                j, wk = wget(nm_, oc)
                for kc in range(nk):
                    mm(acc[:, :], wblk(j, kc), hid[:, k0 + kc, :], k0 + kc == 0, k0 + kc == NF - 1, r=[wk, f"hid{k0 + kc}"], w=[ak])
                wdone(j)
            stt(x_t[:, oc, :], acc[:, :], dc(l, 80 + oc), x_t[:, oc, :], ALU.mult, ALU.add, r=[ak, "drv", f"x{oc}"], w=[f"x{oc}"])

    def conv(l):
        j_ = l // 2
        norm_mod(l, "mix")
        for cc in range(NC_):
            pa, pak = proj16(f"pw1{l}", cc, hbk, lambda kc: hb[:, kc, 1:TT + 1])
            pg, pgk = proj16(f"pw1{l}", NC_ + cc, hbk, lambda kc: hb[:, kc, 1:TT + 1])
            sg, sgk = f32p.get()
            act(sg[:, :], pg[:, :], AF.Sigmoid, bias=pc(f"b1{l}", NC_ + cc), r=[pgk, "prm"], w=[sgk])
            ub, uk = ubp.get()
            cp("pool", ub[:, 0:30], cvc[:, j_, cc, :], r=["cvc"], w=[uk])
            stt(ub[:, 30:TT + 30], pa[:, :], pc(f"b1{l}", cc), sg[:, :], ALU.add, ALU.mult, r=[pak, sgk, "prm"], w=[uk])
            cp("pool", cvc[:, j_, cc, :], ub[:, TT:TT + 30], r=[uk], w=["cvc"])
            a0, a0k = f32p.get(); a1, a1k = f32p.get()
            tsc("dve", a0[:, :], ub[:, 0:TT], pc(f"cw{l}", 0 * 16 + cc), pc(f"cb{l}", cc), ALU.mult, ALU.add, r=[uk, "prm"], w=[a0k])
            tsc("dve", a1[:, :], ub[:, 1:TT + 1], pc(f"cw{l}", 1 * 16 + cc), None, ALU.mult, None, r=[uk, "prm"], w=[a1k])
            for k in range(2, 31):
                a, ak_ = (a0, a0k) if k % 2 == 0 else (a1, a1k)
                stt(a[:, :], ub[:, k:k + TT], pc(f"cw{l}", k * 16 + cc), a[:, :], ALU.mult, ALU.add, r=[uk, "prm", ak_], w=[ak_])
            tt("dve", cb_t[:, cc, :], a0[:, :], a1[:, :], ALU.add, r=[a0k, a1k], w=[f"cb{cc}"])
        pm_, pmk = pd.get()
        for cc in range(NC_):
            s, sk = sqp.get()
            cp("act", s[:, :], cb_t[:, cc, :], r=[f"cb{cc}"], w=[sk])
            mm(pm_[:, :], oneC[:, :], s[:, :], cc == 0, cc == NC_ - 1, r=[sk, "const"], w=[pmk])
        pv_, pvk = pd.get()
        for cc in range(NC_):
            stt(cb_t[:, cc, :], pm_[:, :], -1.0, cb_t[:, cc, :], ALU.mult, ALU.add, r=[f"cb{cc}", pmk], w=[f"cb{cc}"])
            s, sk = sqp.get()
            act(s[:, :], cb_t[:, cc, :], AF.Square, r=[f"cb{cc}"], w=[sk])
            mm(pv_[:, :], oneC[:, :], s[:, :], cc == 0, cc == NC_ - 1, r=[sk, "const"], w=[pvk])
        rs, rk = rstd_p.get()
        rsqrt(rs[:, :], pv_[:, :], LN_EPS, r=[pvk], w=[rk])
        for cc in range(NC_):
            t, tk = f32p.get()
            stt(t[:, :], cb_t[:, cc, :], pc(f"clg{l}", cc), rs[:, :], ALU.mult, ALU.mult, r=[f"cb{cc}", "prm", rk], w=[tk])
            act(hb[:, cc, 1:TT + 1], t[:, :], AF.Silu, bias=pc(f"clb{l}", cc), r=[tk, "prm"], w=[f"hb{cc}"])
        for oc in range(NC_):
            po, pok = proj16(f"pw2{l}", oc, hbk, lambda kc: hb[:, kc, 1:TT + 1])
            t, tk = f32p.get()
            act(t[:, :], po[:, :], AF.Identity, scale=dc(l, 32 + oc), bias=dc(l, 96 + oc), r=[pok, "drv"], w=[tk])
            tt("pool", x_t[:, oc, :], x_t[:, oc, :], t[:, :], ALU.add, r=[tk, f"x{oc}"], w=[f"x{oc}"])

    def rwkv(l, ti):
        j_ = l // 2
        vfirst = l >= 2
        norm_mod(l, "mix")
        for kc in range(NC_):
            cp("pool", hb[:, kc, 0:1], shc[:, j_, kc:kc + 1], r=["shc"], w=[f"hb{kc}"])
            cp("pool", shc[:, j_, kc:kc + 1], hb[:, kc, TT:TT + 1], r=[f"hb{kc}"], w=["shc"])

        def mix(n, dst, di):
            for kc in range(NC_):
                t, tk = f32p.get()
                act(t[:, :], hb[:, kc, 1:TT + 1], AF.Identity, scale=omu[:, j_, n * 16 + kc:n * 16 + kc + 1], r=[f"hb{kc}", "drv"], w=[tk])
                stt(dst[:, kc, :], hb[:, kc, 0:TT], pc(f"mu{l}", n * 16 + kc), t[:, :], ALU.mult, ALU.add,
                    r=[f"hb{kc}", "prm", tk], w=[f"xm{di}_{kc}"])
        if dbg == "r1":
            return
        xk0 = [f"xm0_{kc}" for kc in range(NC_)]
        xk1 = [f"xm1_{kc}" for kc in range(NC_)]
        xk2 = [f"xm2_{kc}" for kc in range(NC_)]
        mix(5, xm[0], 0)
        g1s = []
        for hf, nm in enumerate((f"g1a{l}", f"g1b{l}")):
            p_, pk_ = proj16(nm, 0, xk0, lambda kc: xm[0][:, kc, :])
            g, gk = l1p.get()
            act(g[:, :], p_[:, :], AF.Sigmoid, r=[pk_], w=[gk])
            g1s.append((g, gk))
        if dbg == "r2":
            return
        mix(3, xm[0], 0)
        p_, pk_ = proj16(f"w1{l}", 0, xk0, lambda kc: xm[0][:, kc, :], M=96)
        lw1, lw1k = l1p.get()
        act(lw1[0:96, :], p_[0:96, :], AF.Tanh, r=[pk_], w=[lw1k])
        mix(4, xm[0], 0)
        p_, pk_ = proj16(f"a1{l}", 0, xk0, lambda kc: xm[0][:, kc, :], M=96)
        la1, la1k = l1p.get()
        act(la1[0:96, :], p_[0:96, :], AF.Copy, r=[pk_], w=[la1k])
        mix(2, xm[2], 2)
        if vfirst:
            p_, pk_ = proj16(f"v1{l}", 0, xk2, lambda kc: xm[2][:, kc, :])
            lv1, lv1k = l1p.get()
            act(lv1[:, :], p_[:, :], AF.Copy, r=[pk_], w=[lv1k])
        mix(1, xm[1], 1)
        mix(0, xm[0], 0)

        if dbg == "r3":
            return
        for fc in range(NC_):
            if dbg in ("r4", "r5", "r6", "u1", "u2", "u3", "u4", "u5", "u4a", "u4b", "u4c") and fc > 0:
                return
            Sk = f"S{j_}_{fc}"; Sbk = f"Sb{j_}_{fc}"
            pr, prk = proj16(f"r{l}", fc, xk0, lambda kc: xm[0][:, kc, :])
            r_, rk_ = rwp.get()
            act(r_[:, :], pr[:, :], AF.Copy, r=[prk], w=[rk_])
            pk, pkk = proj16(f"k{l}", fc, xk1, lambda kc: xm[1][:, kc, :])
            k_, kk_ = rwp.get()
            act(k_[:, :], pk[:, :], AF.Copy, r=[pkk], w=[kk_])
            pv, pvk = proj16(f"v{l}", fc, xk2, lambda kc: xm[2][:, kc, :])
            v_, vk_ = rwp.get()
            act(v_[:, :], pv[:, :], AF.Copy, r=[pvk], w=[vk_])
            j2, w2k = wget(f"lo2{l}", fc)
            pw_, pwk = pd.get()
            mm(pw_[:, :], wblk(j2, 0, 128, 96), lw1[0:96, :], True, True, r=[w2k, lw1k], w=[pwk])
            sw, swk = rwp.get()
            act(sw[:, :], pw_[:, :], AF.Sigmoid, bias=pc(f"w0{l}", fc), r=[pwk, "prm"], w=[swk])
            pa_, pak = pd.get()
            mm(pa_[:, :], wblk(j2, 1, 128, 96), la1[0:96, :], True, True, r=[w2k, la1k], w=[pak])
            asg, ask = rwp.get()
            act(asg[:, :], pa_[:, :], AF.Sigmoid, bias=pc(f"a0{l}", fc), r=[pak, "prm"], w=[ask])
            pg_, pgk = pd.get()
            mm(pg_[:, :], wblk(j2, 3), g1s[0][0][:, :], True, False, r=[w2k, g1s[0][1]], w=[pgk])
            mm(pg_[:, :], wblk(j2, 4), g1s[1][0][:, :], False, True, r=[w2k, g1s[1][1]], w=[pgk])
            gt, gtk = b16p.get()
            act(gt[:, :], pg_[:, :], AF.Copy, r=[pgk], w=[gtk])
            if vfirst:
                pv2, pv2k = pd.get()
                mm(pv2[:, :], wblk(j2, 2), lv1[:, :], True, True, r=[w2k, lv1k], w=[pv2k])
                sv, svk = f32p.get()
                act(sv[:, :], pv2[:, :], AF.Sigmoid, bias=pc(f"v0{l}", fc), r=[pv2k, "prm"], w=[svk])
                vf, vfk = vf_p.get()
                dma(vf[:, :], vfs[fc, :, :], r=[f"vfs{fc}"], w=[vfk], semkey=vfk)
                tt("pool", vf[:, :], vf[:, :], v_[:, :], ALU.subtract, r=[vfk, vk_], w=[vfk])
                tt("pool", vf[:, :], vf[:, :], sv[:, :], ALU.mult, r=[vfk, svk], w=[vfk])
                tt("pool", v_[:, :], v_[:, :], vf[:, :], ALU.add, r=[vfk, vk_], w=[vk_])
            elif nlayers > 2:
                dma(vfs[fc, :, :], v_[:, :], r=[vk_], w=[f"vfs{fc}"], semkey=vk_)
            wdone(j2)
            if dbg == "r4":
                return
            kkr, kkrk = rwp.get()
            tsc("dve", kkr[:, :], k_[:, :], pc(f"kk{l}", fc), None, ALU.mult, None, r=[kk_, "prm"], w=[kkrk])
            s, sk = sqp.get()
            act(s[:, :], kkr[:, :], AF.Square, r=[kkrk], w=[sk])
            pn, pnk = pd.get()
            mm(pn[:, :], bo1[:, :], s[:, :], True, True, r=[sk, "const"], w=[pnk])
            rn, rnk = rstd_p.get()
            rsqrt(rn[:, :], pn[:, :], None, r=[pnk], w=[rnk], premax=1e-24)
            tt("dve", kkr[:, :], kkr[:, :], rn[:, :], ALU.mult, r=[kkrk, rnk], w=[kkrk])
            f_, fk = f32p.get()
            act(f_[:, :], asg[:, :], AF.Identity, scale=pc(f"ka{l}", fc), bias=dc(l, 96 + fc), r=[ask, "prm", "drv"], w=[fk])
            tt("pool", k_[:, :], k_[:, :], f_[:, :], ALU.mult, r=[kk_, fk], w=[kk_])
            tt("pool", asg[:, :], asg[:, :], kkr[:, :], ALU.mult, r=[ask, kkrk], w=[ask])
            cw, cwk = rwp.get()
            P.add("dve", lambda e, cw=cw, sw=sw: e.tensor_tensor_scan(out=cw[:, :], data0=rmask[:, :], data1=sw[:, :],
                                                                     initial=0.0, op0=ALU.mult, op1=ALU.add),
                  r=[swk, "const"], w=[cwk])
            tt("pool", sw[:, :], cw[:, :], sw[:, :], ALU.subtract, r=[cwk, swk], w=[swk])
            Wt, Wk = rwp.get()
            act(Wt[:, :], cw[:, :], AF.Exp, scale=-C0, r=[cwk], w=[Wk])
            act(sw[:, :], sw[:, :], AF.Exp, scale=-C0, r=[swk], w=[swk])
            act(cw[:, :], cw[:, :], AF.Exp, scale=C0, r=[cwk], w=[cwk])
            AR, ARk = ARp.get()
            tt("dve", AR[:, :, 128:256], r_[:, :].rearrange("p (c t) -> p c t", t=L), Wt[:, :].rearrange("p (c t) -> p c t", t=L),
               ALU.mult, r=[rk_, Wk], w=[ARk])
            stt(AR[:, :, 0:128], kkr[:, :].rearrange("p (c t) -> p c t", t=L), -1.0, sw[:, :].rearrange("p (c t) -> p c t", t=L),
                ALU.mult, ALU.mult, r=[kkrk, swk], w=[ARk])
            BT, BTk = b16p.get()
            tt("dve", BT[:, :], asg[:, :], cw[:, :], ALU.mult, r=[ask, cwk], w=[BTk])
            KT, KTk = b16p.get()
            tt("dve", KT[:, :], k_[:, :], cw[:, :], ALU.mult, r=[kk_, cwk], w=[KTk])
            bend, bendk = b16p.get(); kend, kendk = b16p.get()
            for c in range(NCH):
                wl = Wt[:, c * L + L - 1:c * L + L]
                tsc("pool", bend[:, c * L:(c + 1) * L], BT[:, c * L:(c + 1) * L], wl, None, ALU.mult, None, r=[BTk, Wk], w=[bendk])
                tsc("pool", kend[:, c * L:(c + 1) * L], KT[:, c * L:(c + 1) * L], wl, None, ALU.mult, None, r=[KTk, Wk], w=[kendk])
            vb, vbk = b16p.get()
            cp("act", vb[:, :], v_[:, :], r=[vk_], w=[vbk])
            rkr, rkrk = sqp.get()
            stt(rkr[:, :], r_[:, :], pc(f"rk{l}", fc), k_[:, :], ALU.mult, ALU.mult, r=[rk_, kk_, "prm"], w=[rkrk])
            if dbg == "r5":
                return
            VT, VTk = TRp.get(); BET, BETk = TRp.get(); KET, KETk = TRp.get()
            for ei, (src, srck, dst, dstk) in enumerate(((vb, vbk, VT, VTk), (bend, bendk, BET, BETk), (kend, kendk, KET, KETk))):
                p_, pk_ = ptp.get()
                for c in range(NCH):
                    tr(p_[:, c, :], src[:, c * L:(c + 1) * L], r=[srck], w=[pk_])
                cp("act" if ei % 2 == 0 else "dve", dst[:, :, :], p_[:, 0:NCH, :], r=[pk_], w=[dstk])
            if dbg == "r6":
                return
            pb, pbk = pd.get()
            mm(pb[:, :], bo1[:, :], rkr[:, :], True, True, r=[rkrk, "const"], w=[pbk])
            tt("dve", v_[:, :], pb[:, :], v_[:, :], ALU.mult, r=[vk_, pbk], w=[vk_])
            yt, ytk = yt_p.get()
            ytok, ytokk = ytok_p.get()
            for c in range(NCH):
                def unit(hh, c=c):
                    rows = slice(hh * 64, hh * 64 + 64)
                    rows = slice(hh * 64, hh * 64 + 64)
                    ARc = AR[:, c, :]
                    BTc = BT[:, c * L:(c + 1) * L]; KTc = KT[:, c * L:(c + 1) * L]
                    bA, bAk = pw.get(); bB, bBk = pw.get()
                    P1 = bA[:, 0:256]; P2 = bA[:, 256:512]; P3 = bB[:, 0:128]
                    mm(P1, BTc[rows, :], ARc[rows, :], True, True, r=[BTk, ARk], w=[bAk])
                    mm(P2, KTc[rows, :], ARc[rows, :], True, True, r=[KTk, ARk], w=[bAk])
                    mm(P3, ARc[rows, 0:128], BTc[rows, :], True, True, r=[BTk, ARk], w=[bBk])
                    E1, E1k = e_p.get(); E2, E2k = e_p.get()
                    tt("dve", E1[:, :], P1, mask2a[:, :], ALU.mult, r=[bAk, "const"], w=[E1k])
                    Ct, Ctk = ct_p.get()
                    tt("dve", Ct[:, :], P1[:, 0:128], maskUR[:, :], ALU.mult, r=[bAk, "const"], w=[Ctk])
                    tt("dve", E2[:, :], P2, mask2[:, :], ALU.mult, r=[bAk, "const"], w=[E2k])
                    M0, M0k = m_p.get()
                    tt("dve", M0[:, :], P3, maskSL[:, :], ALU.mult, r=[bBk, "const"], w=[M0k])
                    Tt, Ttk = tt_p.get()
                    tt("pool", Tt[:, :], E1[:, 0:128], ident[:, :], ALU.add, r=[E1k, "const"], w=[Ttk])
                    yield
                    Aprev, Apk = E1[:, 0:128], E1k
                    Mprev, Mpk = M0[:, :], M0k
                    Q = bB[:, 0:256]; Q2 = bB[:, 256:384]
                    for lev in range(1, 6):
                        if lev < 5:
                            mm(Q[:, 0:128], Mprev, Aprev, True, True, r=[Apk, Mpk], w=[bBk])
                            mm(Q[:, 128:256], Aprev, Mprev, True, True, r=[Apk, Mpk], w=[bBk])
                            AM, AMk = am_p.get()
                            cp("act", AM[:, :], Q, r=[bBk], w=[AMk])
                            Aprev, Apk = AM[:, 0:128], AMk
                            Mprev, Mpk = AM[:, 128:256], AMk
                            yield
                        else:
                            mm(Q[:, 0:128], Aprev, Mprev, True, True, r=[Apk, Mpk], w=[bBk])
                            M6, M6k = m_p.get()
                            cp("act", M6[:, :], Q[:, 0:128], r=[bBk], w=[M6k])
                            Mprev, Mpk = M6[:, :], M6k
                            yield
                        mm(Q2, ident[:, :], Tt[:, :], True, False, r=["const", Ttk], w=[bBk])
                        mm(Q2, Mprev, Tt[:, :], False, True, r=[Mpk, Ttk], w=[bBk])
                        Tn, Tnk = tt_p.get()
                        cp("dve" if lev % 2 else "act", Tn[:, :], Q2, r=[bBk], w=[Tnk])
                        Tt, Ttk = Tn, Tnk
                        yield
                    PX = bB[:, 384:448]; PU = bB[:, 448:512]
                    mm(PX, ARc[rows, 0:128], Sb_t[rows, (j_ * NC_ + fc) * 64:(j_ * NC_ + fc) * 64 + 64], True, False, r=[ARk, Sbk], w=[bBk])
                    mm(PX, E2[:, 0:128], VT[:, c, hh * 64:hh * 64 + 64], False, True, r=[E2k, VTk], w=[bBk])
                    X0, X0k = xu_p.get()
                    cp("act", X0[:, :], PX, r=[bBk], w=[X0k])
                    yield
                    mm(PU, Tt[:, :], X0[:, :], True, True, r=[Ttk, X0k], w=[bBk])
                    W1, W1k = xu_p.get()
                    cp("dve", W1[:, :], PU, r=[bBk], w=[W1k])
                    yield
                    mm(PX, ident[:, :], X0[:, :], True, False, r=["const", X0k], w=[bBk])
                    mm(PX, Ct[:, :], W1[:, :], False, True, r=[Ctk, W1k], w=[bBk])
                    X2, X2k = xu_p.get()
                    cp("act", X2[:, :], PX, r=[bBk], w=[X2k])
                    yield
                    mm(PU, Tt[:, :], X2[:, :], True, True, r=[Ttk, X2k], w=[bBk])
                    U, Uk = xu_p.get()
                    cp("dve", U[:, :], PU, r=[bBk], w=[Uk])
                    yield
                    hc = slice(hh * 64, hh * 64 + 64)
                    PY = bA[:, 0:64]; PS = bA[:, 64:128]
                    mm(PY, ARc[rows, 128:256], Sb_t[rows, (j_ * NC_ + fc) * 64:(j_ * NC_ + fc) * 64 + 64], True, False, r=[Sbk, ARk], w=[bAk])
                    mm(PY, E1[:, 128:256], U[:, :], False, False, r=[Uk, E1k], w=[bAk])
                    mm(PY, E2[:, 128:256], VT[:, c, hc], False, True, r=[VTk, E2k], w=[bAk])
                    mm(PS, BET[:, c, :], U[:, :], True, False, r=[BETk, Uk], w=[bAk])
                    mm(PS, KET[:, c, :], VT[:, c, hc], False, True, r=[KETk, VTk], w=[bAk])
                    cp("act", ytok[:, c, hc], PY, r=[bAk], w=[ytokk])
                    rows_ = rows
                    so = (j_ * NC_ + fc) * 64
                    stmp, stk = xs_p.get()
                    act(stmp[rows_, :], S_t[rows_, so:so + 64], AF.Identity, scale=Wt[rows_, c * L + L - 1:c * L + L], r=[Sk, Wk], w=[stk])
                    tt("dve", S_t[rows_, so:so + 64], PS[rows_, :], stmp[rows_, :], ALU.add, r=[stk, bAk], w=[Sk])
                    cp("act", Sb_t[rows_, (j_ * NC_ + fc) * 64:(j_ * NC_ + fc) * 64 + 64], S_t[rows_, (j_ * NC_ + fc) * 64:(j_ * NC_ + fc) * 64 + 64], r=[Sk], w=[Sbk])

                gens = [unit(0), unit(1)]
                while gens:
                    for g_ in list(gens):
                        try:
                            next(g_)
                        except StopIteration:
                            gens.remove(g_)
            if dbg in ("u1", "u2", "u3", "u4", "u5", "u4a", "u4b", "u4c"):
                return
            py_, pyk = pd.get()
            for c in range(NCH):
                P.add("pe", (lambda o_, i_: (lambda e: e.transpose(o_, i_, identF[:, :])))(py_[:, c * L:(c + 1) * L], ytok[:, c, :]),
                      r=[ytokk, "const"], w=[pyk])
            cp("act", yt[:, :], py_[:, :], r=[pyk], w=[ytk])
            yb, ybk = sqp.get()
            cp("act", yb[:, :], yt[:, :], r=[ytk], w=[ybk])
            pm_, pmk = pd.get()
            mm(pm_[:, :], bo64[:, :], yb[:, :], True, True, r=[ybk, "const"], w=[pmk])
            stt(yt[:, :], pm_[:, :], -1.0, yt[:, :], ALU.mult, ALU.add, r=[ytk, pmk], w=[ytk])
            s2, s2k = sqp.get()
            act(s2[:, :], yt[:, :], AF.Square, r=[ytk], w=[s2k])
            pvv, pvvk = pd.get()
            mm(pvv[:, :], bo64[:, :], s2[:, :], True, True, r=[s2k, "const"], w=[pvvk])
            rs, rsk = rstd_p.get()
            rsqrt(rs[:, :], pvv[:, :], GN_EPS, r=[pvvk], w=[rsk])
            tt("dve", yt[:, :], yt[:, :], rs[:, :], ALU.mult, r=[ytk, rsk], w=[ytk])
            act(yt[:, :], yt[:, :], AF.Identity, scale=pc(f"lng{l}", fc), bias=pc(f"lnb{l}", fc), r=[ytk, "prm"], w=[ytk])
            tt("pool", yt[:, :], yt[:, :], v_[:, :], ALU.add, r=[ytk, vk_], w=[ytk])
            tt("pool", hb[:, fc, 1:TT + 1], yt[:, :], gt[:, :], ALU.mult, r=[ytk, gtk], w=[f"hb{fc}"])
        for oc in range(NC_):
            po, pok = proj16(f"wo{l}", oc, hbk, lambda kc: hb[:, kc, 1:TT + 1])
            stt(x_t[:, oc, :], po[:, :], dc(l, 32 + oc), x_t[:, oc, :], ALU.mult, ALU.add, r=[pok, "drv", f"x{oc}"], w=[f"x{oc}"])

    xv = xin.rearrange("(k p) t -> p k t", p=128)
    ov = out.rearrange("(k p) t -> p k t", p=128)
    for ti in range(NT):
        t0 = ti * TT
        for kc in range(NC_):
            dma(x_t[:, kc, :], xv[:, kc, t0:t0 + TT], r=[], w=[f"x{kc}"], semkey=f"x{kc}")
        for l in range(nlayers):
            if dbg == "pro":
                break
            if l % 2 == 0:
                rwkv(l, ti)
            else:
                conv(l)
            if dbg in ("mix", "r1", "r2", "r3", "r4", "r5", "r6", "u1", "u2", "u3", "u4", "u5", "u4a", "u4b", "u4c"):
                break
            ffn(l)
        acc, ak = pd.get()
        for kc in range(NC_):
            s, sk = sqp.get()
            act(s[:, :], x_t[:, kc, :], AF.Square, r=[f"x{kc}"], w=[sk])
            mm(acc[:, :], oneC[:, :], s[:, :], kc == 0, kc == NC_ - 1, r=[sk, "const"], w=[ak])
        rs, rk = rstd_p.get()
        rsqrt(rs[:, :], acc[:, :], RMS_EPS, r=[ak], w=[rk])
        for kc in range(NC_):
            o, ok = o_p.get()
            stt(o[:, :], x_t[:, kc, :], pc("fin", kc), rs[:, :], ALU.mult, ALU.mult, r=[f"x{kc}", "prm", rk], w=[ok])
            dma(ov[:, kc, t0:t0 + TT], o[:, :], r=[ok], w=[], semkey=ok)

    P.emit(nc, es)
    es.close()
    return nc, lay


def make_inmaps(inp, nlayers, T, lay, batches):
    maps = []
    cst = make_consts()
    for b in batches:
        m = {"x": np.ascontiguousarray(np.asarray(inp["x"][b], np.float32)[:T].T),
             "prm": pack_params(inp, b, nlayers, lay), "cst": cst}
        for l in range(nlayers):
            j = l // 2
            m[f"ada{l}"] = np.ascontiguousarray(inp["ada_w"][l]); m[f"up{l}"] = np.ascontiguousarray(inp["ffn_w_up"][l])
            m[f"dn{l}"] = np.ascontiguousarray(inp["ffn_w_down"][l])
            if l % 2 == 0:
                for i, n in enumerate("rkv"):
                    m[f"{n}{l}"] = np.ascontiguousarray(inp["rwkv_w_rkv"][j][i])
                m[f"wo{l}"] = np.ascontiguousarray(inp["rwkv_w_o"][j])
                for n in ("w1", "w2", "a1", "a2", "g1", "g2"):
                    m[f"{n}{l}"] = np.ascontiguousarray(inp["rwkv_" + n][j])
                if l >= 2:
                    m[f"v1{l}"] = np.ascontiguousarray(inp["rwkv_v1"][j - 1]); m[f"v2{l}"] = np.ascontiguousarray(inp["rwkv_v2"][j - 1])
            else:
                m[f"pw1{l}"] = np.ascontiguousarray(inp["conv_w_pw1"][j]); m[f"pw2{l}"] = np.ascontiguousarray(inp["conv_w_pw2"][j])
        maps.append(m)
    return maps


def run(inp, nlayers, T, batches, dbg=None):
    nc, lay = build(T, nlayers, dbg)
    maps = make_inmaps(inp, nlayers, T, lay, batches)
    res = run_bass_kernel_spmd(nc, maps, core_ids=list(range(len(batches))))
    return [np.ascontiguousarray(r["out"].T) for r in res.results]


def kernel(**inputs):
    inp = {k: np.asarray(v) for k, v in inputs.items()}
    B, T, _ = inp["x"].shape
    outs = run(inp, 4, T, [0, 1, 2, 3])
    return np.stack(outs[:4], axis=0).astype(np.float32)
```

```python
import numpy as np
from contextlib import ExitStack
import concourse.bass as bass
import concourse.mybir as mybir
from concourse.bass_utils import run_bass_kernel_spmd

F32 = mybir.dt.float32
BF16 = mybir.dt.bfloat16
AF = mybir.ActivationFunctionType
ALU = mybir.AluOpType

C = 2048
NC_ = 16
FF = 5504
NF = 43
TT = 512
L = 128
NCH = TT // L
C0 = 0.6065306597126334
RMS_EPS = 1e-6
LN_EPS = 1e-5
GN_EPS = 64e-5
SLOT = 2048
NSLOT = 3


class Op:
    __slots__ = ("eng", "fn", "r", "w", "dma", "semkey", "waits", "sig", "sigval", "idx", "barrier")

    def __init__(self, eng, fn, r, w, dma, semkey):
        self.eng = eng; self.fn = fn; self.r = tuple(r); self.w = tuple(w)
        self.dma = dma; self.semkey = semkey
        self.waits = []; self.sig = False; self.sigval = 0; self.barrier = False


class Prog:
    CE = ("pe", "act", "dve", "pool")
    ROT = 30000

    def __init__(self):
        self.ops = []
        self.ins = {}

    def add(self, eng, fn, r=(), w=(), dma=False, semkey=None, at=None):
        op = Op(eng, fn, r, w, dma, semkey)
        if at is None:
            self.ops.append(op)
        else:
            self.ins.setdefault(at, []).append(op)
        return op

    def pos(self):
        return len(self.ops)

    def barrier(self):
        for e in ("pe", "act", "dve", "pool", "sp"):
            op = Op(e, None, (), (), False, None)
            op.barrier = True
            self.ops.append(op)

    def flatten(self):
        flat = []
        n = len(self.ops)
        for i in range(n + 1):
            if i in self.ins:
                flat.extend(self.ins[i])
            if i < n:
                flat.append(self.ops[i])
        for i, o in enumerate(flat):
            o.idx = i
        return flat

    def analyze(self):
        flat = self.flatten()
        lastw = {}
        lastr = {}
        waited = {e: {} for e in ("pe", "act", "dve", "pool", "sp")}
        dmacnt = {}
        for op in flat:
            if op.dma:
                dmacnt[op.semkey] = dmacnt.get(op.semkey, 0) + 16
                op.sigval = dmacnt[op.semkey]
        lastop = {}
        dmalast = {}
        for i, op in enumerate(flat):
            if op.barrier:
                wd = waited[op.eng]
                for pe_, d in lastop.items():
                    if pe_ == op.eng:
                        continue
                    if wd.get(pe_, -1) < d:
                        wd[pe_] = d; op.waits.append((pe_, d)); flat[d].sig = True
                for k_, v_ in dmalast.items():
                    key = ("dma", k_)
                    if wd.get(key, -1) < v_:
                        wd[key] = v_; op.waits.append((key, v_))
                continue
            if op.dma:
                dmalast[op.semkey] = op.sigval
            else:
                lastop[op.eng] = i
            deps = set()
            raw = set()
            for k in op.r:
                if k in lastw:
                    raw.add(lastw[k])
            for k in op.w:
                if k in lastw:
                    deps.add(lastw[k])
                lr = lastr.get(k)
                if lr:
                    deps.update(lr.values())
            deps |= raw
            deps.discard(i)
            need = {}
            for d in deps:
                p = flat[d]
                if p.dma:
                    key = ("dma", p.semkey); val = p.sigval
                else:
                    if p.eng == op.eng and not op.dma:
                        if p.eng == "pe":
                            continue
                    key = p.eng; val = d
                if need.get(key, -1) < val:
                    need[key] = val
            wd = waited[op.eng]
            for key, val in need.items():
                if wd.get(key, -1) >= val:
                    continue
                wd[key] = val
                op.waits.append((key, val))
                if not isinstance(key, tuple):
                    flat[val].sig = True
            for k in op.w:
                lastw[k] = i
                lastr[k] = {}
            for k in op.r:
                lastr.setdefault(k, {})[op.eng + ("d" if op.dma else "")] = i
        cnt = {e: 0 for e in self.CE}
        for op in flat:
            if op.dma:
                op.sig = True
            elif op.sig:
                cnt[op.eng] += 1
                op.sigval = cnt[op.eng]
        self.flat = flat
        self.cnt = cnt
        self.dmakeys = sorted(dmacnt.keys())
        return flat

    def emit(self, nc, es):
        flat = self.analyze()
        sems = {}
        for e in self.CE:
            n = self.cnt[e] // self.ROT + 1
            sems[e] = [es.enter_context(nc.semaphore(f"s_{e}_{i}")) for i in range(n)]
        dsem = {k: es.enter_context(nc.semaphore(f"d_{j}")) for j, k in enumerate(self.dmakeys)}
        block = es.enter_context(nc.Block())
        ROT = self.ROT

        def semof(e, v):
            return sems[e][(v - 1) // ROT], (v - 1) % ROT + 1

        def run(engname, handle):
            for op in flat:
                if op.eng != engname:
                    continue
                for key, val in op.waits:
                    if isinstance(key, tuple):
                        handle.wait_ge(dsem[key[1]], val)
                    else:
                        s, v = semof(key, flat[val].sigval)
                        handle.wait_ge(s, v)
                if op.fn is None:
                    continue
                inst = op.fn(handle)
                if op.dma:
                    inst.then_inc(dsem[op.semkey], 16)
                elif op.sig:
                    s, v = semof(op.eng, op.sigval)
                    inst.then_inc(s, 1)
            if engname == "sp":
                last = {}
                for op in flat:
                    if op.dma:
                        last[op.semkey] = op.sigval
                for k, v in last.items():
                    handle.wait_ge(dsem[k], v)

        @block.sync
        def _(e):
            run("sp", e)

        @block.tensor
        def _(e):
            run("pe", e)

        @block.scalar
        def _(e):
            run("act", e)

        @block.vector
        def _(e):
            run("dve", e)

        @block.gpsimd
        def _(e):
            run("pool", e)


class Pool:
    def __init__(self, name, aps, keys=None):
        self.name = name; self.aps = aps; self.i = 0
        self.keys = keys or [f"{name}{j}" for j in range(len(aps))]

    def get(self):
        j = self.i % len(self.aps); self.i += 1
        return self.aps[j], self.keys[j]


def _cols(v):
    v = np.asarray(v, np.float32).reshape(-1)
    return np.ascontiguousarray(v.reshape(-1, 128).T)


class ParamLayout:
    def __init__(self, nlayers):
        self.off = {}
        self.n = 0
        for l in range(nlayers):
            self._a(f"adab{l}", 96); self._a(f"gmix{l}", 16); self._a(f"gffn{l}", 16)
            self._a(f"fdw{l}", 3 * NF); self._a(f"fdb{l}", NF)
            if l % 2 == 0:
                for nm, n in (("mu", 96), ("w0", 16), ("a0", 16), ("v0", 16), ("kk", 16), ("ka", 16),
                              ("rk", 16), ("lng", 16), ("lnb", 16)):
                    self._a(f"{nm}{l}", n)
            else:
                for nm, n in (("b1", 32), ("cw", 31 * 16), ("cb", 16), ("clg", 16), ("clb", 16), ("b2", 16)):
                    self._a(f"{nm}{l}", n)
        self._a("fin", 16); self._a("c", 16)

    def _a(self, k, n):
        self.off[k] = (self.n, n); self.n += n


def pack_params(inp, b, nlayers, lay):
    P = np.zeros((128, lay.n), np.float32)

    def put(k, arr):
        o, n = lay.off[k]
        a = _cols(arr)
        assert a.shape[1] == n, (k, a.shape, n)
        P[:, o:o + n] = a

    for l in range(nlayers):
        j = l // 2
        put(f"adab{l}", inp["ada_b"][l]); put(f"gmix{l}", inp["norm_mix_g"][l]); put(f"gffn{l}", inp["norm_ffn_g"][l])
        put(f"fdw{l}", inp["ffn_w_dw"][l]); put(f"fdb{l}", inp["ffn_b_dw"][l])
        if l % 2 == 0:
            put(f"mu{l}", inp["rwkv_mu"][j]); put(f"w0{l}", inp["rwkv_w0"][j]); put(f"a0{l}", inp["rwkv_a0"][j])
            if j > 0:
                put(f"v0{l}", inp["rwkv_v0"][j - 1])
            put(f"kk{l}", inp["rwkv_k_k"][j]); put(f"ka{l}", inp["rwkv_k_a"][j]); put(f"rk{l}", inp["rwkv_r_k"][j])
            put(f"lng{l}", inp["rwkv_lnx_g"][j]); put(f"lnb{l}", inp["rwkv_lnx_b"][j])
        else:
            put(f"b1{l}", inp["conv_b_pw1"][j]); put(f"cw{l}", inp["conv_w_dw"][j]); put(f"cb{l}", inp["conv_b_dw"][j])
            put(f"clg{l}", inp["conv_ln_g"][j]); put(f"clb{l}", inp["conv_ln_b"][j]); put(f"b2{l}", inp["conv_b_pw2"][j])
    put("fin", inp["final_norm_g"]); put("c", inp["c"][b])
    return P


def make_consts():
    K = np.zeros((128, 10 * 128 + TT), np.float32)
    i = np.arange(128)
    K[:, 0:128] = np.eye(128)
    K[:, 128:256] = (i[None, :] > i[:, None])
    K[:, 256:384] = (i[None, :] >= i[:, None])
    K[:, 384:512] = (i[None, :] < i[:, None])
    K[:, 512:640] = ((i[None, :] // 64) == (i[:, None] // 64))
    K[:, 640:768] = K[:, 512:640] / 64.0
    K[:, 768:896] = 1.0 / 2048.0
    m = np.ones(TT, np.float32); m[::L] = 0.0
    K[:, 896:896 + TT] = m[None, :]
    o = 896 + TT
    bd = ((i[None, :] // 64) == (i[:, None] // 64))
    K[:, o:o + 128] = K[:, 128:256] * bd
    K[:, o + 128:o + 256] = K[:, 384:512] * bd
    K[:, o + 256:o + 384] = (i[:, None] < 64) & (i[None, :] >= 64)
    return K


def build(T, nlayers, dbg=None):
    NT = T // TT
    lay = ParamLayout(nlayers)
    nc = bass.Bass("TRN2", target_bir_lowering=False)
    es = ExitStack()
    P = Prog()

    def din(name, shape):
        return nc.dram_tensor(name, list(shape), F32, kind="ExternalInput").ap()

    xin = din("x", (C, T))
    prm = din("prm", (128, lay.n))
    cst = din("cst", (128, 10 * 128 + TT))
    out = nc.dram_tensor("out", [C, T], F32, kind="ExternalOutput").ap()
    W = {}
    for l in range(nlayers):
        W[f"ada{l}"] = din(f"ada{l}", (C, 6 * C))
        W[f"up{l}"] = din(f"up{l}", (C, 2 * FF)); W[f"dn{l}"] = din(f"dn{l}", (FF, C))
        if l % 2 == 0:
            for n in "rkv":
                W[f"{n}{l}"] = din(f"{n}{l}", (C, C))
            W[f"wo{l}"] = din(f"wo{l}", (C, C))
            W[f"w1{l}"] = din(f"w1{l}", (C, 96)); W[f"w2{l}"] = din(f"w2{l}", (96, C))
            W[f"a1{l}"] = din(f"a1{l}", (C, 96)); W[f"a2{l}"] = din(f"a2{l}", (96, C))
            W[f"g1{l}"] = din(f"g1{l}", (C, 256)); W[f"g2{l}"] = din(f"g2{l}", (256, C))
            if l >= 2:
                W[f"v1{l}"] = din(f"v1{l}", (C, 64)); W[f"v2{l}"] = din(f"v2{l}", (64, C))
        else:
            W[f"pw1{l}"] = din(f"pw1{l}", (C, 2 * C)); W[f"pw2{l}"] = din(f"pw2{l}", (C, C))

    SCR = {}

    def scr(name, n_oc, n_kc, Mb):
        SCR[name] = (nc.dram_tensor("s_" + name, [n_oc, 128, n_kc * Mb], BF16, kind="Internal").ap(), n_kc, Mb)

    for l in range(nlayers):
        scr(f"up{l}", 2 * NF, 16, 128); scr(f"dnA{l}", 16, 15, 128); scr(f"dnB{l}", 16, 14, 128); scr(f"dnC{l}", 16, 14, 128)
        if l % 2 == 0:
            for n in ("r", "k", "v", "wo"):
                scr(f"{n}{l}", 16, 16, 128)
            scr(f"g1a{l}", 1, 16, 128); scr(f"g1b{l}", 1, 16, 128)
            scr(f"w1{l}", 1, 16, 96); scr(f"a1{l}", 1, 16, 96)
            if l >= 2:
                scr(f"v1{l}", 1, 16, 128)
            scr(f"lo2{l}", 16, 5, 128)
        else:
            scr(f"pw1{l}", 32, 16, 128); scr(f"pw2{l}", 16, 16, 128)
    vfs = nc.dram_tensor("s_vfirst", [16, 128, TT], F32, kind="Internal").ap()

    def sb(name, shape, dt=F32):
        return es.enter_context(nc.sbuf_tensor(name, list(shape), dt))

    def ps(name, shape, dt=F32):
        return es.enter_context(nc.psum_tensor(name, list(shape), dt))

    prm_t = sb("prm_t", (128, lay.n))
    drv_t = sb("drv_t", (128, nlayers * 112))
    x_t = sb("x_t", (128, NC_, TT))
    hb = sb("hb", (128, NC_, TT + 1), BF16)
    ident = sb("ident", (128, 128), BF16)
    identF = sb("identF", (128, 128))
    mask2 = sb("mask2", (128, 256), BF16)
    mask2a = sb("mask2a", (128, 256), BF16)
    maskSL = sb("maskSL", (128, 128), BF16)
    maskUR = sb("maskUR", (128, 128), BF16)
    bo1 = sb("bo1", (128, 128), BF16)
    bo64 = sb("bo64", (128, 128), BF16)
    oneC = sb("oneC", (128, 128), BF16)
    rmask = sb("rmask", (128, TT))
    wsl = sb("wsl", (128, NSLOT, SLOT), BF16)
    nrw = (nlayers + 1) // 2
    ncv = nlayers // 2
    S_t = sb("S_t", (128, nrw * NC_ * 64))
    Sb_t = sb("Sb_t", (128, nrw * NC_ * 64), BF16)
    shc = sb("shc", (128, nrw, NC_), BF16)
    cvc = sb("cvc", (128, max(ncv, 1), NC_, 30))
    ffc = sb("ffc", (128, nlayers, NF, 2))

    def pc(k, i=0, n=1):
        o, _ = lay.off[k]
        return prm_t[:, o + i:o + i + n]

    def dc(l, i, n=1):
        return drv_t[:, l * 112 + i:l * 112 + i + n]
    omu = sb("omu", (128, nrw, 96))
    ca_t = sb("ca_t", (128, NC_))

    pd_t = [ps(f"pd{i}", (128, 512)) for i in range(3)]
    pw_t = [ps(f"pw{i}", (128, 512)) for i in range(4)]
    pt_t = ps("pt", (128, 8, 128), BF16)
    pd = Pool("pd", [t for t in pd_t])
    pw = Pool("pw", [t for t in pw_t])
    ptp = Pool("pt", [pt_t])
    pall = Pool("pall", pd_t + pw_t, keys=[f"pd{i}" for i in range(3)] + [f"pw{i}" for i in range(4)])

    def mm(o, lhsT, rhs, start, stop, r, w):
        P.add("pe", lambda e: e.matmul(o, lhsT=lhsT, rhs=rhs, start=start, stop=stop), r=r, w=w)

    def tr(o, in_, r, w):
        P.add("pe", lambda e: e.transpose(o, in_, ident[:, :]), r=list(r) + ["const"], w=w)

    def act(o, in_, func, r, w, bias=None, scale=None):
        kw = {}
        if bias is not None:
            kw["bias"] = bias
        if scale is not None:
            kw["scale"] = scale
        P.add("act", lambda e: e.activation(out=o, in_=in_, func=func, **kw), r=r, w=w)

    def tsc(eng, o, in0, s1, s2, op0, op1, r, w):
        if op1 is None:
            P.add(eng, lambda e: e.tensor_scalar(out=o, in0=in0, scalar1=s1, scalar2=None, op0=op0), r=r, w=w)
        else:
            P.add(eng, lambda e: e.tensor_scalar(out=o, in0=in0, scalar1=s1, scalar2=s2, op0=op0, op1=op1), r=r, w=w)

    def rsqrt(o, in_, eps, r, w, premax=None):
        if premax is not None:
            tsc("dve", o, in_, premax, None, ALU.max, None, r=r, w=w)
            act(o, o, AF.Ln, r=w, w=w)
        else:
            act(o, in_, AF.Ln, r=r, w=w, bias=eps)
        act(o, o, AF.Exp, r=w, w=w, scale=-0.5)

    def tt(eng, o, in0, in1, op, r, w):
        P.add(eng, lambda e: e.tensor_tensor(out=o, in0=in0, in1=in1, op=op), r=r, w=w)

    def stt(o, in0, scalar, in1, op0, op1, r, w):
        P.add("dve", lambda e: e.scalar_tensor_tensor(out=o, in0=in0, scalar=scalar, in1=in1, op0=op0, op1=op1), r=r, w=w)

    def cp(eng, o, in_, r, w):
        if eng == "act":
            act(o, in_, AF.Copy, r, w)
        else:
            P.add(eng, lambda e: e.tensor_copy(out=o, in_=in_), r=r, w=w)

    def dma(o, in_, r, w, semkey, at=None, eng="sp"):
        return P.add(eng, lambda e: e.dma_start(out=o, in_=in_), r=r, w=w, dma=True, semkey=semkey, at=at)

    with ExitStack() as pes:
        def psb(name, shape, dt=F32):
            return pes.enter_context(nc.sbuf_tensor(name, list(shape), dt))
        cst_t = psb("cst_t", (128, 10 * 128 + TT))
        dma(cst_t[:, :], cst[:, :], r=[], w=["cst_t"], semkey="cst_t")
        dma(prm_t[:, :], prm[:, :], r=[], w=["prm"], semkey="prm")
        for i, tgt in enumerate((ident, None, None, None, bo1, bo64, oneC)):
            if tgt is not None:
                cp("dve", tgt[:, :], cst_t[:, i * 128:(i + 1) * 128], r=["cst_t"], w=["const"])
        cp("dve", mask2[:, :], cst_t[:, 128:384], r=["cst_t"], w=["const"])
        co_ = 896 + TT
        cp("dve", mask2a[:, 0:128], cst_t[:, co_:co_ + 128], r=["cst_t"], w=["const"])
        cp("dve", mask2a[:, 128:256], cst_t[:, 256:384], r=["cst_t"], w=["const"])
        cp("dve", maskSL[:, :], cst_t[:, co_ + 128:co_ + 256], r=["cst_t"], w=["const"])
        cp("dve", maskUR[:, :], cst_t[:, co_ + 256:co_ + 384], r=["cst_t"], w=["const"])
        cp("dve", identF[:, :], cst_t[:, 0:128], r=["cst_t"], w=["const"])
        cp("act", rmask[:, :], cst_t[:, 896:896 + TT], r=["cst_t"], w=["const"])
        for t_, k_ in ((S_t, "S"), (Sb_t, "Sb"), (shc, "shc"), (cvc, "cvc"), (ffc, "ffc")):
            P.add("pool", (lambda tt_: (lambda e: e.memset(tt_, 0.0)))(t_[:]), r=[], w=[k_])
        act(ca_t[:, :], pc("c", 0, 16), AF.Silu, r=["prm"], w=["ca"])
        st32 = [psb(f"st32_{i}", (128, SLOT)) for i in range(3)]
        st16 = [psb(f"st16_{i}", (128, SLOT), BF16) for i in range(3)]
        cnt = [0]
        cast_engs = ("act", "dve", "pool")

        def cast_block(src_ap, kp, n_kc, Mb, dst_ap):
            i = cnt[0] % 3; cnt[0] += 1
            ne = n_kc * Mb
            s32 = st32[i][0:kp, 0:ne]; s16 = st16[i][0:kp, 0:ne]
            dma(s32.rearrange("p (k m) -> p k m", m=Mb) if n_kc > 1 else s32, src_ap, r=[], w=[f"st32_{i}"], semkey=f"st32_{i}")
            cp(cast_engs[(cnt[0]) % 3], s16, s32, r=[f"st32_{i}"], w=[f"st16_{i}"])
            dma(dst_ap, s16, r=[f"st16_{i}"], w=[], semkey=f"st16_{i}")

        def cast_mat(name, src, r0, n_kc, Mb, c0, n_oc):
            dst, nk, mb = SCR[name]
            for oc in range(n_oc):
                sap = src[r0:r0 + n_kc * 128, c0 + oc * Mb:c0 + (oc + 1) * Mb].rearrange("(k p) m -> p k m", p=128)
                cast_block(sap, 128, n_kc, Mb, dst[oc, :, :])

        for l in range(nlayers):
            cast_mat(f"up{l}", W[f"up{l}"], 0, 16, 128, 0, 2 * NF)
            cast_mat(f"dnA{l}", W[f"dn{l}"], 0, 15, 128, 0, 16)
            cast_mat(f"dnB{l}", W[f"dn{l}"], 15 * 128, 14, 128, 0, 16)
            cast_mat(f"dnC{l}", W[f"dn{l}"], 29 * 128, 14, 128, 0, 16)
            if l % 2 == 0:
                for n in ("r", "k", "v", "wo"):
                    cast_mat(f"{n}{l}", W[f"{n}{l}"], 0, 16, 128, 0, 16)
                cast_mat(f"g1a{l}", W[f"g1{l}"], 0, 16, 128, 0, 1)
                cast_mat(f"g1b{l}", W[f"g1{l}"], 0, 16, 128, 128, 1)
                cast_mat(f"w1{l}", W[f"w1{l}"], 0, 16, 96, 0, 1)
                cast_mat(f"a1{l}", W[f"a1{l}"], 0, 16, 96, 0, 1)
                if l >= 2:
                    i = cnt[0] % 3; cnt[0] += 1
                    s16f = st16[i][:, 0:C]
                    P.add("pool", (lambda a_: (lambda e: e.memset(a_, 0.0)))(s16f), r=[], w=[f"st16_{i}"])
                    s32 = st32[i][:, 0:16 * 64]
                    dma(s32.rearrange("p (k m) -> p k m", m=64), W[f"v1{l}"].rearrange("(k p) m -> p k m", p=128),
                        r=[], w=[f"st32_{i}"], semkey=f"st32_{i}")
                    cp(cast_engs[cnt[0] % 3], s16f.rearrange("p (k m) -> p k m", m=128)[:, :, 0:64],
                       s32.rearrange("p (k m) -> p k m", m=64), r=[f"st32_{i}"], w=[f"st16_{i}"])
                    dma(SCR[f"v1{l}"][0][0, :, :], s16f, r=[f"st16_{i}"], w=[], semkey=f"st16_{i}")
                lo2 = SCR[f"lo2{l}"][0]
                srcs = [(W[f"w2{l}"], 0, 96), (W[f"a2{l}"], 0, 96)]
                srcs.append((W[f"v2{l}"], 0, 64) if l >= 2 else None)
                srcs += [(W[f"g2{l}"], 0, 128), (W[f"g2{l}"], 128, 128)]
                for s_i, sd in enumerate(srcs):
                    i = cnt[0] % 3; cnt[0] += 1
                    s16f = st16[i][:, 0:C]
                    P.add("pool", (lambda a_: (lambda e: e.memset(a_, 0.0)))(s16f), r=[], w=[f"st16_{i}"])
                    if sd is not None:
                        src, r0, kp = sd
                        s32 = st32[i][0:kp, 0:C]; s16 = st16[i][0:kp, 0:C]
                        dma(s32, src[r0:r0 + kp, :], r=[], w=[f"st32_{i}"], semkey=f"st32_{i}")
                        cp(cast_engs[cnt[0] % 3], s16, s32, r=[f"st32_{i}"], w=[f"st16_{i}"])
                    dma(lo2[:, :, s_i * 128:(s_i + 1) * 128].rearrange("f p j -> p f j"),
                        s16f.rearrange("p (f j) -> p f j", j=128), r=[f"st16_{i}"], w=[], semkey=f"st16_{i}")
            else:
                cast_mat(f"pw1{l}", W[f"pw1{l}"], 0, 16, 128, 0, 32)
                cast_mat(f"pw2{l}", W[f"pw2{l}"], 0, 16, 128, 0, 16)

        adw = [psb(f"adw{i}", (128, NC_, 128)) for i in range(2)]
        for l in range(nlayers):
            pm_, pmk = pd.get()
            for oc in range(96):
                a_, ak = adw[oc % 2], f"adw{oc % 2}"
                dma(a_[:, :, :], W[f"ada{l}"][:, oc * 128:(oc + 1) * 128].rearrange("(k p) m -> p k m", p=128),
                    r=[], w=[ak], semkey=ak)
                for kc in range(NC_):
                    mm(pm_[:, oc:oc + 1], a_[:, kc, :], ca_t[:, kc:kc + 1], kc == 0, kc == NC_ - 1, r=[ak, "ca"], w=[pmk])
            mod = psb(f"mod{l}", (128, 96))
            tt("dve", mod[:, :], pm_[:, 0:96], pc(f"adab{l}", 0, 96), ALU.add, r=[pmk, "prm"], w=[f"mod{l}"])
            for hlf, gname in ((0, f"gmix{l}"), (1, f"gffn{l}")):
                b0 = hlf * 48
                stt(dc(l, b0, 16), mod[:, (3 * hlf + 1) * 16:(3 * hlf + 2) * 16], 1.0, pc(gname, 0, 16), ALU.add, ALU.mult,
                    r=[f"mod{l}", "prm"], w=["drv"])
                cp("dve", dc(l, b0 + 16, 16), mod[:, (3 * hlf) * 16:(3 * hlf + 1) * 16], r=[f"mod{l}"], w=["drv"])
                cp("dve", dc(l, b0 + 32, 16), mod[:, (3 * hlf + 2) * 16:(3 * hlf + 3) * 16], r=[f"mod{l}"], w=["drv"])
            if l % 2 == 0:
                tsc("dve", dc(l, 96, 16), pc(f"ka{l}", 0, 16), -1.0, 1.0, ALU.mult, ALU.add, r=["prm"], w=["drv"])
                tsc("dve", omu[:, l // 2, :], pc(f"mu{l}", 0, 96), -1.0, 1.0, ALU.mult, ALU.add, r=["prm"], w=["drv"])
            else:
                tt("dve", dc(l, 96, 16), pc(f"b2{l}", 0, 16), dc(l, 32, 16), ALU.mult, r=["prm", "drv"], w=["drv"])
    P.barrier()
    prologue_end = P.pos()

    slot_i = [0]
    slot_last = [prologue_end] * NSLOT

    def wget(name, oc):
        dst, n_kc, Mb = SCR[name]
        j = slot_i[0] % NSLOT; slot_i[0] += 1
        ne = n_kc * Mb
        at = max(slot_last[j], prologue_end)
        dma(wsl[:, j, 0:ne], dst[oc, :, :], r=[], w=[f"ws{j}"], semkey=f"ws{j}", at=at)
        return j, f"ws{j}"

    def wdone(j):
        slot_last[j] = P.pos()

    def wblk(j, kc, Mb=128, kp=128):
        return wsl[0:kp, j, kc * Mb:(kc + 1) * Mb]

    def mkpool(name, n, shape, dt=F32):
        return Pool(name, [sb(f"{name}{i}", shape, dt) for i in range(n)])

    sqp = mkpool("sq", 2, (128, TT), BF16)
    f32p = mkpool("f", 4, (128, TT))
    rwp = mkpool("rw", 8, (128, TT))
    rstd_p = mkpool("rstd", 2, (128, TT))
    arena = sb("arena", (128, 12288))
    arena_b = arena[:, :].bitcast(BF16)
    xm_v = arena_b.rearrange("p (a k t) -> p a k t", a=3, k=NC_)
    xm = [xm_v[:, i] for i in range(3)]
    hid = arena_b[:, 0:NF * TT].rearrange("p (f t) -> p f t", t=TT)
    cb_t = arena[:, 0:NC_ * TT].rearrange("p (k t) -> p k t", t=TT)
    gbp = mkpool("gb", 1, (128, TT + 2))
    ubp = mkpool("ub", 1, (128, TT + 30))
    l1p = mkpool("l1", 5, (128, TT), BF16)
    b16p = mkpool("b", 6, (128, TT), BF16)
    ARp = mkpool("AR", 1, (128, NCH, 256), BF16)
    TRp = mkpool("TR", 3, (128, NCH, 128), BF16)
    e_p = mkpool("E", 4, (128, 256), BF16)
    am_p = mkpool("AM", 2, (128, 256), BF16)
    m_p = mkpool("M", 3, (128, 128), BF16)
    ct_p = mkpool("Ct", 2, (128, 128), BF16)
    tt_p = mkpool("Tt", 3, (128, 128), BF16)
    xu_p = mkpool("XU", 6, (128, 64), BF16)
    yt_p = mkpool("yt", 1, (128, TT))
    xs_p = mkpool("xs", 2, (128, 64))
    ytok_p = mkpool("ytok", 1, (128, NCH, 128))
    vf_p = mkpool("vf", 1, (128, TT))
    o_p = f32p

    def norm_mod(l, which, func=AF.Identity):
        b0 = 0 if which == "mix" else 48
        acc, ak = pd.get()
        for kc in range(NC_):
            s, sk = sqp.get()
            act(s[:, :], x_t[:, kc, :], AF.Square, r=[f"x{kc}"], w=[sk])
            mm(acc[:, :], oneC[:, :], s[:, :], kc == 0, kc == NC_ - 1, r=[sk, "const"], w=[ak])
        rs, rk = rstd_p.get()
        rsqrt(rs[:, :], acc[:, :], RMS_EPS, r=[ak], w=[rk])
        for kc in range(NC_):
            t, tk = f32p.get()
            stt(t[:, :], x_t[:, kc, :], dc(l, b0 + kc), rs[:, :], ALU.mult, ALU.mult, r=[f"x{kc}", "drv", rk], w=[tk])
            act(hb[:, kc, 1:TT + 1], t[:, :], func, bias=dc(l, b0 + 16 + kc), r=[tk, "drv"], w=[f"hb{kc}"])

    def proj16(name, oc, src_keys, rhs_of, M=128, pool=None):
        j, wk = wget(name, oc)
        acc, ak = (pool or pd).get()
        Mb = SCR[name][2]
        for kc in range(NC_):
            mm(acc[0:M, :], wblk(j, kc, Mb), rhs_of(kc), kc == 0, kc == NC_ - 1, r=[wk, src_keys[kc]], w=[ak])
        wdone(j)
        return acc, ak

    hbk = [f"hb{kc}" for kc in range(NC_)]

    def ffn(l):
        norm_mod(l, "ffn")
        for fc in range(NF):
            pg, pgk = proj16(f"up{l}", fc, hbk, lambda kc: hb[:, kc, 1:TT + 1], pool=pall)
            pv, pvk = proj16(f"up{l}", NF + fc, hbk, lambda kc: hb[:, kc, 1:TT + 1], pool=pall)
            gb, gk = gbp.get()
            cp("pool", gb[:, 0:2], ffc[:, l, fc, :], r=["ffc"], w=[gk])
            act(gb[:, 2:TT + 2], pg[:, :], AF.Copy, r=[pgk], w=[gk])
            t1, t1k = f32p.get()
            act(t1[:, :], pg[:, :], AF.Identity, scale=pc(f"fdw{l}", 2 * NF + fc), bias=pc(f"fdb{l}", fc), r=[pgk, "prm"], w=[t1k])
            cp("pool", ffc[:, l, fc, :], gb[:, TT:TT + 2], r=[gk], w=["ffc"])
            t2, t2k = f32p.get()
            stt(t2[:, :], gb[:, 1:TT + 1], pc(f"fdw{l}", NF + fc), t1[:, :], ALU.mult, ALU.add, r=[gk, t1k, "prm"], w=[t2k])
            t3, t3k = f32p.get()
            stt(t3[:, :], gb[:, 0:TT], pc(f"fdw{l}", fc), t2[:, :], ALU.mult, ALU.add, r=[gk, t2k, "prm"], w=[t3k])
            t4, t4k = f32p.get()
            act(t4[:, :], t3[:, :], AF.Silu, r=[t3k], w=[t4k])
            tt("dve", hid[:, fc, :], pv[:, :], t4[:, :], ALU.mult, r=[t4k, pvk], w=[f"hid{fc}"])
        for oc in range(NC_):
            acc, ak = pd.get()
            for nm_, k0, nk in ((f"dnA{l}", 0, 15), (f"dnB{l}", 15, 14), (f"dnC{l}", 29, 14)):
                j, wk = wget(nm_, oc)
                for kc in range(nk):
                    mm(acc[:, :], wblk(j, kc), hid[:, k0 + kc, :], k0 + kc == 0, k0 + kc == NF - 1, r=[wk, f"hid{k0 + kc}"], w=[ak])
                wdone(j)
            stt(x_t[:, oc, :], acc[:, :], dc(l, 80 + oc), x_t[:, oc, :], ALU.mult, ALU.add, r=[ak, "drv", f"x{oc}"], w=[f"x{oc}"])

    def conv(l):
        j_ = l // 2
        norm_mod(l, "mix")
        for cc in range(NC_):
            pa, pak = proj16(f"pw1{l}", cc, hbk, lambda kc: hb[:, kc, 1:TT + 1])
            pg, pgk = proj16(f"pw1{l}", NC_ + cc, hbk, lambda kc: hb[:, kc, 1:TT + 1])
            sg, sgk = f32p.get()
            act(sg[:, :], pg[:, :], AF.Sigmoid, bias=pc(f"b1{l}", NC_ + cc), r=[pgk, "prm"], w=[sgk])
            ub, uk = ubp.get()
            cp("pool", ub[:, 0:30], cvc[:, j_, cc, :], r=["cvc"], w=[uk])
            stt(ub[:, 30:TT + 30], pa[:, :], pc(f"b1{l}", cc), sg[:, :], ALU.add, ALU.mult, r=[pak, sgk, "prm"], w=[uk])
            cp("pool", cvc[:, j_, cc, :], ub[:, TT:TT + 30], r=[uk], w=["cvc"])
            a0, a0k = f32p.get(); a1, a1k = f32p.get()
            tsc("dve", a0[:, :], ub[:, 0:TT], pc(f"cw{l}", 0 * 16 + cc), pc(f"cb{l}", cc), ALU.mult, ALU.add, r=[uk, "prm"], w=[a0k])
            tsc("dve", a1[:, :], ub[:, 1:TT + 1], pc(f"cw{l}", 1 * 16 + cc), None, ALU.mult, None, r=[uk, "prm"], w=[a1k])
            for k in range(2, 31):
                a, ak_ = (a0, a0k) if k % 2 == 0 else (a1, a1k)
                stt(a[:, :], ub[:, k:k + TT], pc(f"cw{l}", k * 16 + cc), a[:, :], ALU.mult, ALU.add, r=[uk, "prm", ak_], w=[ak_])
            tt("dve", cb_t[:, cc, :], a0[:, :], a1[:, :], ALU.add, r=[a0k, a1k], w=[f"cb{cc}"])
        pm_, pmk = pd.get()
        for cc in range(NC_):
            s, sk = sqp.get()
            cp("act", s[:, :], cb_t[:, cc, :], r=[f"cb{cc}"], w=[sk])
            mm(pm_[:, :], oneC[:, :], s[:, :], cc == 0, cc == NC_ - 1, r=[sk, "const"], w=[pmk])
        pv_, pvk = pd.get()
        for cc in range(NC_):
            stt(cb_t[:, cc, :], pm_[:, :], -1.0, cb_t[:, cc, :], ALU.mult, ALU.add, r=[f"cb{cc}", pmk], w=[f"cb{cc}"])
            s, sk = sqp.get()
            act(s[:, :], cb_t[:, cc, :], AF.Square, r=[f"cb{cc}"], w=[sk])
            mm(pv_[:, :], oneC[:, :], s[:, :], cc == 0, cc == NC_ - 1, r=[sk, "const"], w=[pvk])
        rs, rk = rstd_p.get()
        rsqrt(rs[:, :], pv_[:, :], LN_EPS, r=[pvk], w=[rk])
        for cc in range(NC_):
            t, tk = f32p.get()
            stt(t[:, :], cb_t[:, cc, :], pc(f"clg{l}", cc), rs[:, :], ALU.mult, ALU.mult, r=[f"cb{cc}", "prm", rk], w=[tk])
            act(hb[:, cc, 1:TT + 1], t[:, :], AF.Silu, bias=pc(f"clb{l}", cc), r=[tk, "prm"], w=[f"hb{cc}"])
        for oc in range(NC_):
            po, pok = proj16(f"pw2{l}", oc, hbk, lambda kc: hb[:, kc, 1:TT + 1])
            t, tk = f32p.get()
            act(t[:, :], po[:, :], AF.Identity, scale=dc(l, 32 + oc), bias=dc(l, 96 + oc), r=[pok, "drv"], w=[tk])
            tt("pool", x_t[:, oc, :], x_t[:, oc, :], t[:, :], ALU.add, r=[tk, f"x{oc}"], w=[f"x{oc}"])

    def rwkv(l, ti):
        j_ = l // 2
        vfirst = l >= 2
        norm_mod(l, "mix")
        for kc in range(NC_):
            cp("pool", hb[:, kc, 0:1], shc[:, j_, kc:kc + 1], r=["shc"], w=[f"hb{kc}"])
            cp("pool", shc[:, j_, kc:kc + 1], hb[:, kc, TT:TT + 1], r=[f"hb{kc}"], w=["shc"])

        def mix(n, dst, di):
            for kc in range(NC_):
                t, tk = f32p.get()
                act(t[:, :], hb[:, kc, 1:TT + 1], AF.Identity, scale=omu[:, j_, n * 16 + kc:n * 16 + kc + 1], r=[f"hb{kc}", "drv"], w=[tk])
                stt(dst[:, kc, :], hb[:, kc, 0:TT], pc(f"mu{l}", n * 16 + kc), t[:, :], ALU.mult, ALU.add,
                    r=[f"hb{kc}", "prm", tk], w=[f"xm{di}_{kc}"])
        if dbg == "r1":
            return
        xk0 = [f"xm0_{kc}" for kc in range(NC_)]
        xk1 = [f"xm1_{kc}" for kc in range(NC_)]
        xk2 = [f"xm2_{kc}" for kc in range(NC_)]
        mix(5, xm[0], 0)
        g1s = []
        for hf, nm in enumerate((f"g1a{l}", f"g1b{l}")):
            p_, pk_ = proj16(nm, 0, xk0, lambda kc: xm[0][:, kc, :])
            g, gk = l1p.get()
            act(g[:, :], p_[:, :], AF.Sigmoid, r=[pk_], w=[gk])
            g1s.append((g, gk))
        if dbg == "r2":
            return
        mix(3, xm[0], 0)
        p_, pk_ = proj16(f"w1{l}", 0, xk0, lambda kc: xm[0][:, kc, :], M=96)
        lw1, lw1k = l1p.get()
        act(lw1[0:96, :], p_[0:96, :], AF.Tanh, r=[pk_], w=[lw1k])
        mix(4, xm[0], 0)
        p_, pk_ = proj16(f"a1{l}", 0, xk0, lambda kc: xm[0][:, kc, :], M=96)
        la1, la1k = l1p.get()
        act(la1[0:96, :], p_[0:96, :], AF.Copy, r=[pk_], w=[la1k])
        mix(2, xm[2], 2)
        if vfirst:
            p_, pk_ = proj16(f"v1{l}", 0, xk2, lambda kc: xm[2][:, kc, :])
            lv1, lv1k = l1p.get()
            act(lv1[:, :], p_[:, :], AF.Copy, r=[pk_], w=[lv1k])
        mix(1, xm[1], 1)
        mix(0, xm[0], 0)

        if dbg == "r3":
            return
        for fc in range(NC_):
            if dbg in ("r4", "r5", "r6", "u1", "u2", "u3", "u4", "u5", "u4a", "u4b", "u4c") and fc > 0:
                return
            Sk = f"S{j_}_{fc}"; Sbk = f"Sb{j_}_{fc}"
            pr, prk = proj16(f"r{l}", fc, xk0, lambda kc: xm[0][:, kc, :])
            r_, rk_ = rwp.get()
            act(r_[:, :], pr[:, :], AF.Copy, r=[prk], w=[rk_])
            pk, pkk = proj16(f"k{l}", fc, xk1, lambda kc: xm[1][:, kc, :])
            k_, kk_ = rwp.get()
            act(k_[:, :], pk[:, :], AF.Copy, r=[pkk], w=[kk_])
            pv, pvk = proj16(f"v{l}", fc, xk2, lambda kc: xm[2][:, kc, :])
            v_, vk_ = rwp.get()
            act(v_[:, :], pv[:, :], AF.Copy, r=[pvk], w=[vk_])
            j2, w2k = wget(f"lo2{l}", fc)
            pw_, pwk = pd.get()
            mm(pw_[:, :], wblk(j2, 0, 128, 96), lw1[0:96, :], True, True, r=[w2k, lw1k], w=[pwk])
            sw, swk = rwp.get()
            act(sw[:, :], pw_[:, :], AF.Sigmoid, bias=pc(f"w0{l}", fc), r=[pwk, "prm"], w=[swk])
            pa_, pak = pd.get()
            mm(pa_[:, :], wblk(j2, 1, 128, 96), la1[0:96, :], True, True, r=[w2k, la1k], w=[pak])
            asg, ask = rwp.get()
            act(asg[:, :], pa_[:, :], AF.Sigmoid, bias=pc(f"a0{l}", fc), r=[pak, "prm"], w=[ask])
            pg_, pgk = pd.get()
            mm(pg_[:, :], wblk(j2, 3), g1s[0][0][:, :], True, False, r=[w2k, g1s[0][1]], w=[pgk])
            mm(pg_[:, :], wblk(j2, 4), g1s[1][0][:, :], False, True, r=[w2k, g1s[1][1]], w=[pgk])
            gt, gtk = b16p.get()
            act(gt[:, :], pg_[:, :], AF.Copy, r=[pgk], w=[gtk])
            if vfirst:
                pv2, pv2k = pd.get()
                mm(pv2[:, :], wblk(j2, 2), lv1[:, :], True, True, r=[w2k, lv1k], w=[pv2k])
                sv, svk = f32p.get()
                act(sv[:, :], pv2[:, :], AF.Sigmoid, bias=pc(f"v0{l}", fc), r=[pv2k, "prm"], w=[svk])
                vf, vfk = vf_p.get()
                dma(vf[:, :], vfs[fc, :, :], r=[f"vfs{fc}"], w=[vfk], semkey=vfk)
                tt("pool", vf[:, :], vf[:, :], v_[:, :], ALU.subtract, r=[vfk, vk_], w=[vfk])
                tt("pool", vf[:, :], vf[:, :], sv[:, :], ALU.mult, r=[vfk, svk], w=[vfk])
                tt("pool", v_[:, :], v_[:, :], vf[:, :], ALU.add, r=[vfk, vk_], w=[vk_])
            elif nlayers > 2:
                dma(vfs[fc, :, :], v_[:, :], r=[vk_], w=[f"vfs{fc}"], semkey=vk_)
            wdone(j2)
            if dbg == "r4":
                return
            kkr, kkrk = rwp.get()
            tsc("dve", kkr[:, :], k_[:, :], pc(f"kk{l}", fc), None, ALU.mult, None, r=[kk_, "prm"], w=[kkrk])
            s, sk = sqp.get()
            act(s[:, :], kkr[:, :], AF.Square, r=[kkrk], w=[sk])
            pn, pnk = pd.get()
            mm(pn[:, :], bo1[:, :], s[:, :], True, True, r=[sk, "const"], w=[pnk])
            rn, rnk = rstd_p.get()
            rsqrt(rn[:, :], pn[:, :], None, r=[pnk], w=[rnk], premax=1e-24)
            tt("dve", kkr[:, :], kkr[:, :], rn[:, :], ALU.mult, r=[kkrk, rnk], w=[kkrk])
            f_, fk = f32p.get()
            act(f_[:, :], asg[:, :], AF.Identity, scale=pc(f"ka{l}", fc), bias=dc(l, 96 + fc), r=[ask, "prm", "drv"], w=[fk])
            tt("pool", k_[:, :], k_[:, :], f_[:, :], ALU.mult, r=[kk_, fk], w=[kk_])
            tt("pool", asg[:, :], asg[:, :], kkr[:, :], ALU.mult, r=[ask, kkrk], w=[ask])
            cw, cwk = rwp.get()
            P.add("dve", lambda e, cw=cw, sw=sw: e.tensor_tensor_scan(out=cw[:, :], data0=rmask[:, :], data1=sw[:, :],
                                                                     initial=0.0, op0=ALU.mult, op1=ALU.add),
                  r=[swk, "const"], w=[cwk])
            tt("pool", sw[:, :], cw[:, :], sw[:, :], ALU.subtract, r=[cwk, swk], w=[swk])
            Wt, Wk = rwp.get()
            act(Wt[:, :], cw[:, :], AF.Exp, scale=-C0, r=[cwk], w=[Wk])
            act(sw[:, :], sw[:, :], AF.Exp, scale=-C0, r=[swk], w=[swk])
            act(cw[:, :], cw[:, :], AF.Exp, scale=C0, r=[cwk], w=[cwk])
            AR, ARk = ARp.get()
            tt("dve", AR[:, :, 128:256], r_[:, :].rearrange("p (c t) -> p c t", t=L), Wt[:, :].rearrange("p (c t) -> p c t", t=L),
               ALU.mult, r=[rk_, Wk], w=[ARk])
            stt(AR[:, :, 0:128], kkr[:, :].rearrange("p (c t) -> p c t", t=L), -1.0, sw[:, :].rearrange("p (c t) -> p c t", t=L),
                ALU.mult, ALU.mult, r=[kkrk, swk], w=[ARk])
            BT, BTk = b16p.get()
            tt("dve", BT[:, :], asg[:, :], cw[:, :], ALU.mult, r=[ask, cwk], w=[BTk])
            KT, KTk = b16p.get()
            tt("dve", KT[:, :], k_[:, :], cw[:, :], ALU.mult, r=[kk_, cwk], w=[KTk])
            bend, bendk = b16p.get(); kend, kendk = b16p.get()
            for c in range(NCH):
                wl = Wt[:, c * L + L - 1:c * L + L]
                tsc("pool", bend[:, c * L:(c + 1) * L], BT[:, c * L:(c + 1) * L], wl, None, ALU.mult, None, r=[BTk, Wk], w=[bendk])
                tsc("pool", kend[:, c * L:(c + 1) * L], KT[:, c * L:(c + 1) * L], wl, None, ALU.mult, None, r=[KTk, Wk], w=[kendk])
            vb, vbk = b16p.get()
            cp("act", vb[:, :], v_[:, :], r=[vk_], w=[vbk])
            rkr, rkrk = sqp.get()
            stt(rkr[:, :], r_[:, :], pc(f"rk{l}", fc), k_[:, :], ALU.mult, ALU.mult, r=[rk_, kk_, "prm"], w=[rkrk])
            if dbg == "r5":
                return
            VT, VTk = TRp.get(); BET, BETk = TRp.get(); KET, KETk = TRp.get()
            for ei, (src, srck, dst, dstk) in enumerate(((vb, vbk, VT, VTk), (bend, bendk, BET, BETk), (kend, kendk, KET, KETk))):
                p_, pk_ = ptp.get()
                for c in range(NCH):
                    tr(p_[:, c, :], src[:, c * L:(c + 1) * L], r=[srck], w=[pk_])
                cp("act" if ei % 2 == 0 else "dve", dst[:, :, :], p_[:, 0:NCH, :], r=[pk_], w=[dstk])
            if dbg == "r6":
                return
            pb, pbk = pd.get()
            mm(pb[:, :], bo1[:, :], rkr[:, :], True, True, r=[rkrk, "const"], w=[pbk])
            tt("dve", v_[:, :], pb[:, :], v_[:, :], ALU.mult, r=[vk_, pbk], w=[vk_])
            yt, ytk = yt_p.get()
            ytok, ytokk = ytok_p.get()
            for c in range(NCH):
                def unit(hh, c=c):
                    rows = slice(hh * 64, hh * 64 + 64)
                    rows = slice(hh * 64, hh * 64 + 64)
                    ARc = AR[:, c, :]
                    BTc = BT[:, c * L:(c + 1) * L]; KTc = KT[:, c * L:(c + 1) * L]
                    bA, bAk = pw.get(); bB, bBk = pw.get()
                    P1 = bA[:, 0:256]; P2 = bA[:, 256:512]; P3 = bB[:, 0:128]
                    mm(P1, BTc[rows, :], ARc[rows, :], True, True, r=[BTk, ARk], w=[bAk])
                    mm(P2, KTc[rows, :], ARc[rows, :], True, True, r=[KTk, ARk], w=[bAk])
                    mm(P3, ARc[rows, 0:128], BTc[rows, :], True, True, r=[BTk, ARk], w=[bBk])
                    E1, E1k = e_p.get(); E2, E2k = e_p.get()
                    tt("dve", E1[:, :], P1, mask2a[:, :], ALU.mult, r=[bAk, "const"], w=[E1k])
                    Ct, Ctk = ct_p.get()
                    tt("dve", Ct[:, :], P1[:, 0:128], maskUR[:, :], ALU.mult, r=[bAk, "const"], w=[Ctk])
                    tt("dve", E2[:, :], P2, mask2[:, :], ALU.mult, r=[bAk, "const"], w=[E2k])
                    M0, M0k = m_p.get()
                    tt("dve", M0[:, :], P3, maskSL[:, :], ALU.mult, r=[bBk, "const"], w=[M0k])
                    Tt, Ttk = tt_p.get()
                    tt("pool", Tt[:, :], E1[:, 0:128], ident[:, :], ALU.add, r=[E1k, "const"], w=[Ttk])
                    yield
                    Aprev, Apk = E1[:, 0:128], E1k
                    Mprev, Mpk = M0[:, :], M0k
                    Q = bB[:, 0:256]; Q2 = bB[:, 256:384]
                    for lev in range(1, 6):
                        if lev < 5:
                            mm(Q[:, 0:128], Mprev, Aprev, True, True, r=[Apk, Mpk], w=[bBk])
                            mm(Q[:, 128:256], Aprev, Mprev, True, True, r=[Apk, Mpk], w=[bBk])
                            AM, AMk = am_p.get()
                            cp("act", AM[:, :], Q, r=[bBk], w=[AMk])
                            Aprev, Apk = AM[:, 0:128], AMk
                            Mprev, Mpk = AM[:, 128:256], AMk
                            yield
                        else:
                            mm(Q[:, 0:128], Aprev, Mprev, True, True, r=[Apk, Mpk], w=[bBk])
                            M6, M6k = m_p.get()
                            cp("act", M6[:, :], Q[:, 0:128], r=[bBk], w=[M6k])
                            Mprev, Mpk = M6[:, :], M6k
                            yield
                        mm(Q2, ident[:, :], Tt[:, :], True, False, r=["const", Ttk], w=[bBk])
                        mm(Q2, Mprev, Tt[:, :], False, True, r=[Mpk, Ttk], w=[bBk])
                        Tn, Tnk = tt_p.get()
                        cp("dve" if lev % 2 else "act", Tn[:, :], Q2, r=[bBk], w=[Tnk])
                        Tt, Ttk = Tn, Tnk
                        yield
                    PX = bB[:, 384:448]; PU = bB[:, 448:512]
                    mm(PX, ARc[rows, 0:128], Sb_t[rows, (j_ * NC_ + fc) * 64:(j_ * NC_ + fc) * 64 + 64], True, False, r=[ARk, Sbk], w=[bBk])
                    mm(PX, E2[:, 0:128], VT[:, c, hh * 64:hh * 64 + 64], False, True, r=[E2k, VTk], w=[bBk])
                    X0, X0k = xu_p.get()
                    cp("act", X0[:, :], PX, r=[bBk], w=[X0k])
                    yield
                    mm(PU, Tt[:, :], X0[:, :], True, True, r=[Ttk, X0k], w=[bBk])
                    W1, W1k = xu_p.get()
                    cp("dve", W1[:, :], PU, r=[bBk], w=[W1k])
                    yield
                    mm(PX, ident[:, :], X0[:, :], True, False, r=["const", X0k], w=[bBk])
                    mm(PX, Ct[:, :], W1[:, :], False, True, r=[Ctk, W1k], w=[bBk])
                    X2, X2k = xu_p.get()
                    cp("act", X2[:, :], PX, r=[bBk], w=[X2k])
                    yield
                    mm(PU, Tt[:, :], X2[:, :], True, True, r=[Ttk, X2k], w=[bBk])
                    U, Uk = xu_p.get()
                    cp("dve", U[:, :], PU, r=[bBk], w=[Uk])
                    yield
                    hc = slice(hh * 64, hh * 64 + 64)
                    PY = bA[:, 0:64]; PS = bA[:, 64:128]
                    mm(PY, ARc[rows, 128:256], Sb_t[rows, (j_ * NC_ + fc) * 64:(j_ * NC_ + fc) * 64 + 64], True, False, r=[Sbk, ARk], w=[bAk])
                    mm(PY, E1[:, 128:256], U[:, :], False, False, r=[Uk, E1k], w=[bAk])
                    mm(PY, E2[:, 128:256], VT[:, c, hc], False, True, r=[VTk, E2k], w=[bAk])
                    mm(PS, BET[:, c, :], U[:, :], True, False, r=[BETk, Uk], w=[bAk])
                    mm(PS, KET[:, c, :], VT[:, c, hc], False, True, r=[KETk, VTk], w=[bAk])
                    cp("act", ytok[:, c, hc], PY, r=[bAk], w=[ytokk])
                    rows_ = rows
                    so = (j_ * NC_ + fc) * 64
                    stmp, stk = xs_p.get()
                    act(stmp[rows_, :], S_t[rows_, so:so + 64], AF.Identity, scale=Wt[rows_, c * L + L - 1:c * L + L], r=[Sk, Wk], w=[stk])
                    tt("dve", S_t[rows_, so:so + 64], PS[rows_, :], stmp[rows_, :], ALU.add, r=[stk, bAk], w=[Sk])
                    cp("act", Sb_t[rows_, (j_ * NC_ + fc) * 64:(j_ * NC_ + fc) * 64 + 64], S_t[rows_, (j_ * NC_ + fc) * 64:(j_ * NC_ + fc) * 64 + 64], r=[Sk], w=[Sbk])

                gens = [unit(0), unit(1)]
                while gens:
                    for g_ in list(gens):
                        try:
                            next(g_)
                        except StopIteration:
                            gens.remove(g_)
            if dbg in ("u1", "u2", "u3", "u4", "u5", "u4a", "u4b", "u4c"):
                return
            py_, pyk = pd.get()
            for c in range(NCH):
                P.add("pe", (lambda o_, i_: (lambda e: e.transpose(o_, i_, identF[:, :])))(py_[:, c * L:(c + 1) * L], ytok[:, c, :]),
                      r=[ytokk, "const"], w=[pyk])
            cp("act", yt[:, :], py_[:, :], r=[pyk], w=[ytk])
            yb, ybk = sqp.get()
            cp("act", yb[:, :], yt[:, :], r=[ytk], w=[ybk])
            pm_, pmk = pd.get()
            mm(pm_[:, :], bo64[:, :], yb[:, :], True, True, r=[ybk, "const"], w=[pmk])
            stt(yt[:, :], pm_[:, :], -1.0, yt[:, :], ALU.mult, ALU.add, r=[ytk, pmk], w=[ytk])
            s2, s2k = sqp.get()
            act(s2[:, :], yt[:, :], AF.Square, r=[ytk], w=[s2k])
            pvv, pvvk = pd.get()
            mm(pvv[:, :], bo64[:, :], s2[:, :], True, True, r=[s2k, "const"], w=[pvvk])
            rs, rsk = rstd_p.get()
            rsqrt(rs[:, :], pvv[:, :], GN_EPS, r=[pvvk], w=[rsk])
            tt("dve", yt[:, :], yt[:, :], rs[:, :], ALU.mult, r=[ytk, rsk], w=[ytk])
            act(yt[:, :], yt[:, :], AF.Identity, scale=pc(f"lng{l}", fc), bias=pc(f"lnb{l}", fc), r=[ytk, "prm"], w=[ytk])
            tt("pool", yt[:, :], yt[:, :], v_[:, :], ALU.add, r=[ytk, vk_], w=[ytk])
            tt("pool", hb[:, fc, 1:TT + 1], yt[:, :], gt[:, :], ALU.mult, r=[ytk, gtk], w=[f"hb{fc}"])
        for oc in range(NC_):
            po, pok = proj16(f"wo{l}", oc, hbk, lambda kc: hb[:, kc, 1:TT + 1])
            stt(x_t[:, oc, :], po[:, :], dc(l, 32 + oc), x_t[:, oc, :], ALU.mult, ALU.add, r=[pok, "drv", f"x{oc}"], w=[f"x{oc}"])

    xv = xin.rearrange("(k p) t -> p k t", p=128)
    ov = out.rearrange("(k p) t -> p k t", p=128)
    for ti in range(NT):
        t0 = ti * TT
        for kc in range(NC_):
            dma(x_t[:, kc, :], xv[:, kc, t0:t0 + TT], r=[], w=[f"x{kc}"], semkey=f"x{kc}")
        for l in range(nlayers):
            if dbg == "pro":
                break
            if l % 2 == 0:
                rwkv(l, ti)
            else:
                conv(l)
            if dbg in ("mix", "r1", "r2", "r3", "r4", "r5", "r6", "u1", "u2", "u3", "u4", "u5", "u4a", "u4b", "u4c"):
                break
            ffn(l)
        acc, ak = pd.get()
        for kc in range(NC_):
            s, sk = sqp.get()
            act(s[:, :], x_t[:, kc, :], AF.Square, r=[f"x{kc}"], w=[sk])
            mm(acc[:, :], oneC[:, :], s[:, :], kc == 0, kc == NC_ - 1, r=[sk, "const"], w=[ak])
        rs, rk = rstd_p.get()
        rsqrt(rs[:, :], acc[:, :], RMS_EPS, r=[ak], w=[rk])
        for kc in range(NC_):
            o, ok = o_p.get()
            stt(o[:, :], x_t[:, kc, :], pc("fin", kc), rs[:, :], ALU.mult, ALU.mult, r=[f"x{kc}", "prm", rk], w=[ok])
            dma(ov[:, kc, t0:t0 + TT], o[:, :], r=[ok], w=[], semkey=ok)

    P.emit(nc, es)
    es.close()
    return nc, lay


def make_inmaps(inp, nlayers, T, lay, batches):
    maps = []
    cst = make_consts()
    for b in batches:
        m = {"x": np.ascontiguousarray(np.asarray(inp["x"][b], np.float32)[:T].T),
             "prm": pack_params(inp, b, nlayers, lay), "cst": cst}
        for l in range(nlayers):
            j = l // 2
            m[f"ada{l}"] = np.ascontiguousarray(inp["ada_w"][l]); m[f"up{l}"] = np.ascontiguousarray(inp["ffn_w_up"][l])
            m[f"dn{l}"] = np.ascontiguousarray(inp["ffn_w_down"][l])
            if l % 2 == 0:
                for i, n in enumerate("rkv"):
                    m[f"{n}{l}"] = np.ascontiguousarray(inp["rwkv_w_rkv"][j][i])
                m[f"wo{l}"] = np.ascontiguousarray(inp["rwkv_w_o"][j])
                for n in ("w1", "w2", "a1", "a2", "g1", "g2"):
                    m[f"{n}{l}"] = np.ascontiguousarray(inp["rwkv_" + n][j])
                if l >= 2:
                    m[f"v1{l}"] = np.ascontiguousarray(inp["rwkv_v1"][j - 1]); m[f"v2{l}"] = np.ascontiguousarray(inp["rwkv_v2"][j - 1])
            else:
                m[f"pw1{l}"] = np.ascontiguousarray(inp["conv_w_pw1"][j]); m[f"pw2{l}"] = np.ascontiguousarray(inp["conv_w_pw2"][j])
        maps.append(m)
    return maps


def run(inp, nlayers, T, batches, dbg=None):
    nc, lay = build(T, nlayers, dbg)
    maps = make_inmaps(inp, nlayers, T, lay, batches)
    res = run_bass_kernel_spmd(nc, maps, core_ids=list(range(len(batches))))
    return [np.ascontiguousarray(r["out"].T) for r in res.results]


def kernel(**inputs):
    inp = {k: np.asarray(v) for k, v in inputs.items()}
    B, T, _ = inp["x"].shape
    outs = run(inp, 4, T, [0, 1, 2, 3])
    return np.stack(outs[:4], axis=0).astype(np.float32)
```
